# Optimizing a Trainium2 kernel written in Bass

```python
import jax
import jax.numpy as jnp
from jax import lax
import numpy as np

D_MODEL = 1024
BATCH = 16
SEQ = 4096
DEPTH = 4

N_EVEN = (DEPTH + 1) // 2
N_ODD = DEPTH // 2
EPS = 1e-6
ROPE_THETA = 10000.0

D_FF = -(-(8 * D_MODEL) // (3 * 256)) * 256

CONV_WIDTH = D_MODEL // 2
POOL_WIDTH = D_MODEL - CONV_WIDTH
POOL_WINDOWS = (2, 4, 8, 16)
POOL_GROUP = POOL_WIDTH // len(POOL_WINDOWS)
SHORT_CONV_K = 3
EVEN_IN = 3 * CONV_WIDTH + POOL_WIDTH

HEAD_DIM = 64
NSA_WIDTH = D_MODEL // 2
NSA_HEADS = NSA_WIDTH // HEAD_DIM
NSA_KV_GROUPS = 2
GROUP_SIZE = NSA_HEADS // NSA_KV_GROUPS
KV_WIDTH = NSA_KV_GROUPS * HEAD_DIM
CMP_BLOCK = 32
CMP_STRIDE = 16
CMP_HIDDEN = 4 * HEAD_DIM
SEL_BLOCK = 64
N_SELECT = 8
WINDOW = 512
Q_BLOCK = 128
FORCE_SCORE = 1e4
NEG = -1e30
TINY = 1e-30
CONF_WIDTH = D_MODEL - NSA_WIDTH
CONF_K = 31
ODD_PARTS = (NSA_WIDTH, KV_WIDTH, KV_WIDTH, KV_WIDTH, KV_WIDTH, KV_WIDTH, KV_WIDTH,
             3 * NSA_HEADS, 2 * CONF_WIDTH)
ODD_IN = sum(ODD_PARTS)

kernel_name = 'hybrid_shortconv_pool_nsa_conformer_trunk'


def rms_norm(x, g):
    xf = x.astype(jnp.float32)
    y = xf * lax.rsqrt(jnp.mean(xf * xf, axis=-1, keepdims=True) + EPS)
    return (y * g.astype(jnp.float32)).astype(x.dtype)


def layer_norm(x, g, b):
    xf = x.astype(jnp.float32)
    mu = jnp.mean(xf, axis=-1, keepdims=True)
    var = jnp.mean(jnp.square(xf - mu), axis=-1, keepdims=True)
    y = (xf - mu) * lax.rsqrt(var + EPS)
    return (y * g.astype(jnp.float32) + b.astype(jnp.float32)).astype(x.dtype)


def rope_tables(seq_len):
    inv = 1.0 / (ROPE_THETA ** (jnp.arange(0, HEAD_DIM, 2, dtype=jnp.float32) / HEAD_DIM))
    ang = jnp.arange(seq_len, dtype=jnp.float32)[:, None] * inv[None, :]
    return jnp.cos(ang), jnp.sin(ang)


def apply_rope(x, cos, sin):
    xf = x.astype(jnp.float32)
    x1, x2 = jnp.split(xf, 2, axis=-1)
    c = cos[None, :, None, :]
    s = sin[None, :, None, :]
    return jnp.concatenate([x1 * c - x2 * s, x2 * c + x1 * s], axis=-1).astype(x.dtype)


def causal_dwconv(x, w):
    k, c = w.shape
    return lax.conv_general_dilated(
        x, w[:, None, :].astype(x.dtype), window_strides=(1,), padding=[(k - 1, 0)],
        dimension_numbers=('NWC', 'WIO', 'NWC'), feature_group_count=c)


def masked_softmax(s, mask):
    s = jnp.where(mask, s.astype(jnp.float32), NEG)
    m = jnp.max(s, axis=-1, keepdims=True)
    e = jnp.where(mask, jnp.exp(s - m), 0.0)
    return e / jnp.maximum(jnp.sum(e, axis=-1, keepdims=True), TINY)


def swiglu(h, wg, wu, wd):
    return (jax.nn.silu(h @ wg) * (h @ wu)) @ wd


def multiscale_pool(v, pool_w, pool_scale):
    bsz, t, _ = v.shape
    vg = v.reshape(bsz, t, len(POOL_WINDOWS), POOL_GROUP)
    cs = jnp.cumsum(vg.astype(jnp.float32), axis=1)
    pos = jnp.arange(t)
    means = []
    for gi, w in enumerate(POOL_WINDOWS):
        c = cs[:, :, gi]
        prev = jnp.pad(c, ((0, 0), (w, 0), (0, 0)))[:, :t]
        cnt = jnp.minimum(pos + 1, w).astype(jnp.float32)[None, :, None]
        means.append((c - prev) / cnt)
    pooled = jnp.stack(means, axis=2).astype(v.dtype) - vg
    y = jnp.einsum('btgc,gcd->btgd', pooled, pool_w)
    return y.reshape(bsz, t, POOL_WIDTH) * pool_scale


def short_conv_pool_mixer(h, w_in, conv_w, pool_w, pool_scale, w_out):
    u = h @ w_in
    b_gate, c_gate, v_conv, v_pool = jnp.split(
        u, [CONV_WIDTH, 2 * CONV_WIDTH, 3 * CONV_WIDTH], axis=-1)
    y_conv = b_gate * causal_dwconv(c_gate * v_conv, conv_w)
    y_pool = multiscale_pool(v_pool, pool_w, pool_scale)
    return jnp.concatenate([y_conv, y_pool], axis=-1) @ w_out


def compress_blocks(tok, pos_emb, w1, w2):
    bsz, g, t, dh = tok.shape
    n_cmp = (t - CMP_BLOCK) // CMP_STRIDE + 1
    idx = jnp.arange(n_cmp)[:, None] * CMP_STRIDE + jnp.arange(CMP_BLOCK)[None, :]
    blocks = tok[:, :, idx] + pos_emb
    flat = blocks.reshape(bsz, g, n_cmp, CMP_BLOCK * dh)
    return jax.nn.gelu(flat @ w1) @ w2


def native_sparse_attention(q, k_cmp, v_cmp, k_slc, v_slc, k_win, v_win, gates,
                            q_norm, k_norm, cmp_pos, cmp_w1, cmp_w2, cos, sin):
    bsz, t = q.shape[0], q.shape[1]
    q = apply_rope(rms_norm(q, q_norm), cos, sin)
    ks = apply_rope(rms_norm(k_slc, k_norm[0]), cos, sin)
    kw = apply_rope(rms_norm(k_win, k_norm[1]), cos, sin)
    kc_tok = apply_rope(k_cmp, cos, sin)
    to_bgtd = lambda a: a.transpose(0, 2, 1, 3)
    kc = rms_norm(compress_blocks(to_bgtd(kc_tok), cmp_pos[0], cmp_w1[0], cmp_w2[0]), k_norm[2])
    vc = compress_blocks(to_bgtd(v_cmp), cmp_pos[1], cmp_w1[1], cmp_w2[1])
    ks, vs, kw, vw = to_bgtd(ks), to_bgtd(v_slc), to_bgtd(kw), to_bgtd(v_win)

    q5 = q.reshape(bsz, t, NSA_KV_GROUPS, GROUP_SIZE, HEAD_DIM).transpose(0, 2, 3, 1, 4)
    g5 = gates.reshape(bsz, t, NSA_KV_GROUPS, GROUP_SIZE, 3).transpose(0, 2, 3, 1, 4)

    n_cmp = kc.shape[2]
    n_sb = t // SEL_BLOCK
    n_sel = min(N_SELECT, n_sb)
    cmp_start = jnp.arange(n_cmp) * CMP_STRIDE
    cmp_end = cmp_start + CMP_BLOCK - 1
    sb_start = jnp.arange(n_sb) * SEL_BLOCK
    overlap = ((cmp_start[:, None] < sb_start[None, :] + SEL_BLOCK)
               & (cmp_end[:, None] >= sb_start[None, :])).astype(jnp.float32)
    ks_blocks = ks.reshape(bsz, NSA_KV_GROUPS, n_sb, SEL_BLOCK, HEAD_DIM)
    vs_blocks = vs.reshape(bsz, NSA_KV_GROUPS, n_sb, SEL_BLOCK, HEAD_DIM)
    kw_pad = jnp.pad(kw, ((0, 0), (0, 0), (WINDOW, 0), (0, 0)))
    vw_pad = jnp.pad(vw, ((0, 0), (0, 0), (WINDOW, 0), (0, 0)))
    bi = jnp.arange(bsz)[:, None, None, None]
    gi = jnp.arange(NSA_KV_GROUPS)[None, :, None, None]
    scale = HEAD_DIM ** -0.5

    def query_block(qb):
        q0 = qb * Q_BLOCK
        qblk = lax.dynamic_slice_in_dim(q5, q0, Q_BLOCK, axis=3)
        gblk = lax.dynamic_slice_in_dim(g5, q0, Q_BLOCK, axis=3)
        pos = q0 + jnp.arange(Q_BLOCK)
        s_c = jnp.einsum('bgrqd,bgnd->bgrqn', qblk, kc).astype(jnp.float32) * scale
        p_cmp = masked_softmax(s_c, cmp_end[None, :] <= pos[:, None])
        o_cmp = jnp.einsum('bgrqn,bgnd->bgrqd', p_cmp.astype(vc.dtype), vc)
        imp = jnp.einsum('bgrqn,nj->bgqj', p_cmp, overlap)
        cur = pos // SEL_BLOCK
        jj = jnp.arange(n_sb)[None, :]
        forced = (jj == 0) | (jj == cur[:, None]) | (jj == cur[:, None] - 1)
        valid = sb_start[None, :] <= pos[:, None]
        score = jnp.where(valid, jnp.where(forced, FORCE_SCORE, imp), -1.0)
        _, top_idx = lax.top_k(score, n_sel)
        k_sel = ks_blocks[bi, gi, top_idx].reshape(bsz, NSA_KV_GROUPS, Q_BLOCK, n_sel * SEL_BLOCK, HEAD_DIM)
        v_sel = vs_blocks[bi, gi, top_idx].reshape(bsz, NSA_KV_GROUPS, Q_BLOCK, n_sel * SEL_BLOCK, HEAD_DIM)
        tok_pos = (top_idx[..., None] * SEL_BLOCK + jnp.arange(SEL_BLOCK)).reshape(
            bsz, NSA_KV_GROUPS, Q_BLOCK, n_sel * SEL_BLOCK)
        sel_mask = (tok_pos <= pos[None, None, :, None])[:, :, None]
        s_s = jnp.einsum('bgrqd,bgqkd->bgrqk', qblk, k_sel).astype(jnp.float32) * scale
        p_s = masked_softmax(s_s, sel_mask)
        o_slc = jnp.einsum('bgrqk,bgqkd->bgrqd', p_s.astype(v_sel.dtype), v_sel)
        k_band = lax.dynamic_slice_in_dim(kw_pad, q0, WINDOW + Q_BLOCK, axis=2)
        v_band = lax.dynamic_slice_in_dim(vw_pad, q0, WINDOW + Q_BLOCK, axis=2)
        kpos = q0 - WINDOW + jnp.arange(WINDOW + Q_BLOCK)
        w_mask = ((kpos[None, :] <= pos[:, None]) & (kpos[None, :] > pos[:, None] - WINDOW)
                  & (kpos[None, :] >= 0))
        s_w = jnp.einsum('bgrqd,bgkd->bgrqk', qblk, k_band).astype(jnp.float32) * scale
        p_w = masked_softmax(s_w, w_mask)
        o_win = jnp.einsum('bgrqk,bgkd->bgrqd', p_w.astype(v_band.dtype), v_band)
        return gblk[..., 0:1] * o_cmp + gblk[..., 1:2] * o_slc + gblk[..., 2:3] * o_win

    out = lax.map(query_block, jnp.arange(t // Q_BLOCK))
    return out.transpose(1, 0, 4, 2, 3, 5).reshape(bsz, t, NSA_WIDTH)


def sparse_attn_conformer_mixer(h, w_in, q_norm, k_norm, cmp_pos, cmp_w1, cmp_w2,
                                conf_dw, conf_dw_b, conf_ln_g, conf_ln_b, w_out, cos, sin):
    bsz, t, _ = h.shape
    u = h @ w_in
    offs = []
    acc = 0
    for p in ODD_PARTS[:-1]:
        acc += p
        offs.append(acc)
    q, kc, vc, ks, vs, kw, vw, gt, cf = jnp.split(u, offs, axis=-1)
    heads = lambda a: a.reshape(bsz, t, NSA_HEADS, HEAD_DIM)
    kvs = lambda a: a.reshape(bsz, t, NSA_KV_GROUPS, HEAD_DIM)
    gates = jax.nn.sigmoid(gt).reshape(bsz, t, NSA_HEADS, 3)
    o_nsa = native_sparse_attention(heads(q), kvs(kc), kvs(vc), kvs(ks), kvs(vs), kvs(kw), kvs(vw),
                                    gates, q_norm, k_norm, cmp_pos, cmp_w1, cmp_w2, cos, sin)
    a, b = jnp.split(cf, 2, axis=-1)
    z = a * jax.nn.sigmoid(b)
    z = causal_dwconv(z, conf_dw) + conf_dw_b
    z = jax.nn.silu(layer_norm(z, conf_ln_g, conf_ln_b))
    return jnp.concatenate([o_nsa, z], axis=-1) @ w_out


def setup_inputs(seed: int = 0) -> dict:
    key = jax.random.key(seed)
    ks = jax.random.split(key, 24)
    nrm = lambda k, shape, s: jax.random.normal(k, shape, jnp.float32) * s
    d = D_MODEL
    return {
        'x': nrm(ks[0], (BATCH, SEQ, d), 1.0),
        'norm_mix': 1.0 + nrm(ks[1], (DEPTH, d), 0.05),
        'norm_ffn': 1.0 + nrm(ks[2], (DEPTH, d), 0.05),
        'ffn_gate': nrm(ks[3], (DEPTH, d, D_FF), d ** -0.5),
        'ffn_up': nrm(ks[4], (DEPTH, d, D_FF), d ** -0.5),
        'ffn_down': nrm(ks[5], (DEPTH, D_FF, d), 0.5 * D_FF ** -0.5),
        'w_in_even': nrm(ks[6], (N_EVEN, d, EVEN_IN), d ** -0.5),
        'conv_a': nrm(ks[7], (N_EVEN, SHORT_CONV_K, CONV_WIDTH), SHORT_CONV_K ** -0.5),
        'pool_w': nrm(ks[8], (N_EVEN, len(POOL_WINDOWS), POOL_GROUP, POOL_GROUP), POOL_GROUP ** -0.5),
        'pool_scale': 1.0 + nrm(ks[9], (N_EVEN, POOL_WIDTH), 0.1),
        'w_out_even': nrm(ks[10], (N_EVEN, d, d), 0.5 * d ** -0.5),
        'w_in_odd': nrm(ks[11], (N_ODD, d, ODD_IN), d ** -0.5),
        'q_norm': 1.0 + nrm(ks[12], (N_ODD, HEAD_DIM), 0.05),
        'k_norm': 1.0 + nrm(ks[13], (N_ODD, 3, HEAD_DIM), 0.05),
        'cmp_pos': nrm(ks[14], (N_ODD, 2, CMP_BLOCK, HEAD_DIM), 0.1),
        'cmp_w1': nrm(ks[15], (N_ODD, 2, CMP_BLOCK * HEAD_DIM, CMP_HIDDEN), (CMP_BLOCK * HEAD_DIM) ** -0.5),
        'cmp_w2': nrm(ks[16], (N_ODD, 2, CMP_HIDDEN, HEAD_DIM), CMP_HIDDEN ** -0.5),
        'conf_dw': nrm(ks[17], (N_ODD, CONF_K, CONF_WIDTH), CONF_K ** -0.5),
        'conf_dw_b': nrm(ks[18], (N_ODD, CONF_WIDTH), 0.02),
        'conf_ln_g': 1.0 + nrm(ks[19], (N_ODD, CONF_WIDTH), 0.05),
        'conf_ln_b': nrm(ks[20], (N_ODD, CONF_WIDTH), 0.02),
        'w_out_odd': nrm(ks[21], (N_ODD, d, d), 0.5 * d ** -0.5),
    }


def reference(x, norm_mix, norm_ffn, ffn_gate, ffn_up, ffn_down,
              w_in_even, conv_a, pool_w, pool_scale, w_out_even,
              w_in_odd, q_norm, k_norm, cmp_pos, cmp_w1, cmp_w2,
              conf_dw, conf_dw_b, conf_ln_g, conf_ln_b, w_out_odd):
    cos, sin = rope_tables(x.shape[1])
    for i in range(DEPTH):
        h = rms_norm(x, norm_mix[i])
        if i % 2 == 0:
            e = i // 2
            x = x + short_conv_pool_mixer(h, w_in_even[e], conv_a[e], pool_w[e],
                                          pool_scale[e], w_out_even[e])
        else:
            o = i // 2
            x = x + sparse_attn_conformer_mixer(h, w_in_odd[o], q_norm[o], k_norm[o], cmp_pos[o],
                                                cmp_w1[o], cmp_w2[o], conf_dw[o], conf_dw_b[o],
                                                conf_ln_g[o], conf_ln_b[o], w_out_odd[o], cos, sin)
        x = x + swiglu(rms_norm(x, norm_ffn[i]), ffn_gate[i], ffn_up[i], ffn_down[i])
    return x
```

```python
import numpy as np
import concourse.bass as bass
import concourse.mybir as mybir
from concourse.bass_utils import run_bass_kernel_spmd

F32 = mybir.dt.float32
BF16 = mybir.dt.bfloat16
U8 = mybir.dt.uint8
AF = mybir.ActivationFunctionType
ALU = mybir.AluOpType

D = 1024
SEQ = 4096
NSEQ = 2
TOK = NSEQ * SEQ
NB = TOK // 512
DFF = 2816
DEPTH = 4
EPS = 1e-6
NEGBIG = -30000.0
ENGS = ["tensor", "vector", "scalar", "gpsimd", "sync"]
NPV = 192


class Op:
    __slots__ = ("eng", "fn", "deps", "is_dma", "sem", "val", "pos", "signal", "prev", "bar")


class Buf:
    __slots__ = ("w", "r", "rd")

    def __init__(self):
        self.w = None
        self.r = {}
        self.rd = []


class Prog:
    def __init__(self, nc):
        self.nc = nc
        self.q = {e: [] for e in ENGS}
        self.dmas = []

    def op(self, eng, fn, R=(), W=(), dma=False):
        o = Op()
        o.eng = eng
        o.fn = fn
        o.is_dma = dma
        o.signal = dma
        o.prev = None
        o.sem = None
        o.val = 0
        o.bar = False
        deps = set()
        for b in R:
            if b.w is not None:
                deps.add(b.w)
        for b in W:
            if b.w is not None:
                deps.add(b.w)
            deps.update(b.r.values())
            deps.update(b.rd)
        for b in R:
            if dma:
                b.rd.append(o)
            else:
                b.r[eng] = o
        for b in W:
            b.w = o
            b.r = {}
            b.rd = []
        deps.discard(o)
        o.deps = deps
        for d in deps:
            d.signal = True
        o.pos = len(self.q[eng])
        self.q[eng].append(o)
        if dma:
            self.dmas.append(o)
        return o

    def barrier(self):
        lasts = [self.q[e][-1] for e in ENGS if self.q[e]]
        deps = set(lasts) | set(self.dmas)
        self.dmas = []
        for e in ENGS:
            o = Op()
            o.eng = e
            o.fn = None
            o.bar = True
            o.is_dma = False
            o.signal = False
            o.prev = None
            o.sem = None
            o.val = 0
            o.deps = set(deps)
            for d in deps:
                d.signal = True
            o.pos = len(self.q[e])
            self.q[e].append(o)

    def emit(self, esem, dsem):
        nc = self.nc
        for e in ENGS:
            cnt = 0
            slot_last = {}
            nd = 0
            for o in self.q[e]:
                if o.is_dma:
                    ns = len(dsem[e])
                    s = nd % ns
                    o.sem = dsem[e][s]
                    o.val = 16 * (nd // ns + 1)
                    o.prev = slot_last.get(s)
                    slot_last[s] = o
                    nd += 1
                elif o.signal:
                    cnt += 1
                    o.sem = esem[e]
                    o.val = cnt

        def run(ename, eng):
            waited = {}
            for o in self.q[ename]:
                waits = []
                for d in o.deps:
                    if (not d.is_dma) and d.eng == ename and not o.bar:
                        if ename == "tensor":
                            continue
                        if o.pos - d.pos > 3:
                            continue
                    waits.append((d.sem, d.val))
                if o.prev is not None:
                    waits.append((o.prev.sem, o.prev.val))
                for (s, v) in waits:
                    key = id(s)
                    if waited.get(key, 0) >= v:
                        continue
                    waited[key] = v
                    eng.wait_ge(s, v)
                if o.bar:
                    if o.signal:
                        eng.sem_inc(o.sem, 1)
                    continue
                inst = o.fn(eng)
                if o.signal:
                    inst.then_inc(o.sem, 16 if o.is_dma else 1)

        with nc.Block() as block:
            @block.tensor
            def _(t):
                run("tensor", t)

            @block.vector
            def _(v):
                run("vector", v)

            @block.scalar
            def _(s):
                run("scalar", s)

            @block.gpsimd
            def _(g):
                run("gpsimd", g)

            @block.sync
            def _(sy):
                run("sync", sy)


class Arena:
    def __init__(self, ap, size):
        self.ap = ap
        self.size = size
        self.off = 0
        self.stack = []

    def push(self):
        self.stack.append(self.off)

    def pop(self):
        self.off = self.stack.pop()

    def alloc(self, dtype, *free):
        n = 1
        for f in free:
            n *= f
        es = 4 if dtype == F32 else 2
        off = (self.off + 63) // 64 * 64
        nb = n * es
        assert off + nb <= self.size, ("SBUF arena overflow", off, nb, self.size)
        self.off = off + nb
        self.peak = max(getattr(self, "peak", 0), self.off)
        v = self.ap[:, off:off + nb].bitcast(dtype)
        if len(free) > 1:
            names = [f"d{i}" for i in range(len(free))]
            kw = {names[i]: free[i] for i in range(1, len(free))}
            v = v.rearrange("p (" + " ".join(names) + ") -> p " + " ".join(names), **kw)
        return v


class Ctx:
    pass


def bufs(n):
    return [Buf() for _ in range(n)]


def load_weight_rows(P, dst, src, nk, bl, eng="gpsimd"):
    for k in range(nk):
        P.op(eng, (lambda e, k=k: e.dma_start(out=dst[:, k, :], in_=src[k * 128:(k + 1) * 128, :])),
             W=[bl[k]], dma=True)


def rms_square(P, C, xb, bx):
    sq, bsq = C.sq, C.bsq
    P.op("scalar", lambda e: e.activation(out=sq, in_=xb, func=AF.Square), R=[bx], W=[bsq])


def rms_block(P, C, xb, bx, h, bh, gcol0, pvb, do_square=True):
    sq, bsq, rstd, brstd, ps, bps = C.sq, C.bsq, C.rstd, C.brstd, C.ps_ss, C.bps_ss
    if do_square:
        rms_square(P, C, xb, bx)
    for k in range(8):
        P.op("tensor", (lambda e, k=k: e.matmul(ps, C.ones_d[:, :], sq[:, k, :], start=(k == 0), stop=(k == 7))),
             R=[bsq, C.bconst], W=[bps])
    P.op("scalar", lambda e: e.activation(out=rstd, in_=ps, func=AF.Sqrt, bias=C.epsc[:, 0:1], scale=1.0),
         R=[bps, C.bconst], W=[brstd])
    P.op("vector", lambda e: e.reciprocal(out=rstd, in_=rstd), R=[brstd], W=[brstd])
    for k in range(8):
        P.op("vector", (lambda e, k=k: e.scalar_tensor_tensor(out=h[:, k, :], in0=xb[:, k, :],
                                                             scalar=C.pv[:, gcol0 + k:gcol0 + k + 1], in1=rstd,
                                                             op0=ALU.mult, op1=ALU.mult)),
             R=[bx, brstd, pvb], W=[bh])


def xview(ap2d):
    return ap2d.rearrange("(k p) t -> p k t", p=128)


def phase_ffn(P, C, l, hf, src, addsrc, dst):
    A = C.A
    A.push()
    NJ = 11
    wg = A.alloc(BF16, 8, NJ * 128)
    wu = A.alloc(BF16, 8, NJ * 128)
    wd = A.alloc(BF16, NJ, 1024)
    bwg, bwu, bwd = bufs(8), bufs(8), bufs(NJ)
    c0 = hf * NJ * 128
    load_weight_rows(P, wg, C.W["ffn_gate"][l, :, c0:c0 + NJ * 128], 8, bwg)
    load_weight_rows(P, wu, C.W["ffn_up"][l, :, c0:c0 + NJ * 128], 8, bwu)
    load_weight_rows(P, wd, C.W["ffn_down"][l, c0:c0 + NJ * 128, :], NJ, bwd)
    pvb = Buf()
    P.op("sync", lambda e: e.dma_start(out=C.pv, in_=C.W["pvec"][l]), W=[pvb], dma=True)

    xb = [A.alloc(F32, 8, 512) for _ in range(2)]
    bx = bufs(2)
    ob = A.alloc(F32, 8, 512)
    bob = Buf()
    h = [A.alloc(BF16, 8, 512) for _ in range(2)]
    bh = bufs(2)
    C.sq = A.alloc(BF16, 8, 512)
    C.bsq = Buf()
    C.rstd = A.alloc(F32, 512)
    C.brstd = Buf()
    a = A.alloc(BF16, NJ, 512)
    ba = bufs(NJ)
    sg = [A.alloc(F32, 512) for _ in range(2)]
    bsg = bufs(2)
    ps = C.psum
    bank = lambda i: ps[:, i * 512:(i + 1) * 512]
    pg = [bank(0), bank(1)]
    pu = [bank(2), bank(3)]
    po = [bank(4), bank(5)]
    C.ps_ss = bank(6)
    bpg, bpu, bpo = bufs(2), bufs(2), bufs(2)
    C.bps_ss = Buf()
    sv, dv = xview(src), xview(dst)
    av = xview(addsrc) if addsrc is not None else None

    def load_x(b):
        s = b % 2
        P.op("sync", lambda e: e.dma_start(out=xb[s], in_=sv[:, :, b * 512:(b + 1) * 512]), W=[bx[s]], dma=True)

    def rms(b, do_square=True):
        s = b % 2
        rms_block(P, C, xb[s], bx[s], h[s], bh[s], 8, pvb, do_square)

    load_x(0)
    load_x(1)
    rms(0)
    cntl = [0]

    def do_block(b):
        s = b % 2
        if av is not None:
            P.op("sync", lambda e, b=b: e.dma_start(out=ob, in_=av[:, :, b * 512:(b + 1) * 512]), W=[bob], dma=True)
        for j in range(NJ):
            i = cntl[0] % 2
            cntl[0] += 1
            for k in range(8):
                P.op("tensor", (lambda e, k=k, j=j, i=i: e.matmul(pg[i], wg[:, k, j * 128:(j + 1) * 128], h[s][:, k, :],
                                                                   start=(k == 0), stop=(k == 7))),
                     R=[bwg[k], bh[s]], W=[bpg[i]])
            for k in range(8):
                P.op("tensor", (lambda e, k=k, j=j, i=i: e.matmul(pu[i], wu[:, k, j * 128:(j + 1) * 128], h[s][:, k, :],
                                                                   start=(k == 0), stop=(k == 7))),
                     R=[bwu[k], bh[s]], W=[bpu[i]])
            P.op("scalar", (lambda e, i=i: e.activation(out=sg[i], in_=pg[i], func=AF.Silu)), R=[bpg[i]], W=[bsg[i]])
            P.op("vector", (lambda e, i=i, j=j: e.tensor_tensor(out=a[:, j, :], in0=sg[i], in1=pu[i], op=ALU.mult)),
                 R=[bsg[i], bpu[i]], W=[ba[j]])
            if j == 5 and b + 1 < NB and EARLY_SQ:
                rms_square(P, C, xb[1 - s], bx[1 - s])
        if b + 1 < NB:
            rms(b + 1, do_square=not EARLY_SQ)
        for m in range(8):
            i = m % 2
            for j in range(NJ):
                P.op("tensor", (lambda e, m=m, j=j, i=i: e.matmul(po[i], wd[:, j, m * 128:(m + 1) * 128], a[:, j, :],
                                                                   start=(j == 0), stop=(j == NJ - 1))),
                     R=[bwd[j], ba[j]], W=[bpo[i]])
            if av is None:
                P.op("vector", (lambda e, m=m, i=i: e.tensor_tensor(out=ob[:, m, :], in0=po[i], in1=xb[s][:, m, :], op=ALU.add)),
                     R=[bpo[i], bx[s]], W=[bob])
            else:
                P.op("vector", (lambda e, m=m, i=i: e.tensor_tensor(out=ob[:, m, :], in0=po[i], in1=ob[:, m, :], op=ALU.add)),
                     R=[bpo[i], bob], W=[bob])
        if b + 2 < NB:
            load_x(b + 2)
        P.op("sync", lambda e, b=b: e.dma_start(out=dv[:, :, b * 512:(b + 1) * 512], in_=ob), R=[bob], dma=True)

    for b in range(NB):
        do_block(b)
    P.barrier()
    A.pop()


def phase_even(P, C, l, src, dst):
    A = C.A
    A.push()
    e_ = l // 2
    win = A.alloc(BF16, 8, 2048)
    wout = A.alloc(BF16, 8, 1024)
    wpool = A.alloc(BF16, 4, 128)
    bwin, bwout, bwp = bufs(8), bufs(8), Buf()
    load_weight_rows(P, win, C.W["w_in_even"][e_], 8, bwin)
    load_weight_rows(P, wout, C.W["w_out_even"][e_], 8, bwout)
    for g in range(4):
        P.op("gpsimd", (lambda e, g=g: e.dma_start(out=wpool[:, g, :], in_=C.W["pool_w"][e_, g])), W=[bwp], dma=True)
    pvb = Buf()
    P.op("sync", lambda e: e.dma_start(out=C.pv, in_=C.W["pvec"][l]), W=[pvb], dma=True)
    pv = C.pv
    CA0 = 16
    PS0 = 28

    xb = [A.alloc(F32, 8, 512) for _ in range(2)]
    bx = bufs(2)
    h = [A.alloc(BF16, 8, 512) for _ in range(2)]
    bh = bufs(2)
    C.sq = A.alloc(BF16, 8, 512)
    C.bsq = Buf()
    C.rstd = A.alloc(F32, 512)
    C.brstd = Buf()
    cv = [A.alloc(F32, 4, 514) for _ in range(2)]
    bcv = [bufs(4) for _ in range(2)]
    bcvh = bufs(2)
    vp = [A.alloc(F32, 4, 528) for _ in range(2)]
    bvp = [bufs(4) for _ in range(2)]
    bvph = bufs(2)
    cgs = [A.alloc(F32, 512) for _ in range(2)]
    bcgs = bufs(2)
    tt = [A.alloc(F32, 512) for _ in range(2)]
    btt = bufs(2)
    sA = A.alloc(F32, 528)
    sB = A.alloc(F32, 528)
    bsA, bsB = Buf(), Buf()
    pl = [A.alloc(BF16, 512) for _ in range(4)]
    bpl = bufs(4)
    tmpc = A.alloc(F32, 16)
    btmpc = Buf()
    cat = A.alloc(BF16, 8, 512)
    bcat = bufs(8)
    ps = C.psum
    bank = lambda i: ps[:, i * 512:(i + 1) * 512]
    pb = [bank(i) for i in range(5)]
    bpb = bufs(5)
    po = [bank(5), bank(6)]
    bpo = bufs(2)
    C.ps_ss = bank(7)
    C.bps_ss = Buf()
    sv, dv = xview(src), xview(dst)
    rr = [0]

    def nextbank():
        i = rr[0] % 5
        rr[0] += 1
        return i

    def load_x(b):
        s = b % 2
        P.op("sync", lambda e: e.dma_start(out=xb[s], in_=sv[:, :, b * 512:(b + 1) * 512]), W=[bx[s]], dma=True)

    def rms(b):
        s = b % 2
        rms_block(P, C, xb[s], bx[s], h[s], bh[s], 0, pvb)

    def proj(m, s):
        i = nextbank()
        for k in range(8):
            P.op("tensor", (lambda e, k=k, i=i: e.matmul(pb[i], win[:, k, m * 128:(m + 1) * 128], h[s][:, k, :],
                                                          start=(k == 0), stop=(k == 7))),
                 R=[bwin[k], bh[s]], W=[bpb[i]])
        return i

    load_x(0)
    load_x(1)
    rms(0)
    WIN = (2, 4, 8, 16)

    def do_block(b):
        s = b % 2
        first = (b % (SEQ // 512) == 0)
        if first:
            P.op("gpsimd", lambda e: e.memset(cv[s][:, :, 0:2], 0.0), W=[bcvh[s]])
            P.op("gpsimd", lambda e: e.memset(vp[s][:, :, 0:16], 0.0), W=[bvph[s]])
        else:
            P.op("gpsimd", lambda e: e.tensor_copy(out=cv[s][:, :, 0:2], in_=cv[1 - s][:, :, 512:514]),
                 R=bcv[1 - s], W=[bcvh[s]])
            P.op("gpsimd", lambda e: e.tensor_copy(out=vp[s][:, :, 0:16], in_=vp[1 - s][:, :, 512:528]),
                 R=bvp[1 - s], W=[bvph[s]])
        for g in range(4):
            gi = g
            w = WIN[g]
            i = proj(12 + g, s)
            V = vp[s]
            P.op("scalar", (lambda e, i=i, g=g: e.activation(out=V[:, g, 16:528], in_=pb[i], func=AF.Copy)),
                 R=[bpb[i]], W=[bvp[s][g]])
            rv = [bvp[s][g], bvph[s]]
            P.op("vector", (lambda e, g=g: e.tensor_tensor(out=sA[:, 1:528], in0=V[:, g, 1:528], in1=V[:, g, 0:527], op=ALU.add)),
                 R=rv, W=[bsA])
            fin, bfin = sA, bsA
            if w >= 4:
                P.op("vector", lambda e: e.tensor_tensor(out=sB[:, 3:528], in0=sA[:, 3:528], in1=sA[:, 1:526], op=ALU.add),
                     R=[bsA], W=[bsB])
                fin, bfin = sB, bsB
            if w >= 8:
                P.op("vector", lambda e: e.tensor_tensor(out=sA[:, 7:528], in0=sB[:, 7:528], in1=sB[:, 3:524], op=ALU.add),
                     R=[bsB], W=[bsA])
                fin, bfin = sA, bsA
            if w >= 16:
                P.op("vector", lambda e: e.tensor_tensor(out=sB[:, 15:528], in0=sA[:, 15:528], in1=sA[:, 7:520], op=ALU.add),
                     R=[bsA], W=[bsB])
                fin, bfin = sB, bsB
            P.op("vector", (lambda e, fin=fin, g=g, gi=gi, w=w: e.scalar_tensor_tensor(out=pl[gi], in0=fin[:, 16:528], scalar=1.0 / w,
                                                                                      in1=V[:, g, 16:528], op0=ALU.mult, op1=ALU.subtract)),
                 R=[bfin] + rv, W=[bpl[gi]])
            if first:
                P.op("vector", (lambda e, fin=fin, w=w: e.tensor_tensor(out=tmpc[:, 0:w - 1], in0=fin[:, 16:16 + w - 1], in1=C.invc[:, 0:w - 1], op=ALU.mult)),
                     R=[bfin, C.bconst], W=[btmpc])
                P.op("vector", (lambda e, g=g, gi=gi, w=w: e.tensor_tensor(out=pl[gi][:, 0:w - 1], in0=tmpc[:, 0:w - 1], in1=V[:, g, 16:16 + w - 1], op=ALU.subtract)),
                     R=[btmpc] + rv, W=[bpl[gi]])
        for c in range(4):
            ci = c % 2
            i = proj(4 + c, s)
            P.op("scalar", (lambda e, i=i, ci=ci: e.activation(out=cgs[ci], in_=pb[i], func=AF.Copy)),
                 R=[bpb[i]], W=[bcgs[ci]])
            i = proj(8 + c, s)
            P.op("vector", (lambda e, i=i, ci=ci, c=c: e.tensor_tensor(out=cv[s][:, c, 2:514], in0=pb[i], in1=cgs[ci], op=ALU.mult)),
                 R=[bpb[i], bcgs[ci]], W=[bcv[s][c]])
            i = proj(c, s)
            w = lambda k, c=c: pv[:, CA0 + c * 3 + k:CA0 + c * 3 + k + 1]
            P.op("vector", (lambda e, ci=ci, c=c, w=w: e.tensor_scalar(out=tt[ci], in0=cv[s][:, c, 2:514], scalar1=w(2), scalar2=None, op0=ALU.mult)),
                 R=[bcv[s][c], pvb], W=[btt[ci]])
            P.op("vector", (lambda e, ci=ci, c=c, w=w: e.scalar_tensor_tensor(out=tt[ci], in0=cv[s][:, c, 1:513], scalar=w(1), in1=tt[ci], op0=ALU.mult, op1=ALU.add)),
                 R=[bcv[s][c], bcvh[s], pvb, btt[ci]], W=[btt[ci]])
            P.op("vector", (lambda e, ci=ci, c=c, w=w: e.scalar_tensor_tensor(out=tt[ci], in0=cv[s][:, c, 0:512], scalar=w(0), in1=tt[ci], op0=ALU.mult, op1=ALU.add)),
                 R=[bcv[s][c], bcvh[s], pvb, btt[ci]], W=[btt[ci]])
            P.op("vector", (lambda e, ci=ci, c=c, i=i: e.tensor_tensor(out=cat[:, c, :], in0=tt[ci], in1=pb[i], op=ALU.mult)),
                 R=[btt[ci], bpb[i]], W=[bcat[c]])
        for g in range(4):
            gi = g
            i = nextbank()
            P.op("tensor", (lambda e, i=i, g=g, gi=gi: e.matmul(pb[i], wpool[:, g, :], pl[gi], start=True, stop=True)),
                 R=[bwp, bpl[gi]], W=[bpb[i]])
            P.op("scalar", (lambda e, i=i, g=g: e.activation(out=cat[:, 4 + g, :], in_=pb[i], func=AF.Copy, scale=pv[:, PS0 + g:PS0 + g + 1])),
                 R=[bpb[i], pvb], W=[bcat[4 + g]])
        if b + 1 < NB:
            rms(b + 1)
        for m in range(8):
            i = m % 2
            for c in range(8):
                P.op("tensor", (lambda e, m=m, c=c, i=i: e.matmul(po[i], wout[:, c, m * 128:(m + 1) * 128], cat[:, c, :],
                                                                   start=(c == 0), stop=(c == 7))),
                     R=[bwout[c], bcat[c]], W=[bpo[i]])
            P.op("vector", (lambda e, m=m, i=i: e.tensor_tensor(out=xb[s][:, m, :], in0=po[i], in1=xb[s][:, m, :], op=ALU.add)),
                 R=[bpo[i], bx[s]], W=[bx[s]])
        P.op("sync", lambda e, b=b: e.dma_start(out=dv[:, :, b * 512:(b + 1) * 512], in_=xb[s]), R=[bx[s]], dma=True)
        if b + 2 < NB:
            load_x(b + 2)

    for b in range(NB):
        do_block(b)
    P.barrier()
    A.pop()


OD_GQ = 16
OD_KN = 17
OD_CB = 20
OD_LG = 24
OD_LB = 28
OD_DW = 32


def phase_conf(P, C, l, src, dst):
    A = C.A
    A.push()
    o_ = l // 2
    wcf = A.alloc(BF16, 8, 1024)
    wout = A.alloc(BF16, 4, 1024)
    bwcf, bwout = bufs(8), bufs(4)
    load_weight_rows(P, wcf, C.W["w_in_odd"][o_, :, 1304:2328], 8, bwcf)
    load_weight_rows(P, wout, C.W["w_out_odd"][o_, 512:1024, :], 4, bwout)
    pvb = Buf()
    P.op("sync", lambda e: e.dma_start(out=C.pv, in_=C.W["pvec"][l]), W=[pvb], dma=True)
    pv = C.pv
    dg = A.alloc(BF16, 124, 128)
    bdg = Buf()
    for idx in range(124):
        P.op("vector", (lambda e, idx=idx: e.tensor_scalar(out=dg[:, idx, :], in0=C.identb, scalar1=pv[:, OD_DW + idx:OD_DW + idx + 1],
                                                          scalar2=None, op0=ALU.mult)),
             R=[pvb, C.bconst], W=[bdg])
    NX = 3
    xb = [A.alloc(F32, 8, 512) for _ in range(NX)]
    bx = bufs(NX)
    h = [A.alloc(BF16, 8, 512) for _ in range(2)]
    bh = bufs(2)
    C.sq = A.alloc(BF16, 8, 512)
    C.bsq = Buf()
    C.rstd = A.alloc(F32, 512)
    C.brstd = Buf()
    z = [A.alloc(BF16, 4, 542) for _ in range(2)]
    bz = [bufs(4) for _ in range(2)]
    bzh = bufs(2)
    sgb = [A.alloc(F32, 512) for _ in range(2)]
    bsgb = bufs(2)
    zc = [A.alloc(F32, 4, 512) for _ in range(2)]
    bzc = [bufs(4) for _ in range(2)]
    zcb = A.alloc(BF16, 4, 512)
    bzcb = bufs(4)
    zq = A.alloc(BF16, 4, 512)
    bzq = bufs(4)
    mu = [A.alloc(F32, 512) for _ in range(2)]
    e2 = [A.alloc(F32, 512) for _ in range(2)]
    bmu, be2 = bufs(2), bufs(2)
    m2 = A.alloc(F32, 512)
    rs = A.alloc(F32, 512)
    bm2, brs = Buf(), Buf()
    tt = [A.alloc(F32, 512) for _ in range(2)]
    btt = bufs(2)
    cat = A.alloc(BF16, 4, 512)
    bcat = bufs(4)
    ps = C.psum
    bank = lambda i: ps[:, i * 512:(i + 1) * 512]
    pb = [bank(i) for i in range(3)]
    bpb = bufs(3)
    pc = [bank(3), bank(4)]
    bpc = bufs(2)
    pmu, pe2 = bank(5), bank(6)
    bpmu, bpe2 = Buf(), Buf()
    C.ps_ss = bank(7)
    C.bps_ss = Buf()
    sv, dv = xview(src), xview(dst)
    rr = [0]

    def nextbank():
        i = rr[0] % 3
        rr[0] += 1
        return i

    def load_x(b):
        s = b % NX
        P.op("sync", lambda e: e.dma_start(out=xb[s], in_=sv[:, :, b * 512:(b + 1) * 512]), W=[bx[s]], dma=True)

    def rms(b):
        rms_block(P, C, xb[b % NX], bx[b % NX], h[b % 2], bh[b % 2], 0, pvb)

    def proj(col0, s):
        i = nextbank()
        for k in range(8):
            P.op("tensor", (lambda e, k=k, i=i: e.matmul(pb[i], wcf[:, k, col0:col0 + 128], h[s][:, k, :],
                                                          start=(k == 0), stop=(k == 7))),
                 R=[bwcf[k], bh[s]], W=[bpb[i]])
        return i

    def part_a(b, hooks=None):
        s = b % 2
        first = (b % (SEQ // 512) == 0)
        if first:
            P.op("gpsimd", lambda e: e.memset(z[s][:, :, 0:30], 0.0), W=[bzh[s]])
        else:
            P.op("gpsimd", lambda e: e.tensor_copy(out=z[s][:, :, 0:30], in_=z[1 - s][:, :, 512:542]),
                 R=bz[1 - s], W=[bzh[s]])
        for c in range(4):
            ci = c % 2
            ia = proj(c * 128, s)
            ib = proj(512 + c * 128, s)
            P.op("scalar", (lambda e, ib=ib, ci=ci: e.activation(out=sgb[ci], in_=pb[ib], func=AF.Sigmoid)),
                 R=[bpb[ib]], W=[bsgb[ci]])
            P.op("vector", (lambda e, ia=ia, ci=ci, c=c: e.tensor_tensor(out=z[s][:, c, 30:542], in0=pb[ia], in1=sgb[ci], op=ALU.mult)),
                 R=[bpb[ia], bsgb[ci]], W=[bz[s][c]])
            if hooks is not None:
                hooks[c]()
            for k in range(31):
                P.op("tensor", (lambda e, c=c, k=k, ci=ci: e.matmul(pc[ci], dg[:, c * 31 + k, :], z[s][:, c, k:k + 512],
                                                                     start=(k == 0), stop=(k == 30))),
                     R=[bdg, bz[s][c], bzh[s]], W=[bpc[ci]])
            P.op("scalar", (lambda e, c=c, ci=ci: e.activation(out=zc[s][:, c, :], in_=pc[ci], func=AF.Identity,
                                                               bias=pv[:, OD_CB + c:OD_CB + c + 1], scale=1.0)),
                 R=[bpc[ci], pvb], W=[bzc[s][c]])
            P.op("scalar", (lambda e, c=c, ci=ci: e.activation(out=zq[:, c, :], in_=pc[ci], func=AF.Square,
                                                               bias=pv[:, OD_CB + c:OD_CB + c + 1], scale=1.0)),
                 R=[bpc[ci], pvb], W=[bzq[c]])
            P.op("gpsimd", (lambda e, c=c: e.tensor_copy(out=zcb[:, c, :], in_=zc[s][:, c, :])), R=[bzc[s][c]], W=[bzcb[c]])
        for c in range(4):
            P.op("tensor", (lambda e, c=c: e.matmul(pmu, C.ones512, zcb[:, c, :], start=(c == 0), stop=(c == 3))),
                 R=[bzcb[c], C.bconst], W=[bpmu])
        for c in range(4):
            P.op("tensor", (lambda e, c=c: e.matmul(pe2, C.ones512, zq[:, c, :], start=(c == 0), stop=(c == 3))),
                 R=[bzq[c], C.bconst], W=[bpe2])
        P.op("scalar", lambda e: e.activation(out=mu[s], in_=pmu, func=AF.Copy), R=[bpmu], W=[bmu[s]])
        P.op("scalar", lambda e: e.activation(out=e2[s], in_=pe2, func=AF.Copy), R=[bpe2], W=[be2[s]])

    def part_b1(b, cs_=None):
        s = b % 2
        if cs_ is None or cs_ == 0:
            P.op("vector", lambda e: e.tensor_tensor(out=m2, in0=mu[s], in1=mu[s], op=ALU.mult), R=[bmu[s]], W=[bm2])
            P.op("vector", lambda e: e.tensor_tensor(out=m2, in0=e2[s], in1=m2, op=ALU.subtract), R=[be2[s], bm2], W=[bm2])
            P.op("scalar", lambda e: e.activation(out=rs, in_=m2, func=AF.Sqrt, bias=C.epsc[:, 0:1], scale=1.0),
                 R=[bm2, C.bconst], W=[brs])
            P.op("vector", lambda e: e.reciprocal(out=rs, in_=rs), R=[brs], W=[brs])
        for c in (range(4) if cs_ is None else [cs_]):
            ci = c % 2
            P.op("vector", (lambda e, c=c, ci=ci: e.tensor_tensor(out=tt[ci], in0=zc[s][:, c, :], in1=mu[s], op=ALU.subtract)),
                 R=[bzc[s][c], bmu[s]], W=[btt[ci]])
            P.op("vector", (lambda e, c=c, ci=ci: e.tensor_tensor(out=tt[ci], in0=tt[ci], in1=rs, op=ALU.mult)),
                 R=[btt[ci], brs], W=[btt[ci]])
            P.op("scalar", (lambda e, c=c, ci=ci: e.activation(out=cat[:, c, :], in_=tt[ci], func=AF.Silu,
                                                               bias=pv[:, OD_LB + c:OD_LB + c + 1], scale=pv[:, OD_LG + c:OD_LG + c + 1])),
                 R=[btt[ci], pvb], W=[bcat[c]])

    def part_b2(b):
        sx = b % NX
        for m in range(8):
            i = nextbank()
            for c in range(4):
                P.op("tensor", (lambda e, m=m, c=c, i=i: e.matmul(pb[i], wout[:, c, m * 128:(m + 1) * 128], cat[:, c, :],
                                                                   start=(c == 0), stop=(c == 3))),
                     R=[bwout[c], bcat[c]], W=[bpb[i]])
            P.op("vector", (lambda e, m=m, i=i: e.tensor_tensor(out=xb[sx][:, m, :], in0=pb[i], in1=xb[sx][:, m, :], op=ALU.add)),
                 R=[bpb[i], bx[sx]], W=[bx[sx]])
        P.op("sync", lambda e: e.dma_start(out=dv[:, :, b * 512:(b + 1) * 512], in_=xb[sx]), R=[bx[sx]], dma=True)

    load_x(0)
    load_x(1)
    load_x(2)
    rms(0)
    part_a(0)
    rms(1)
    for b in range(NB):
        if b + 1 < NB:
            part_a(b + 1, hooks=[(lambda b=b, c=c: part_b1(b, c)) for c in range(4)])
        else:
            part_b1(b)
        if b + 2 < NB:
            rms(b + 2)
        part_b2(b)
        if b + 3 < NB:
            load_x(b + 3)
    P.barrier()
    A.pop()


KC0, VC0, KS0, VS0, KW0, VW0, GT0 = 512, 640, 768, 896, 1024, 1152, 1280
OD_POS = 160
TINY = 1e-30


def phase_nsa(P, C, l, src, addsrc, dst):
    A = C.A
    A.push()
    o_ = l // 2
    pv = C.pv
    W = C.W
    NBS = SEQ // 512
    wnsa = A.alloc(BF16, 8, 1304)
    bwn = bufs(8)
    wout = A.alloc(BF16, 4, 1024)
    bwo = bufs(4)
    for k in range(8):
        rows = W["w_in_odd"][o_, k * 128:(k + 1) * 128, :]
        P.op("gpsimd", (lambda e, k=k, rows=rows: e.dma_start(out=wnsa[:, k, :], in_=rows[:, 0:1304])),
             W=[bwn[k]], dma=True)
    load_weight_rows(P, wout, W["w_out_odd"][o_, 0:512, :], 4, bwo)
    pvb = Buf()
    P.op("sync", lambda e: e.dma_start(out=pv, in_=W["pvec"][l]), W=[pvb], dma=True)
    prot = A.alloc(BF16, 128)
    bd = A.alloc(BF16, 128)
    tri1 = A.alloc(BF16, 512)
    tri2 = A.alloc(BF16, 512)
    gext = A.alloc(F32, 126)
    mext = A.alloc(F32, 126)
    ovl = A.alloc(BF16, 2, 64)
    posb = A.alloc(BF16, 32)
    bcn = Buf()
    CN = W["nsac"]
    CB = W["nsacb"]
    P.op("sync", lambda e: e.dma_start(out=prot, in_=CB[:, 0:128]), W=[bcn], dma=True)
    P.op("sync", lambda e: e.dma_start(out=bd, in_=CB[:, 128:256]), W=[bcn], dma=True)
    for r in range(4):
        P.op("sync", (lambda e, r=r: e.dma_start(out=tri1[:, r * 128:(r + 1) * 128], in_=CB[:, 256:384])), W=[bcn], dma=True)
        P.op("sync", (lambda e, r=r: e.dma_start(out=tri2[:, r * 128:(r + 1) * 128], in_=CB[:, 384:512])), W=[bcn], dma=True)
    P.op("sync", lambda e: e.dma_start(out=gext, in_=CN[:, 512:638]), W=[bcn], dma=True)
    P.op("sync", lambda e: e.dma_start(out=mext, in_=CN[:, 640:766]), W=[bcn], dma=True)
    P.op("sync", lambda e: e.dma_start(out=ovl, in_=CB[:, 768:896].rearrange("p (a j) -> p a j", a=2)), W=[bcn], dma=True)
    P.op("vector", lambda e: e.tensor_copy(out=posb, in_=pv[:, OD_POS:OD_POS + 32]), R=[pvb], W=[bcn])
    QT = A.alloc(BF16, 4, SEQ)
    bQT = [bufs(NBS) for _ in range(4)]
    KS2 = [A.alloc(BF16, SEQ) for _ in range(2)]
    bKS = [bufs(NBS) for _ in range(2)]
    bE = Buf()
    KWT = A.alloc(BF16, SEQ)
    bKW = bufs(NBS)
    KCT = A.alloc(BF16, SEQ)
    bKC = bufs(NBS)
    VCT = A.alloc(BF16, SEQ)
    bVC = bufs(NBS)
    VS1 = A.alloc(BF16, 32, 2, 66)
    VW1 = A.alloc(BF16, 32, 2, 66)
    bVS = bufs(NBS)
    bVW = bufs(NBS)
    bones = Buf()
    GT = A.alloc(F32, 32, 24)
    bGT = bufs(NBS)
    KCN = [A.alloc(BF16, 256) for _ in range(2)]
    bKCN = bufs(2)
    VCM1 = A.alloc(BF16, 2, 2, 66)
    bVCM = bufs(2)
    selin = A.alloc(F32, 128)
    bselin = Buf()
    bz = A.alloc(F32, 4)
    bbz = Buf()
    P.op("sync", lambda e: e.dma_start(out=KS2[0][64:128, :], in_=W["eind"][:, :]), W=[bE], dma=True)
    P.op("sync", lambda e: e.dma_start(out=KS2[1][0:64, :], in_=W["eind"][:, :]), W=[bE], dma=True)
    P.op("vector", lambda e: e.memset(VS1[:, :, :, 64:65], 1.0), W=[bones])
    P.op("vector", lambda e: e.memset(VW1[:, :, :, 64:65], 1.0), W=[bones])
    P.op("vector", lambda e: e.memset(VCM1[:, :, :, 64:65], 1.0), W=[bones])
    P.op("vector", lambda e: e.memset(selin, 0.0), W=[bselin])
    ps = C.psum
    bank = lambda i: ps[:, i * 512:(i + 1) * 512]
    sv, dv, av = xview(src), xview(dst), xview(addsrc)
    P.barrier()

    def load_cmp(w1a, w1b, w2k, w2v, bw):
        for kind in range(2):
            w1 = W["cmp_w1"][o_, kind]
            w1v = w1.rearrange("(c p) h -> p c h", p=128)
            w1s = w1.rearrange("(c two d) h -> two d c h", two=2, d=64)
            import os
            lc = int(os.environ.get("LC", 15))
            for c4 in range(4):
                cs_ = slice(c4 * 4, c4 * 4 + 4)
                if lc & 1:
                    P.op("gpsimd", (lambda e, kind=kind, w1v=w1v, cs_=cs_: e.dma_start(out=w1a[kind][:, cs_, :], in_=w1v[:, cs_, :])),
                         W=[bw], dma=True)
                if lc & 2:
                    P.op("gpsimd", (lambda e, kind=kind, w1s=w1s, cs_=cs_: e.dma_start(out=w1b[kind][0:64, cs_, :], in_=w1s[1][:, cs_, :])),
                         W=[bw], dma=True)
                    P.op("gpsimd", (lambda e, kind=kind, w1s=w1s, cs_=cs_: e.dma_start(out=w1b[kind][64:128, cs_, :], in_=w1s[0][:, cs_, :])),
                         W=[bw], dma=True)
        w2 = W["cmp_w2"][o_]
        if lc & 4:
            for half in range(2):
                P.op("gpsimd", (lambda e, half=half: e.dma_start(out=w2k[:, :, half * 64:(half + 1) * 64],
                                                                  in_=w2[0].rearrange("(c p) d -> p c d", p=128))), W=[bw], dma=True)
        if lc & 8:
            P.op("gpsimd", lambda e: e.dma_start(out=w2v, in_=w2[1].rearrange("(c p) d -> p c d", p=128)), W=[bw], dma=True)

    def w1sel(w1a, w1b, kind, g, i):
        t = w1a[kind] if (i % 2) == g else w1b[kind]
        return t[g * 64:(g + 1) * 64, i // 2, :]

    def stage_cmp(first):
        A.push()
        w1a = [A.alloc(BF16, 16, 256) for _ in range(2)]
        w1b = [A.alloc(BF16, 16, 256) for _ in range(2)]
        w2k = A.alloc(BF16, 2, 128)
        w2v = A.alloc(BF16, 2, 64)
        bw = Buf()
        load_cmp(w1a, w1b, w2k, w2v, bw)
        gz = [A.alloc(BF16, 256) for _ in range(2)]
        bgz = bufs(2)
        sqk = A.alloc(BF16, 256)
        bsqk = Buf()
        rk = A.alloc(F32, 256)
        brk = Buf()
        pz = [bank(0), bank(1)]
        bpz = bufs(2)
        pk, pssk, pvc, pbias = bank(2), bank(3), bank(4), bank(5)
        bpk, bpssk, bpvc, bpbias = Buf(), Buf(), Buf(), Buf()
        import os
        cdbg = int(os.environ.get("CMP_DBG", 9))
        if cdbg < 2:
            P.barrier()
            A.pop()
            return
        if first:
            for kind in range(2 if cdbg != 2 else 1):
                g = kind
                for hc in range(2):
                    col = kind * 2 + hc
                    for i in range(32):
                        P.op("tensor", (lambda e, kind=kind, g=g, hc=hc, i=i, col=col: e.matmul(
                            pbias[:, col:col + 1], w1sel(w1a, w1b, kind, g, i)[:, hc * 128:(hc + 1) * 128],
                            posb[g * 64:(g + 1) * 64, i:i + 1], start=(i == 0), stop=(i == 31))),
                            R=[bw, bcn], W=[bpbias])
            P.op("vector", lambda e: e.tensor_copy(out=bz, in_=pbias[:, 0:4]), R=[bpbias], W=[bbz])
        if cdbg < 4:
            P.barrier()
            A.pop()
            return
        for kind in range(2):
            X = KCT if kind == 0 else VCT
            bX = bKC if kind == 0 else bVC
            Xv = X.rearrange("p (n s) -> p n s", s=16)
            for g in range(2 if cdbg > 4 else 1):
                for hc in range(2):
                    for i in range(32):
                        rhs = Xv[g * 64:(g + 1) * 64, 0:255, i] if i < 16 else Xv[g * 64:(g + 1) * 64, 1:256, i - 16]
                        P.op("tensor", (lambda e, kind=kind, g=g, hc=hc, i=i, rhs=rhs: e.matmul(
                            pz[hc][:, 0:255], w1sel(w1a, w1b, kind, g, i)[:, hc * 128:(hc + 1) * 128], rhs,
                            start=(i == 0), stop=(i == 31))), R=[bw] + bX, W=[bpz[hc]])
                    P.op("scalar", (lambda e, kind=kind, hc=hc: e.activation(out=gz[hc][:, 0:255], in_=pz[hc][:, 0:255],
                                                                            func=AF.Gelu_apprx_tanh,
                                                                            bias=bz[:, kind * 2 + hc:kind * 2 + hc + 1], scale=1.0)),
                         R=[bpz[hc], bbz], W=[bgz[hc]])
                if cdbg < 6:
                    continue
                if kind == 0:
                    for hc in range(2):
                        P.op("tensor", (lambda e, hc=hc: e.matmul(pk[:, 0:255], w2k[:, hc, :], gz[hc][:, 0:255],
                                                                   start=(hc == 0), stop=(hc == 1))),
                             R=[bw, bgz[hc]], W=[bpk])
                    P.op("scalar", lambda e: e.activation(out=sqk[:, 0:255], in_=pk[:, 0:255], func=AF.Square), R=[bpk], W=[bsqk])
                    P.op("tensor", lambda e: e.matmul(pssk[:, 0:255], bd, sqk[:, 0:255], start=True, stop=True),
                         R=[bsqk, bcn], W=[bpssk])
                    P.op("scalar", lambda e: e.activation(out=rk[:, 0:255], in_=pssk[:, 0:255], func=AF.Sqrt,
                                                          bias=C.epsc[:, 0:1], scale=1.0), R=[bpssk, C.bconst], W=[brk])
                    P.op("vector", lambda e: e.reciprocal(out=rk[:, 0:255], in_=rk[:, 0:255]), R=[brk], W=[brk])
                    P.op("vector", (lambda e, g=g: e.scalar_tensor_tensor(out=KCN[g][:, 0:255], in0=pk[:, 0:255],
                                                                         scalar=pv[:, OD_KN + 2:OD_KN + 3], in1=rk[:, 0:255],
                                                                         op0=ALU.mult, op1=ALU.mult)),
                         R=[bpk, brk, pvb], W=[bKCN[g]])
                else:
                    for nt in range(2):
                        nn = 128 if nt == 0 else 127
                        for hc in range(2):
                            P.op("tensor", (lambda e, nt=nt, nn=nn, hc=hc: e.matmul(
                                pvc[0:nn, nt * 64:(nt + 1) * 64], gz[hc][:, nt * 128:nt * 128 + nn], w2v[:, hc, :],
                                start=(hc == 0), stop=(hc == 1))), R=[bw, bgz[hc]], W=[bpvc])
                    P.op("scalar", (lambda e, g=g: e.activation(out=VCM1[:, g, :, 0:64],
                                                                in_=pvc[:, 0:128].rearrange("p (a d) -> p a d", a=2), func=AF.Copy)),
                         R=[bpvc], W=[bVCM[g]])
        P.barrier()
        A.pop()

    def stage_proj(sq_):
        A.push()
        xb = A.alloc(F32, 8, 512)
        bx = Buf()
        h = [A.alloc(BF16, 8, 512) for _ in range(2)]
        bh = bufs(2)
        C.sq = A.alloc(BF16, 8, 512)
        C.bsq = Buf()
        C.rstd = A.alloc(F32, 512)
        C.brstd = Buf()
        cs = [A.alloc(F32, 512) for _ in range(2)]
        sn = [A.alloc(F32, 512) for _ in range(2)]
        bcs = bufs(2)
        yb = [A.alloc(BF16, 512) for _ in range(2)]
        byb = bufs(2)
        qsq = [A.alloc(BF16, 512) for _ in range(2)]
        bqsq = bufs(2)
        t1 = [A.alloc(F32, 512) for _ in range(2)]
        bt1 = bufs(2)
        t2 = [A.alloc(F32, 512) for _ in range(2)]
        bt2 = bufs(2)
        rq = [A.alloc(F32, 512) for _ in range(2)]
        brq = bufs(2)
        pq = [bank(0), bank(1), bank(2)]
        bpq = bufs(3)
        ppr = [bank(3), bank(4)]
        bppr = bufs(2)
        pss = bank(5)
        bpss = Buf()
        ptm = bank(6)
        bptm = Buf()
        C.ps_ss = bank(7)
        C.bps_ss = Buf()
        rr = [0]
        cc = [0]
        tok0 = sq_ * SEQ

        def load_x(bs):
            P.op("sync", lambda e: e.dma_start(out=xb, in_=sv[:, :, tok0 + bs * 512:tok0 + (bs + 1) * 512]), W=[bx], dma=True)

        def rms(bs):
            s = bs % 2
            rms_block(P, C, xb, bx, h[s], bh[s], 0, pvb)

        def chunk(bs, col0, gcol, qscale, dests, norm=True):
            s = bs % 2
            ts = bs % 2
            i = rr[0] % 3
            rr[0] += 1
            j = cc[0] % 2
            cc[0] += 1
            for k in range(8):
                P.op("tensor", (lambda e, k=k: e.matmul(pq[i], wnsa[:, k, col0:col0 + 128], h[s][:, k, :],
                                                         start=(k == 0), stop=(k == 7))),
                     R=[bwn[k], bh[s]], W=[bpq[i]])
            if gcol is not None:
                P.op("scalar", lambda e: e.activation(out=yb[j], in_=pq[i], func=AF.Copy, scale=pv[:, gcol:gcol + 1]),
                     R=[bpq[i], pvb], W=[byb[j]])
            else:
                P.op("scalar", lambda e: e.activation(out=yb[j], in_=pq[i], func=AF.Copy), R=[bpq[i]], W=[byb[j]])
            P.op("tensor", lambda e: e.matmul(ppr[j], prot, yb[j], start=True, stop=True), R=[byb[j], bcn], W=[bppr[j]])
            if norm:
                P.op("scalar", lambda e: e.activation(out=qsq[j], in_=pq[i], func=AF.Square), R=[bpq[i]], W=[bqsq[j]])
                P.op("tensor", lambda e: e.matmul(pss, bd, qsq[j], start=True, stop=True), R=[bqsq[j], bcn], W=[bpss])
                P.op("scalar", lambda e: e.activation(out=rq[j], in_=pss, func=AF.Sqrt, bias=C.epsc[:, 0:1], scale=1.0),
                     R=[bpss, C.bconst], W=[brq[j]])
                P.op("vector", lambda e: e.reciprocal(out=rq[j], in_=rq[j]), R=[brq[j]], W=[brq[j]])
            if gcol is not None:
                P.op("vector", lambda e: e.scalar_tensor_tensor(out=t1[j], in0=pq[i], scalar=pv[:, gcol:gcol + 1], in1=cs[ts],
                                                                op0=ALU.mult, op1=ALU.mult),
                     R=[bpq[i], pvb, bcs[ts], byb[j], bqsq[j]], W=[bt1[j]])
            else:
                P.op("vector", lambda e: e.tensor_tensor(out=t1[j], in0=pq[i], in1=cs[ts], op=ALU.mult),
                     R=[bpq[i], bcs[ts], byb[j]], W=[bt1[j]])
            P.op("vector", lambda e: e.tensor_tensor(out=t2[j], in0=ppr[j], in1=sn[ts], op=ALU.mult),
                 R=[bppr[j], bcs[ts]], W=[bt2[j]])
            if norm:
                P.op("gpsimd", lambda e: e.tensor_tensor(out=t1[j], in0=t1[j], in1=t2[j], op=ALU.add),
                     R=[bt1[j], bt2[j]], W=[bt1[j]])
                for (dst_ap, prange, bdst) in dests:
                    lo, hi = prange
                    P.op("vector", (lambda e, dst_ap=dst_ap, lo=lo, hi=hi: e.scalar_tensor_tensor(
                        out=dst_ap, in0=t1[j][lo:hi], scalar=qscale, in1=rq[j][lo:hi], op0=ALU.mult, op1=ALU.mult)),
                        R=[bt1[j], brq[j]], W=[bdst])
            else:
                for (dst_ap, prange, bdst) in dests:
                    lo, hi = prange
                    P.op("vector", (lambda e, dst_ap=dst_ap, lo=lo, hi=hi: e.tensor_tensor(
                        out=dst_ap, in0=t1[j][lo:hi], in1=t2[j][lo:hi], op=ALU.add)),
                        R=[bt1[j], bt2[j]], W=[bdst])

        def do_block(bs):
            s = bs % 2
            ts = bs % 2
            c0 = bs * 512
            P.op("sync", lambda e: e.dma_start(out=cs[ts], in_=W["ropec"][:, c0:c0 + 512]), W=[bcs[ts]], dma=True)
            P.op("sync", lambda e: e.dma_start(out=sn[ts], in_=W["ropes"][:, c0:c0 + 512]), W=[bcs[ts]], dma=True)
            import os
            dbg = int(os.environ.get("NSA_DBG", 9))
            if dbg < 2:
                return
            for m in range(4):
                chunk(bs, m * 128, OD_GQ, 0.125, [(QT[:, m, c0:c0 + 512], (0, 128), bQT[m][bs])])
            if dbg < 3:
                return
            chunk(bs, KS0, OD_KN + 0, 1.0, [(KS2[0][0:64, c0:c0 + 512], (0, 64), bKS[0][bs]),
                                            (KS2[1][64:128, c0:c0 + 512], (64, 128), bKS[1][bs])])
            chunk(bs, KW0, OD_KN + 1, 1.0, [(KWT[:, c0:c0 + 512], (0, 128), bKW[bs])])
            if dbg < 4:
                return
            chunk(bs, KC0, None, 1.0, [(KCT[:, c0:c0 + 512], (0, 128), bKC[bs])], norm=False)
            if dbg < 5:
                return
            i = rr[0] % 3
            rr[0] += 1
            for k in range(8):
                P.op("tensor", (lambda e, k=k: e.matmul(pq[i], wnsa[:, k, VC0:VC0 + 128], h[s][:, k, :],
                                                         start=(k == 0), stop=(k == 7))),
                     R=[bwn[k], bh[s]], W=[bpq[i]])
            P.op("scalar", lambda e: e.activation(out=VCT[:, c0:c0 + 512], in_=pq[i], func=AF.Copy), R=[bpq[i]], W=[bVC[bs]])
            if dbg < 6:
                return
            for jt in range(4):
                tile_ = bs * 4 + jt
                for gi_, (col0, n, off) in enumerate(((VS0, 128, 0), (VW0, 128, 128), (GT0, 24, 256))):
                    if dbg < 7 + gi_ and dbg < 9 and not (dbg == 6 and gi_ == 0):
                        continue
                    for k in range(8):
                        P.op("tensor", (lambda e, k=k, jt=jt, col0=col0, n=n, off=off: e.matmul(
                            ptm[:, off:off + n], h[s][:, k, jt * 128:(jt + 1) * 128], wnsa[:, k, col0:col0 + n],
                            start=(k == 0), stop=(k == 7))), R=[bwn[k], bh[s]], W=[bptm])
                if dbg == 6:
                    continue
                cpm = int(os.environ.get("NSA_CP", 7))
                if cpm & 1:
                    P.op("scalar", (lambda e, tile_=tile_: e.activation(out=VS1[:, tile_, :, 0:64],
                                                                       in_=ptm[:, 0:128].rearrange("p (a d) -> p a d", a=2), func=AF.Copy)),
                         R=[bptm, bones], W=[bVS[bs]])
                if cpm & 2:
                    P.op("scalar", (lambda e, tile_=tile_: e.activation(out=VW1[:, tile_, :, 0:64],
                                                                       in_=ptm[:, 128:256].rearrange("p (a d) -> p a d", a=2), func=AF.Copy)),
                         R=[bptm, bones], W=[bVW[bs]])
                if cpm & 4:
                    P.op("scalar", (lambda e, tile_=tile_: e.activation(out=GT[:, tile_, :], in_=ptm[:, 256:280], func=AF.Sigmoid)),
                         R=[bptm], W=[bGT[bs]])

        load_x(0)
        rms(0)
        for bs in range(NBS):
            if bs + 1 < NBS:
                load_x(bs + 1)
                rms(bs + 1)
            do_block(bs)
        P.barrier()
        A.pop()

    def stage_attn(sq_):
        import os
        A.push()
        cm0 = A.alloc(BF16, 32, 128)
        cm1 = A.alloc(BF16, 16, 128)
        bcm = Buf()
        P.op("sync", lambda e: e.dma_start(out=cm0, in_=W["cmask0"].rearrange("p (a q) -> p a q", a=32)), W=[bcm], dma=True)
        P.op("sync", lambda e: e.dma_start(out=cm1, in_=W["cmask1"].rearrange("p (a q) -> p a q", a=16)), W=[bcm], dma=True)
        Q2 = [A.alloc(BF16, 512) for _ in range(2)]
        bQ2q = bufs(2)
        bQ2b = bufs(2)
        NPE = 5
        Pe = [A.alloc(BF16, 512) for _ in range(NPE)]
        bPe = bufs(NPE)
        Pc = [A.alloc(BF16, 512) for _ in range(2)]
        bPc = bufs(2)
        lmx = A.alloc(F32, 4)
        blmx = Buf()
        cf = A.alloc(F32, 4)
        bcf = Buf()
        rlc = A.alloc(F32, 4)
        brlc = Buf()
        tmpi = A.alloc(F32, 4, 64)
        btmpi = Buf()
        imp = A.alloc(F32, 64)
        bimp = Buf()
        score = A.alloc(F32, 64)
        bscore = Buf()
        m8 = A.alloc(F32, 8)
        bm8 = Buf()
        oacc = [A.alloc(F32, 4, 64) for _ in range(2)]
        boacc = bufs(2)
        otmp = A.alloc(F32, 4, 64)
        botmp = Buf()
        OT = A.alloc(BF16, 4, 512)
        bOT = bufs(4)
        xo = A.alloc(F32, 8, 512)
        bxo = Buf()
        NSB = 3
        psc = [bank(0), bank(1), bank(7)]
        bpsc = bufs(NSB)
        poc = bank(2).rearrange("p (r d) -> p r d", r=4)
        pos_ = bank(3).rearrange("p (r d) -> p r d", r=4)
        pow_ = bank(4).rearrange("p (r d) -> p r d", r=4)
        bpoc, bpos, bpow = Buf(), Buf(), Buf()
        pimp = bank(5)[:, 0:256].rearrange("p (r j) -> p r j", r=4)
        misc = bank(6)
        pT = misc[:, 0:128]
        pTo = misc[:, 128:256]
        bpimp = Buf()
        bpT = Buf()
        bpTo = bpT
        rr = [0]
        pp = [0]
        pc_ = [0]
        tok0 = sq_ * SEQ

        def sbank():
            i = rr[0] % NSB
            rr[0] += 1
            return i

        def pslot():
            i = pp[0] % NPE
            pp[0] += 1
            return i

        def q2copy(g, qb):
            hs = slice(g * 64, (g + 1) * 64)
            rQ = [bQT[m][qb // 4] for m in range(4)]
            P.op("gpsimd", lambda e: e.tensor_copy(out=Q2[g][hs, :].rearrange("p (r q) -> p r q", r=4),
                                                   in_=QT[hs, :, qb * 128:(qb + 1) * 128]),
                 R=rQ, W=[bQ2q[g]])

        def gate_coef(g, qb, br, psrc, bpsrc):
            P.op("vector", lambda e: e.tensor_scalar(out=lmx.unsqueeze(2), in0=psrc[:, :, 64:65], scalar1=TINY, scalar2=None, op0=ALU.max),
                 R=[bpsrc], W=[blmx])
            P.op("vector", lambda e: e.reciprocal(out=cf, in_=lmx), R=[blmx], W=[bcf])
            if br == 0:
                P.op("vector", lambda e: e.tensor_copy(out=rlc, in_=cf), R=[bcf], W=[brlc])
            P.op("vector", lambda e: e.tensor_tensor(out=cf, in0=cf,
                                                     in1=GT[:, qb, 12 * g:12 * g + 12].rearrange("p (r b) -> p b r", b=3)[:, br, :], op=ALU.mult),
                 R=[bcf, bGT[qb // 4]], W=[bcf])

        def make_tiles(g, qb, nxt):
            hs = slice(g * 64, (g + 1) * 64)
            bsl = slice((1 - g) * 64, (2 - g) * 64)
            q2 = Q2[g]
            oa = oacc[g]
            boa = boacc[g]
            bc_ = lambda: cf.unsqueeze(2).to_broadcast([128, 4, 64])
            tl = []
            ntl = 1 if qb < 16 else 2
            for nt in range(ntl):
                nn = 128 if nt == 0 else 127
                i = sbank()
                j = pslot()
                jc = pc_[0] % 2
                pc_[0] += 1

                def S(nt=nt, nn=nn, i=i):
                    P.op("tensor", lambda e: e.matmul(psc[i][0:nn, :], KCN[g][hs, nt * 128:nt * 128 + nn], q2[hs, :], start=True, stop=True),
                         R=[bKCN[g], bQ2q[g]], W=[bpsc[i]])

                def E(nt=nt, nn=nn, i=i, j=j, jc=jc):
                    P.op("scalar", lambda e: e.activation(out=Pe[j][0:nn, :], in_=psc[i][0:nn, :], func=AF.Exp), R=[bpsc[i]], W=[bPe[j]])
                    mk = cm0[0:nn, qb, :] if nt == 0 else cm1[0:nn, qb - 16, :]
                    P.op("vector", lambda e: e.tensor_tensor(
                        out=Pc[jc][0:nn, :].rearrange("p (r q) -> p r q", r=4), in0=Pe[j][0:nn, :].rearrange("p (r q) -> p r q", r=4),
                        in1=mk.unsqueeze(1).to_broadcast([nn, 4, 128]), op=ALU.mult),
                        R=[bPe[j], bcm], W=[bPc[jc]])

                def V(nt=nt, nn=nn, jc=jc):
                    for r in range(4):
                        P.op("tensor", (lambda e, r=r: e.matmul(poc[:, r, 0:65], Pc[jc][0:nn, r * 128:(r + 1) * 128], VCM1[0:nn, g, nt, 0:65],
                                                                start=(nt == 0 and r == 0), stop=(nt == ntl - 1))),
                             R=[bPc[jc], bVCM[g], bones], W=[bpoc])
                    for r in range(4):
                        P.op("tensor", (lambda e, r=r: e.matmul(pimp[:, r, :], Pc[jc][0:nn, r * 128:(r + 1) * 128], ovl[0:nn, nt, :],
                                                                start=(nt == 0 and r == 0), stop=(nt == ntl - 1))),
                             R=[bPc[jc], bcn], W=[bpimp])
                tl.append({"S": S, "E": E, "V": V, "post": None})

            def post_cmp():
                gate_coef(g, qb, 0, poc, bpoc)
                P.op("vector", lambda e: e.tensor_tensor(out=tmpi, in0=pimp, in1=rlc.unsqueeze(2).to_broadcast([128, 4, 64]), op=ALU.mult),
                     R=[bpimp, brlc], W=[btmpi])
                P.op("vector", lambda e: e.tensor_reduce(out=imp, in_=tmpi.rearrange("p r j -> p j r"), axis=mybir.AxisListType.X, op=ALU.add),
                     R=[btmpi], W=[bimp])
                w0 = 62 - 2 * qb
                P.op("vector", lambda e: e.tensor_tensor(out=score, in0=imp, in1=mext[:, w0:w0 + 64], op=ALU.mult), R=[bimp, bcn], W=[bscore])
                P.op("vector", lambda e: e.tensor_tensor(out=score, in0=score, in1=gext[:, w0:w0 + 64], op=ALU.add), R=[bscore, bcn], W=[bscore])
                P.op("vector", lambda e: e.memset(score[:, 0:1], 1.0e4), R=[bscore], W=[bscore])
                P.op("vector", lambda e: e.max(out=m8, in_=score), R=[bscore], W=[bm8])
                P.op("vector", lambda e: e.tensor_scalar(out=selin[:, (1 - g) * 64:(2 - g) * 64], in0=score, scalar1=m8[:, 7:8],
                                                         scalar2=NEGBIG, op0=ALU.is_lt, op1=ALU.mult),
                     R=[bscore, bm8], W=[bselin])
                P.op("vector", lambda e: e.tensor_tensor(out=oa, in0=poc[:, :, 0:64], in1=bc_(), op=ALU.mult), R=[bpoc, bcf], W=[boa])

            def post_cmp_pe():
                P.op("tensor", lambda e: e.transpose(pT, selin, C.identf), R=[bselin, C.bconst], W=[bpT])
                P.op("scalar", lambda e: e.activation(out=q2[bsl, :].rearrange("p (r q) -> p r q", r=4),
                                                      in_=pT[bsl, :].unsqueeze(1).to_broadcast([64, 4, 128]), func=AF.Copy),
                     R=[bpT], W=[bQ2b[g], bpT])
            tl[-1]["post"] = post_cmp
            tl[-1]["postpe"] = post_cmp_pe
            tl[-1]["tag"] = ("cmp", g, qb)
            k0 = max(0, qb - 4)
            for kt in range(k0, qb + 1):
                i = sbank()
                j = pslot()
                kb = kt // 4
                far = (kt == qb - 4)
                diag = (kt == qb)

                def S(kt=kt, i=i, far=far, diag=diag, kb=kb):
                    P.op("tensor", lambda e: e.matmul(psc[i], KWT[hs, kt * 128:(kt + 1) * 128], q2[hs, :], start=True, stop=not (far or diag)),
                         R=[bKW[kb], bQ2q[g]], W=[bpsc[i]])
                    if diag:
                        P.op("tensor", lambda e: e.matmul(psc[i], C.identb, tri1, start=False, stop=True), R=[bcn, C.bconst], W=[bpsc[i]])
                    if far:
                        P.op("tensor", lambda e: e.matmul(psc[i], C.identb, tri2, start=False, stop=True), R=[bcn, C.bconst], W=[bpsc[i]])

                def E(i=i, j=j):
                    P.op("scalar", lambda e: e.activation(out=Pe[j], in_=psc[i], func=AF.Exp), R=[bpsc[i]], W=[bPe[j]])

                def V(kt=kt, j=j, kb=kb):
                    for r in range(4):
                        P.op("tensor", (lambda e, r=r: e.matmul(pow_[:, r, 0:65], Pe[j][:, r * 128:(r + 1) * 128], VW1[:, kt, g, 0:65],
                                                                start=(kt == k0 and r == 0), stop=(kt == qb))),
                             R=[bPe[j], bVW[kb], bones], W=[bpow])
                tl.append({"S": S, "E": E, "V": V, "post": None})

            def post_win():
                gate_coef(g, qb, 2, pow_, bpow)
                P.op("vector", lambda e: e.tensor_tensor(out=otmp, in0=pow_[:, :, 0:64], in1=bc_(), op=ALU.mult), R=[bpow, bcf], W=[botmp])
                P.op("vector", lambda e: e.tensor_tensor(out=oa, in0=oa, in1=otmp, op=ALU.add), R=[boa, botmp], W=[boa])
            tl[-1]["post"] = post_win
            for kt in range(qb + 1):
                i = sbank()
                j = pslot()
                kb = kt // 4

                def S(kt=kt, i=i, kb=kb):
                    P.op("tensor", lambda e: e.matmul(psc[i], KS2[g][:, kt * 128:(kt + 1) * 128], q2, start=True, stop=(kt != qb)),
                         R=[bKS[g][kb], bE, bQ2q[g], bQ2b[g]], W=[bpsc[i]])
                    if kt == qb:
                        P.op("tensor", lambda e: e.matmul(psc[i], C.identb, tri1, start=False, stop=True), R=[bcn, C.bconst], W=[bpsc[i]])

                def E(i=i, j=j):
                    P.op("scalar", lambda e: e.activation(out=Pe[j], in_=psc[i], func=AF.Exp), R=[bpsc[i]], W=[bPe[j]])

                def V(kt=kt, j=j, kb=kb):
                    for r in range(4):
                        P.op("tensor", (lambda e, r=r: e.matmul(pos_[:, r, 0:65], Pe[j][:, r * 128:(r + 1) * 128], VS1[:, kt, g, 0:65],
                                                                start=(kt == 0 and r == 0), stop=(kt == qb))),
                             R=[bPe[j], bVS[kb], bones], W=[bpos])
                tl.append({"S": S, "E": E, "V": V, "post": None, "flush": ("cmp", g, qb) if kt == 0 else None})

            def post_sel():
                gate_coef(g, qb, 1, pos_, bpos)
                P.op("vector", lambda e: e.tensor_tensor(out=otmp, in0=pos_[:, :, 0:64], in1=bc_(), op=ALU.mult), R=[bpos, bcf], W=[botmp])
                P.op("vector", lambda e: e.tensor_tensor(out=oa, in0=oa, in1=otmp, op=ALU.add), R=[boa, botmp], W=[boa])

            def post_sel_pe():
                qsub = qb % 4
                for pair in range(2):
                    ch = 2 * g + pair
                    P.op("tensor", (lambda e, pair=pair: e.transpose(pTo, oa[:, 2 * pair:2 * pair + 2, :].rearrange("p r d -> p (r d)"), C.identf)),
                         R=[boa, C.bconst], W=[bpTo])
                    P.op("scalar", (lambda e, ch=ch: e.activation(out=OT[:, ch, qsub * 128:(qsub + 1) * 128], in_=pTo, func=AF.Copy)),
                         R=[bpTo], W=[bOT[ch], bpTo])
                if g == 1 and qb % 4 == 3:
                    out_block(qb // 4)
            tl[-1]["post"] = post_sel
            tl[-1]["postpe"] = post_sel_pe
            tl[-1]["tag"] = ("sel", g, qb)
            if nxt is not None:
                tl[0]["pre"] = (lambda: q2copy(*nxt))
            return tl

        def out_block(bs):
            t0 = tok0 + bs * 512
            P.op("sync", lambda e: e.dma_start(out=xo, in_=av[:, :, t0:t0 + 512]), W=[bxo], dma=True)
            for m in range(8):
                i = sbank()
                for c in range(4):
                    P.op("tensor", (lambda e, m=m, c=c, i=i: e.matmul(psc[i], wout[:, c, m * 128:(m + 1) * 128], OT[:, c, :],
                                                                       start=(c == 0), stop=(c == 3))),
                         R=[bwo[c], bOT[c]], W=[bpsc[i]])
                P.op("vector", (lambda e, m=m, i=i: e.tensor_tensor(out=xo[:, m, :], in0=psc[i], in1=xo[:, m, :], op=ALU.add)),
                     R=[bpsc[i], bxo], W=[bxo])
            P.op("sync", lambda e: e.dma_start(out=dv[:, :, t0:t0 + 512], in_=xo), R=[bxo], dma=True)

        nqb = int(os.environ.get("NSA_NQB", 32))
        order = [(g, qb) for qb in range(nqb) for g in range(2)]
        q2copy(*order[0])
        tiles = []
        for n_, (g, qb) in enumerate(order):
            nxt = order[n_ + 1] if n_ + 1 < len(order) else None
            tiles.extend(make_tiles(g, qb, nxt))
        LAG = int(os.environ.get("NSA_LAG", 2))
        DEFER = int(os.environ.get("NSA_DEFER", 3))
        pend = []
        for idx in range(len(tiles) + LAG):
            while pend and pend[0][0] <= idx:
                pend.pop(0)[2]()
            if idx < len(tiles):
                t = tiles[idx]
                if t.get("flush"):
                    for it in [p_ for p_ in pend if p_[1] == t["flush"]]:
                        pend.remove(it)
                        it[2]()
                if t.get("pre"):
                    t["pre"]()
                t["S"]()
                t["E"]()
            jx = idx - LAG
            if jx >= 0:
                t = tiles[jx]
                t["V"]()
                if t["post"]:
                    t["post"]()
                if t.get("postpe"):
                    pend.append([idx + DEFER, t["tag"], t["postpe"]])
        while pend:
            pend.pop(0)[2]()
        P.barrier()
        A.pop()

    import os
    stg = os.environ.get("NSA_STAGES", "123")
    for sq_ in range(int(os.environ.get("NSA_NSEQ", NSEQ))):
        if "1" in stg:
            stage_proj(sq_)
        if "2" in stg:
            stage_cmp(sq_ == 0)
        if "3" in stg:
            stage_attn(sq_)
    A.pop()

WSPEC = {
    "ffn_gate": [DEPTH, D, DFF], "ffn_up": [DEPTH, D, DFF], "ffn_down": [DEPTH, DFF, D],
    "w_in_even": [2, D, 2048], "pool_w": [2, 4, 128, 128], "w_out_even": [2, D, D],
    "w_in_odd": [2, D, 2328], "w_out_odd": [2, D, D],
    "cmp_w1": [2, 2, 2048, 256], "cmp_w2": [2, 2, 256, 64],
    "pvec": [DEPTH, 128, NPV],
    "consts": [128, 512],
    "nsac": [128, 1024],
    "nsacb": [128, 1024],
    "ropec": [128, SEQ], "ropes": [128, SEQ],
    "eind": [64, SEQ],
    "cmask0": [128, 32 * 128], "cmask1": [128, 16 * 128],
}


import os as _os
EARLY_SQ = _os.environ.get('EARLY_SQ', '1') == '1'
BF16_CONSTS = ("nsacb", "eind", "cmask0", "cmask1") if _os.environ.get("BFC", "1") == "1" else ()


def build(phases=None, dump=None):
    nc = bass.Bass("TRN2", target_bir_lowering=False)
    xT = nc.dram_tensor("xT", [D, TOK], F32, kind="ExternalInput").ap()
    W = {}
    for name, shp in WSPEC.items():
        W[name] = nc.dram_tensor(name, shp, BF16 if name in BF16_CONSTS else F32, kind="ExternalInput").ap()
    yT = nc.dram_tensor("yT", [D, TOK], F32, kind="ExternalOutput").ap()
    scrB = nc.dram_tensor("scrB", [D, TOK], F32).ap()
    P = Prog(nc)
    C = Ctx()
    C.W = W
    ARENA = 206 * 1024
    with nc.sbuf_tensor("arena", [128, ARENA], U8) as arena_t, nc.psum_tensor("psum", [128, 4096], F32) as psum_t:
        A = Arena(arena_t[:, :], ARENA)
        C.A = A
        C.psum = psum_t[:, :]
        C.pv = A.alloc(F32, NPV)
        C.ones_d = A.alloc(BF16, 128)
        C.invc = A.alloc(F32, 16)
        cst = A.alloc(F32, 512)
        C.bconst = Buf()
        bc0 = Buf()
        P.op("sync", lambda e: e.dma_start(out=cst, in_=W["consts"]), W=[bc0], dma=True)
        P.op("vector", lambda e: e.tensor_copy(out=C.invc, in_=cst[:, 0:16]), R=[bc0], W=[C.bconst])
        P.op("vector", lambda e: e.memset(C.ones_d, 1.0 / 1024.0), W=[C.bconst])
        C.ones512 = A.alloc(BF16, 128)
        P.op("vector", lambda e: e.memset(C.ones512, 1.0 / 512.0), W=[C.bconst])
        C.identf = cst[:, 128:256]
        C.identb = A.alloc(BF16, 128)
        P.op("vector", lambda e: e.tensor_copy(out=C.identb, in_=cst[:, 128:256]), R=[bc0], W=[C.bconst])
        C.epsc = A.alloc(F32, 2)
        P.op("vector", lambda e: e.memset(C.epsc, EPS), W=[C.bconst])
        P.barrier()

        if phases is None:
            phases = default_phases()
        bufmap = {"x": xT, "A": yT, "B": scrB}
        for ph in phases:
            kind = ph[0]
            if kind == "even":
                phase_even(P, C, ph[1], bufmap[ph[2]], bufmap[ph[3]])
            elif kind == "nsa":
                phase_nsa(P, C, ph[1], bufmap[ph[2]], bufmap[ph[3]], bufmap[ph[4]])
            elif kind == "conf":
                phase_conf(P, C, ph[1], bufmap[ph[2]], bufmap[ph[3]])
            elif kind == "ffn":
                phase_ffn(P, C, ph[1], ph[2], bufmap[ph[3]], bufmap[ph[4]] if ph[4] else None, bufmap[ph[5]])
            else:
                raise ValueError(kind)

        sems = {}
        import contextlib
        with contextlib.ExitStack() as st:
            esem = {e: st.enter_context(nc.semaphore("e_" + e)) for e in ENGS}
            dsem = {"sync": [st.enter_context(nc.semaphore(f"ds{i}")) for i in range(12)],
                    "gpsimd": [st.enter_context(nc.semaphore(f"dg{i}")) for i in range(12)],
                    "scalar": [st.enter_context(nc.semaphore(f"da{i}")) for i in range(4)],
                    "vector": [], "tensor": []}
            P.emit(esem, dsem)
    return nc


def default_phases():
    ph = []
    for l in range(DEPTH):
        src = "x" if l == 0 else "A"
        if l % 2 == 0:
            ph.append(("even", l, src, "A"))
        else:
            ph.append(("conf", l, "A", "B"))
            ph.append(("nsa", l, "A", "B", "A"))
        ph.append(("ffn", l, 0, "A", None, "B"))
        ph.append(("ffn", l, 1, "A", "B", "A"))
    return ph


def host_pvec(inp):
    pv = np.zeros((DEPTH, 128, NPV), np.float32)
    col = lambda v: np.ascontiguousarray(v.reshape(-1, 128).T)
    for l in range(DEPTH):
        pv[l, :, 0:8] = col(inp["norm_mix"][l])
        pv[l, :, 8:16] = col(inp["norm_ffn"][l])
        if l % 2 == 0:
            e = l // 2
            ca = inp["conv_a"][e]
            for c in range(4):
                for k in range(3):
                    pv[l, :, 16 + c * 3 + k] = ca[k, c * 128:(c + 1) * 128]
            pv[l, :, 28:32] = col(inp["pool_scale"][e])
        else:
            o = l // 2
            pv[l, :, OD_GQ] = np.tile(inp["q_norm"][o], 2)
            for i in range(3):
                pv[l, :, OD_KN + i] = np.tile(inp["k_norm"][o, i], 2)
            pv[l, :, OD_CB:OD_CB + 4] = col(inp["conf_dw_b"][o])
            pv[l, :, OD_LG:OD_LG + 4] = col(inp["conf_ln_g"][o])
            pv[l, :, OD_LB:OD_LB + 4] = col(inp["conf_ln_b"][o])
            dw = inp["conf_dw"][o]
            for c in range(4):
                pv[l, :, OD_DW + c * 31:OD_DW + (c + 1) * 31] = dw[:, c * 128:(c + 1) * 128].T
            pv[l, 0:64, OD_POS:OD_POS + 32] = inp["cmp_pos"][o, 0].T
            pv[l, 64:128, OD_POS:OD_POS + 32] = inp["cmp_pos"][o, 1].T
    return pv


def host_consts():
    c = np.zeros((128, 512), np.float32)
    c[:, 0:16] = (1.0 / (np.arange(16) + 1.0))[None, :]
    c[:, 128:256] = np.eye(128, dtype=np.float32)
    return c


def host_nsa_consts():
    out = {}
    c = np.zeros((128, 1024), np.float32)
    m = np.arange(128)
    perm = np.where((m % 64) < 32, m + 32, m - 32)
    prot = np.zeros((128, 128), np.float32)
    prot[perm, m] = 1.0
    c[:, 0:128] = prot
    c[:, 128:256] = ((m[:, None] // 64) == (m[None, :] // 64)).astype(np.float32) / 64.0
    k = np.arange(128)[:, None]
    q = np.arange(128)[None, :]
    c[:, 256:384] = np.where(k > q, NEGBIG, 0.0)
    c[:, 384:512] = np.where(k <= q, NEGBIG, 0.0)
    ql = np.arange(128)[:, None]
    cq = (ql >= 64).astype(np.int64)
    dl = np.arange(126)[None, :] - 62
    g = np.zeros((128, 126), np.float32)
    g[dl > cq] = -1.0
    g[(dl == cq) | (dl == cq - 1)] = 1.0e4
    c[:, 512:638] = g
    c[:, 640:766] = (dl < cq - 1).astype(np.float32)
    for nt in range(2):
        n = nt * 128 + np.arange(128)[:, None]
        j = np.arange(64)[None, :]
        ov = ((16 * n < 64 * j + 64) & (16 * n + 31 >= 64 * j) & (n < 255)).astype(np.float32)
        c[:, 768 + nt * 64:768 + (nt + 1) * 64] = ov
    out["nsac"] = c
    inv = 1.0 / (10000.0 ** (np.arange(0, 64, 2, dtype=np.float32) / 64.0))
    ang = np.arange(SEQ, dtype=np.float32)[:, None] * inv[None, :].astype(np.float32)
    cos = np.cos(ang).astype(np.float32).T
    sin = np.sin(ang).astype(np.float32).T
    out["ropec"] = np.ascontiguousarray(np.tile(cos, (4, 1)))
    out["ropes"] = np.ascontiguousarray(np.concatenate([-sin, sin, -sin, sin], axis=0))
    out["eind"] = (np.arange(64)[:, None] == (np.arange(SEQ)[None, :] // 64)).astype(np.float32)
    t = np.arange(32 * 128)[None, :]
    n0 = np.arange(128)[:, None]
    out["cmask0"] = (16 * n0 + 31 <= t).astype(np.float32)
    t1 = 16 * 128 + np.arange(16 * 128)[None, :]
    out["cmask1"] = (16 * (n0 + 128) + 31 <= t1).astype(np.float32)
    return out


def make_in_maps(inp, ncores=8):
    pv = host_pvec(inp)
    consts = host_consts()
    shared = {k: np.ascontiguousarray(inp[k], dtype=np.float32) for k in WSPEC if k in inp}
    shared["pvec"] = pv
    shared["consts"] = consts
    import ml_dtypes
    hc = host_nsa_consts()
    hc["nsacb"] = hc["nsac"]
    for k_ in BF16_CONSTS:
        hc[k_] = hc[k_].astype(ml_dtypes.bfloat16)
    shared.update(hc)
    wio = shared["w_in_odd"].copy()
    qcols = wio[:, :, 0:512].reshape(2, D, 2, 4, 64).transpose(0, 1, 3, 2, 4).reshape(2, D, 512)
    wio[:, :, 0:512] = qcols
    shared["w_in_odd"] = wio
    maps = []
    x = inp["x"]
    for c in range(ncores):
        xs = x[c * NSEQ:(c + 1) * NSEQ].reshape(TOK, D)
        m = dict(shared)
        m["xT"] = np.ascontiguousarray(xs.T)
        maps.append(m)
    return maps


def kernel(**inputs):
    inp = {k: np.asarray(v) for k, v in inputs.items()}
    nc = build()
    maps = make_in_maps(inp, 8)
    res = run_bass_kernel_spmd(nc, maps, core_ids=list(range(8)))
    out = np.empty((8 * NSEQ, SEQ, D), np.float32)
    for c in range(8):
        yT = res.results[c]["yT"]
        out[c * NSEQ:(c + 1) * NSEQ] = np.ascontiguousarray(yT.T).reshape(NSEQ, SEQ, D)
    return out
```

```python
import numpy as np
import concourse.bass as bass
import concourse.mybir as mybir
from concourse.bass_utils import run_bass_kernel_spmd

F32 = mybir.dt.float32
BF16 = mybir.dt.bfloat16
U8 = mybir.dt.uint8
AF = mybir.ActivationFunctionType
ALU = mybir.AluOpType

D = 1024
SEQ = 4096
NSEQ = 2
TOK = NSEQ * SEQ
NB = TOK // 512
DFF = 2816
DEPTH = 4
EPS = 1e-6
NEGBIG = -30000.0
ENGS = ["tensor", "vector", "scalar", "gpsimd", "sync"]
NPV = 192


class Op:
    __slots__ = ("eng", "fn", "deps", "is_dma", "sem", "val", "pos", "signal", "prev", "bar")


class Buf:
    __slots__ = ("w", "r", "rd")

    def __init__(self):
        self.w = None
        self.r = {}
        self.rd = []


class Prog:
    def __init__(self, nc):
        self.nc = nc
        self.q = {e: [] for e in ENGS}
        self.dmas = []

    def op(self, eng, fn, R=(), W=(), dma=False):
        o = Op()
        o.eng = eng
        o.fn = fn
        o.is_dma = dma
        o.signal = dma
        o.prev = None
        o.sem = None
        o.val = 0
        o.bar = False
        deps = set()
        for b in R:
            if b.w is not None:
                deps.add(b.w)
        for b in W:
            if b.w is not None:
                deps.add(b.w)
            deps.update(b.r.values())
            deps.update(b.rd)
        for b in R:
            if dma:
                b.rd.append(o)
            else:
                b.r[eng] = o
        for b in W:
            b.w = o
            b.r = {}
            b.rd = []
        deps.discard(o)
        o.deps = deps
        for d in deps:
            d.signal = True
        o.pos = len(self.q[eng])
        self.q[eng].append(o)
        if dma:
            self.dmas.append(o)
        return o

    def barrier(self):
        lasts = [self.q[e][-1] for e in ENGS if self.q[e]]
        deps = set(lasts) | set(self.dmas)
        self.dmas = []
        for e in ENGS:
            o = Op()
            o.eng = e
            o.fn = None
            o.bar = True
            o.is_dma = False
            o.signal = False
            o.prev = None
            o.sem = None
            o.val = 0
            o.deps = set(deps)
            for d in deps:
                d.signal = True
            o.pos = len(self.q[e])
            self.q[e].append(o)

    def emit(self, esem, dsem):
        nc = self.nc
        for e in ENGS:
            cnt = 0
            slot_last = {}
            nd = 0
            for o in self.q[e]:
                if o.is_dma:
                    ns = len(dsem[e])
                    s = nd % ns
                    o.sem = dsem[e][s]
                    o.val = 16 * (nd // ns + 1)
                    o.prev = slot_last.get(s)
                    slot_last[s] = o
                    nd += 1
                elif o.signal:
                    cnt += 1
                    o.sem = esem[e]
                    o.val = cnt

        def run(ename, eng):
            waited = {}
            for o in self.q[ename]:
                waits = []
                for d in o.deps:
                    if (not d.is_dma) and d.eng == ename and not o.bar:
                        if ename == "tensor":
                            continue
                        if o.pos - d.pos > 3:
                            continue
                    waits.append((d.sem, d.val))
                if o.prev is not None:
                    waits.append((o.prev.sem, o.prev.val))
                for (s, v) in waits:
                    key = id(s)
                    if waited.get(key, 0) >= v:
                        continue
                    waited[key] = v
                    eng.wait_ge(s, v)
                if o.bar:
                    if o.signal:
                        eng.sem_inc(o.sem, 1)
                    continue
                inst = o.fn(eng)
                if o.signal:
                    inst.then_inc(o.sem, 16 if o.is_dma else 1)

        with nc.Block() as block:
            @block.tensor
            def _(t):
                run("tensor", t)

            @block.vector
            def _(v):
                run("vector", v)

            @block.scalar
            def _(s):
                run("scalar", s)

            @block.gpsimd
            def _(g):
                run("gpsimd", g)

            @block.sync
            def _(sy):
                run("sync", sy)


class Arena:
    def __init__(self, ap, size):
        self.ap = ap
        self.size = size
        self.off = 0
        self.stack = []

    def push(self):
        self.stack.append(self.off)

    def pop(self):
        self.off = self.stack.pop()

    def alloc(self, dtype, *free):
        n = 1
        for f in free:
            n *= f
        es = 4 if dtype == F32 else 2
        off = (self.off + 63) // 64 * 64
        nb = n * es
        assert off + nb <= self.size, ("SBUF arena overflow", off, nb, self.size)
        self.off = off + nb
        self.peak = max(getattr(self, "peak", 0), self.off)
        v = self.ap[:, off:off + nb].bitcast(dtype)
        if len(free) > 1:
            names = [f"d{i}" for i in range(len(free))]
            kw = {names[i]: free[i] for i in range(1, len(free))}
            v = v.rearrange("p (" + " ".join(names) + ") -> p " + " ".join(names), **kw)
        return v


class Ctx:
    pass


def bufs(n):
    return [Buf() for _ in range(n)]


def load_weight_rows(P, dst, src, nk, bl, eng="gpsimd"):
    for k in range(nk):
        P.op(eng, (lambda e, k=k: e.dma_start(out=dst[:, k, :], in_=src[k * 128:(k + 1) * 128, :])),
             W=[bl[k]], dma=True)


def rms_square(P, C, xb, bx):
    sq, bsq = C.sq, C.bsq
    P.op("scalar", lambda e: e.activation(out=sq, in_=xb, func=AF.Square), R=[bx], W=[bsq])


def rms_block(P, C, xb, bx, h, bh, gcol0, pvb, do_square=True):
    sq, bsq, rstd, brstd, ps, bps = C.sq, C.bsq, C.rstd, C.brstd, C.ps_ss, C.bps_ss
    if do_square:
        rms_square(P, C, xb, bx)
    for k in range(8):
        P.op("tensor", (lambda e, k=k: e.matmul(ps, C.ones_d[:, :], sq[:, k, :], start=(k == 0), stop=(k == 7))),
             R=[bsq, C.bconst], W=[bps])
    P.op("scalar", lambda e: e.activation(out=rstd, in_=ps, func=AF.Ln, bias=C.epsc[:, 0:1], scale=1.0),
         R=[bps, C.bconst], W=[brstd])
    P.op("scalar", lambda e: e.activation(out=rstd, in_=rstd, func=AF.Exp, scale=-0.5), R=[brstd], W=[brstd])
    for k in range(8):
        P.op("vector", (lambda e, k=k: e.scalar_tensor_tensor(out=h[:, k, :], in0=xb[:, k, :],
                                                             scalar=C.pv[:, gcol0 + k:gcol0 + k + 1], in1=rstd,
                                                             op0=ALU.mult, op1=ALU.mult)),
             R=[bx, brstd, pvb], W=[bh])


def xview(ap2d):
    return ap2d.rearrange("(k p) t -> p k t", p=128)


def phase_ffn(P, C, l, hf, src, addsrc, dst):
    A = C.A
    A.push()
    NJ = 11
    wg = A.alloc(BF16, 8, NJ * 128)
    wu = A.alloc(BF16, 8, NJ * 128)
    wd = A.alloc(BF16, NJ, 1024)
    bwg, bwu, bwd = bufs(8), bufs(8), bufs(NJ)
    c0 = hf * NJ * 128
    load_weight_rows(P, wg, C.W["ffn_gate"][l, :, c0:c0 + NJ * 128], 8, bwg)
    load_weight_rows(P, wu, C.W["ffn_up"][l, :, c0:c0 + NJ * 128], 8, bwu)
    load_weight_rows(P, wd, C.W["ffn_down"][l, c0:c0 + NJ * 128, :], NJ, bwd)
    pvb = Buf()
    P.op("sync", lambda e: e.dma_start(out=C.pv, in_=C.W["pvec"][l]), W=[pvb], dma=True)

    xb = [A.alloc(F32, 8, 512) for _ in range(2)]
    bx = bufs(2)
    ob = A.alloc(F32, 8, 512)
    bob = Buf()
    h = [A.alloc(BF16, 8, 512) for _ in range(2)]
    bh = bufs(2)
    C.sq = A.alloc(BF16, 8, 512)
    C.bsq = Buf()
    C.rstd = A.alloc(F32, 512)
    C.brstd = Buf()
    a = A.alloc(BF16, NJ, 512)
    ba = bufs(NJ)
    sg = [A.alloc(F32, 512) for _ in range(2)]
    bsg = bufs(2)
    ps = C.psum
    bank = lambda i: ps[:, i * 512:(i + 1) * 512]
    pg = [bank(0), bank(1)]
    pu = [bank(2), bank(3)]
    po = [bank(4), bank(5)]
    C.ps_ss = bank(6)
    bpg, bpu, bpo = bufs(2), bufs(2), bufs(2)
    C.bps_ss = Buf()
    sv, dv = xview(src), xview(dst)
    av = xview(addsrc) if addsrc is not None else None

    def load_x(b):
        s = b % 2
        P.op("sync", lambda e: e.dma_start(out=xb[s], in_=sv[:, :, b * 512:(b + 1) * 512]), W=[bx[s]], dma=True)

    def rms(b, do_square=True):
        s = b % 2
        rms_block(P, C, xb[s], bx[s], h[s], bh[s], 8, pvb, do_square)

    load_x(0)
    load_x(1)
    rms(0)
    cntl = [0]

    def do_block(b):
        s = b % 2
        if av is not None:
            P.op("sync", lambda e, b=b: e.dma_start(out=ob, in_=av[:, :, b * 512:(b + 1) * 512]), W=[bob], dma=True)
        for j in range(NJ):
            i = cntl[0] % 2
            cntl[0] += 1
            for k in range(8):
                P.op("tensor", (lambda e, k=k, j=j, i=i: e.matmul(pg[i], wg[:, k, j * 128:(j + 1) * 128], h[s][:, k, :],
                                                                   start=(k == 0), stop=(k == 7))),
                     R=[bwg[k], bh[s]], W=[bpg[i]])
            for k in range(8):
                P.op("tensor", (lambda e, k=k, j=j, i=i: e.matmul(pu[i], wu[:, k, j * 128:(j + 1) * 128], h[s][:, k, :],
                                                                   start=(k == 0), stop=(k == 7))),
                     R=[bwu[k], bh[s]], W=[bpu[i]])
            P.op("scalar", (lambda e, i=i: e.activation(out=sg[i], in_=pg[i], func=AF.Silu)), R=[bpg[i]], W=[bsg[i]])
            P.op("vector", (lambda e, i=i, j=j: e.tensor_tensor(out=a[:, j, :], in0=sg[i], in1=pu[i], op=ALU.mult)),
                 R=[bsg[i], bpu[i]], W=[ba[j]])
            if j == 5 and b + 1 < NB and EARLY_SQ:
                rms_square(P, C, xb[1 - s], bx[1 - s])
        if b + 1 < NB:
            rms(b + 1, do_square=not EARLY_SQ)
        for m in range(8):
            i = m % 2
            for j in range(NJ):
                P.op("tensor", (lambda e, m=m, j=j, i=i: e.matmul(po[i], wd[:, j, m * 128:(m + 1) * 128], a[:, j, :],
                                                                   start=(j == 0), stop=(j == NJ - 1))),
                     R=[bwd[j], ba[j]], W=[bpo[i]])
            if av is None:
                P.op("vector", (lambda e, m=m, i=i: e.tensor_tensor(out=ob[:, m, :], in0=po[i], in1=xb[s][:, m, :], op=ALU.add)),
                     R=[bpo[i], bx[s]], W=[bob])
            else:
                P.op("vector", (lambda e, m=m, i=i: e.tensor_tensor(out=ob[:, m, :], in0=po[i], in1=ob[:, m, :], op=ALU.add)),
                     R=[bpo[i], bob], W=[bob])
        if b + 2 < NB:
            load_x(b + 2)
        P.op("sync", lambda e, b=b: e.dma_start(out=dv[:, :, b * 512:(b + 1) * 512], in_=ob), R=[bob], dma=True)

    for b in range(NB):
        do_block(b)
    P.barrier()
    A.pop()


def phase_even(P, C, l, src, dst):
    A = C.A
    A.push()
    e_ = l // 2
    win = A.alloc(BF16, 8, 2048)
    wout = A.alloc(BF16, 8, 1024)
    wpool = A.alloc(BF16, 4, 128)
    bwin, bwout, bwp = bufs(8), bufs(8), Buf()
    load_weight_rows(P, win, C.W["w_in_even"][e_], 8, bwin)
    load_weight_rows(P, wout, C.W["w_out_even"][e_], 8, bwout)
    for g in range(4):
        P.op("gpsimd", (lambda e, g=g: e.dma_start(out=wpool[:, g, :], in_=C.W["pool_w"][e_, g])), W=[bwp], dma=True)
    pvb = Buf()
    P.op("sync", lambda e: e.dma_start(out=C.pv, in_=C.W["pvec"][l]), W=[pvb], dma=True)
    pv = C.pv
    CA0 = 16
    PS0 = 28

    xb = [A.alloc(F32, 8, 512) for _ in range(2)]
    bx = bufs(2)
    h = [A.alloc(BF16, 8, 512) for _ in range(2)]
    bh = bufs(2)
    C.sq = A.alloc(BF16, 8, 512)
    C.bsq = Buf()
    C.rstd = A.alloc(F32, 512)
    C.brstd = Buf()
    cv = [A.alloc(F32, 4, 514) for _ in range(2)]
    bcv = [bufs(4) for _ in range(2)]
    bcvh = bufs(2)
    vp = [A.alloc(F32, 4, 528) for _ in range(2)]
    bvp = [bufs(4) for _ in range(2)]
    bvph = bufs(2)
    cgs = [A.alloc(F32, 512) for _ in range(2)]
    bcgs = bufs(2)
    tt = [A.alloc(F32, 512) for _ in range(2)]
    btt = bufs(2)
    sA = A.alloc(F32, 528)
    sB = A.alloc(F32, 528)
    bsA, bsB = Buf(), Buf()
    pl = [A.alloc(BF16, 512) for _ in range(4)]
    bpl = bufs(4)
    tmpc = A.alloc(F32, 16)
    btmpc = Buf()
    cat = A.alloc(BF16, 8, 512)
    bcat = bufs(8)
    ps = C.psum
    bank = lambda i: ps[:, i * 512:(i + 1) * 512]
    pb = [bank(i) for i in range(5)]
    bpb = bufs(5)
    po = [bank(5), bank(6)]
    bpo = bufs(2)
    C.ps_ss = bank(7)
    C.bps_ss = Buf()
    sv, dv = xview(src), xview(dst)
    rr = [0]

    def nextbank():
        i = rr[0] % 5
        rr[0] += 1
        return i

    def load_x(b):
        s = b % 2
        P.op("sync", lambda e: e.dma_start(out=xb[s], in_=sv[:, :, b * 512:(b + 1) * 512]), W=[bx[s]], dma=True)

    def rms(b):
        s = b % 2
        rms_block(P, C, xb[s], bx[s], h[s], bh[s], 0, pvb)

    def proj(m, s):
        i = nextbank()
        for k in range(8):
            P.op("tensor", (lambda e, k=k, i=i: e.matmul(pb[i], win[:, k, m * 128:(m + 1) * 128], h[s][:, k, :],
                                                          start=(k == 0), stop=(k == 7))),
                 R=[bwin[k], bh[s]], W=[bpb[i]])
        return i

    load_x(0)
    load_x(1)
    rms(0)
    WIN = (2, 4, 8, 16)

    def do_block(b):
        s = b % 2
        first = (b % (SEQ // 512) == 0)
        if first:
            P.op("gpsimd", lambda e: e.memset(cv[s][:, :, 0:2], 0.0), W=[bcvh[s]])
            P.op("gpsimd", lambda e: e.memset(vp[s][:, :, 0:16], 0.0), W=[bvph[s]])
        else:
            P.op("gpsimd", lambda e: e.tensor_copy(out=cv[s][:, :, 0:2], in_=cv[1 - s][:, :, 512:514]),
                 R=bcv[1 - s], W=[bcvh[s]])
            P.op("gpsimd", lambda e: e.tensor_copy(out=vp[s][:, :, 0:16], in_=vp[1 - s][:, :, 512:528]),
                 R=bvp[1 - s], W=[bvph[s]])
        for g in range(4):
            gi = g
            w = WIN[g]
            i = proj(12 + g, s)
            V = vp[s]
            P.op("scalar", (lambda e, i=i, g=g: e.activation(out=V[:, g, 16:528], in_=pb[i], func=AF.Copy)),
                 R=[bpb[i]], W=[bvp[s][g]])
            rv = [bvp[s][g], bvph[s]]
            P.op("vector", (lambda e, g=g: e.tensor_tensor(out=sA[:, 1:528], in0=V[:, g, 1:528], in1=V[:, g, 0:527], op=ALU.add)),
                 R=rv, W=[bsA])
            fin, bfin = sA, bsA
            if w >= 4:
                P.op("vector", lambda e: e.tensor_tensor(out=sB[:, 3:528], in0=sA[:, 3:528], in1=sA[:, 1:526], op=ALU.add),
                     R=[bsA], W=[bsB])
                fin, bfin = sB, bsB
            if w >= 8:
                P.op("vector", lambda e: e.tensor_tensor(out=sA[:, 7:528], in0=sB[:, 7:528], in1=sB[:, 3:524], op=ALU.add),
                     R=[bsB], W=[bsA])
                fin, bfin = sA, bsA
            if w >= 16:
                P.op("vector", lambda e: e.tensor_tensor(out=sB[:, 15:528], in0=sA[:, 15:528], in1=sA[:, 7:520], op=ALU.add),
                     R=[bsA], W=[bsB])
                fin, bfin = sB, bsB
            P.op("vector", (lambda e, fin=fin, g=g, gi=gi, w=w: e.scalar_tensor_tensor(out=pl[gi], in0=fin[:, 16:528], scalar=1.0 / w,
                                                                                      in1=V[:, g, 16:528], op0=ALU.mult, op1=ALU.subtract)),
                 R=[bfin] + rv, W=[bpl[gi]])
            if first:
                P.op("vector", (lambda e, fin=fin, w=w: e.tensor_tensor(out=tmpc[:, 0:w - 1], in0=fin[:, 16:16 + w - 1], in1=C.invc[:, 0:w - 1], op=ALU.mult)),
                     R=[bfin, C.bconst], W=[btmpc])
                P.op("vector", (lambda e, g=g, gi=gi, w=w: e.tensor_tensor(out=pl[gi][:, 0:w - 1], in0=tmpc[:, 0:w - 1], in1=V[:, g, 16:16 + w - 1], op=ALU.subtract)),
                     R=[btmpc] + rv, W=[bpl[gi]])
        for c in range(4):
            ci = c % 2
            i = proj(4 + c, s)
            P.op("scalar", (lambda e, i=i, ci=ci: e.activation(out=cgs[ci], in_=pb[i], func=AF.Copy)),
                 R=[bpb[i]], W=[bcgs[ci]])
            i = proj(8 + c, s)
            P.op("vector", (lambda e, i=i, ci=ci, c=c: e.tensor_tensor(out=cv[s][:, c, 2:514], in0=pb[i], in1=cgs[ci], op=ALU.mult)),
                 R=[bpb[i], bcgs[ci]], W=[bcv[s][c]])
            i = proj(c, s)
            w = lambda k, c=c: pv[:, CA0 + c * 3 + k:CA0 + c * 3 + k + 1]
            P.op("vector", (lambda e, ci=ci, c=c, w=w: e.tensor_scalar(out=tt[ci], in0=cv[s][:, c, 2:514], scalar1=w(2), scalar2=None, op0=ALU.mult)),
                 R=[bcv[s][c], pvb], W=[btt[ci]])
            P.op("vector", (lambda e, ci=ci, c=c, w=w: e.scalar_tensor_tensor(out=tt[ci], in0=cv[s][:, c, 1:513], scalar=w(1), in1=tt[ci], op0=ALU.mult, op1=ALU.add)),
                 R=[bcv[s][c], bcvh[s], pvb, btt[ci]], W=[btt[ci]])
            P.op("vector", (lambda e, ci=ci, c=c, w=w: e.scalar_tensor_tensor(out=tt[ci], in0=cv[s][:, c, 0:512], scalar=w(0), in1=tt[ci], op0=ALU.mult, op1=ALU.add)),
                 R=[bcv[s][c], bcvh[s], pvb, btt[ci]], W=[btt[ci]])
            P.op("vector", (lambda e, ci=ci, c=c, i=i: e.tensor_tensor(out=cat[:, c, :], in0=tt[ci], in1=pb[i], op=ALU.mult)),
                 R=[btt[ci], bpb[i]], W=[bcat[c]])
        for g in range(4):
            gi = g
            i = nextbank()
            P.op("tensor", (lambda e, i=i, g=g, gi=gi: e.matmul(pb[i], wpool[:, g, :], pl[gi], start=True, stop=True)),
                 R=[bwp, bpl[gi]], W=[bpb[i]])
            P.op("scalar", (lambda e, i=i, g=g: e.activation(out=cat[:, 4 + g, :], in_=pb[i], func=AF.Copy, scale=pv[:, PS0 + g:PS0 + g + 1])),
                 R=[bpb[i], pvb], W=[bcat[4 + g]])
        if b + 1 < NB:
            rms(b + 1)
        for m in range(8):
            i = m % 2
            for c in range(8):
                P.op("tensor", (lambda e, m=m, c=c, i=i: e.matmul(po[i], wout[:, c, m * 128:(m + 1) * 128], cat[:, c, :],
                                                                   start=(c == 0), stop=(c == 7))),
                     R=[bwout[c], bcat[c]], W=[bpo[i]])
            P.op("vector", (lambda e, m=m, i=i: e.tensor_tensor(out=xb[s][:, m, :], in0=po[i], in1=xb[s][:, m, :], op=ALU.add)),
                 R=[bpo[i], bx[s]], W=[bx[s]])
        P.op("sync", lambda e, b=b: e.dma_start(out=dv[:, :, b * 512:(b + 1) * 512], in_=xb[s]), R=[bx[s]], dma=True)
        if b + 2 < NB:
            load_x(b + 2)

    for b in range(NB):
        do_block(b)
    P.barrier()
    A.pop()


OD_GQ = 16
OD_KN = 17
OD_CB = 20
OD_LG = 24
OD_LB = 28
OD_DW = 32


def phase_conf(P, C, l, src, dst):
    A = C.A
    A.push()
    o_ = l // 2
    wcf = A.alloc(BF16, 8, 1024)
    wout = A.alloc(BF16, 4, 1024)
    bwcf, bwout = bufs(8), bufs(4)
    load_weight_rows(P, wcf, C.W["w_in_odd"][o_, :, 1304:2328], 8, bwcf)
    load_weight_rows(P, wout, C.W["w_out_odd"][o_, 512:1024, :], 4, bwout)
    pvb = Buf()
    P.op("sync", lambda e: e.dma_start(out=C.pv, in_=C.W["pvec"][l]), W=[pvb], dma=True)
    pv = C.pv
    dg = A.alloc(BF16, 124, 128)
    bdg = Buf()
    for idx in range(124):
        P.op("vector", (lambda e, idx=idx: e.tensor_scalar(out=dg[:, idx, :], in0=C.identb, scalar1=pv[:, OD_DW + idx:OD_DW + idx + 1],
                                                          scalar2=None, op0=ALU.mult)),
             R=[pvb, C.bconst], W=[bdg])
    NX = 3
    xb = [A.alloc(F32, 8, 512) for _ in range(NX)]
    bx = bufs(NX)
    h = [A.alloc(BF16, 8, 512) for _ in range(2)]
    bh = bufs(2)
    C.sq = A.alloc(BF16, 8, 512)
    C.bsq = Buf()
    C.rstd = A.alloc(F32, 512)
    C.brstd = Buf()
    z = [A.alloc(BF16, 4, 542) for _ in range(2)]
    bz = [bufs(4) for _ in range(2)]
    bzh = bufs(2)
    sgb = [A.alloc(F32, 512) for _ in range(2)]
    bsgb = bufs(2)
    zc = [A.alloc(F32, 4, 512) for _ in range(2)]
    bzc = [bufs(4) for _ in range(2)]
    zcb = A.alloc(BF16, 4, 512)
    bzcb = bufs(4)
    zq = A.alloc(BF16, 4, 512)
    bzq = bufs(4)
    mu = [A.alloc(F32, 512) for _ in range(2)]
    e2 = [A.alloc(F32, 512) for _ in range(2)]
    bmu, be2 = bufs(2), bufs(2)
    m2 = A.alloc(F32, 512)
    rs = A.alloc(F32, 512)
    bm2, brs = Buf(), Buf()
    tt = [A.alloc(F32, 512) for _ in range(2)]
    btt = bufs(2)
    cat = A.alloc(BF16, 4, 512)
    bcat = bufs(4)
    ps = C.psum
    bank = lambda i: ps[:, i * 512:(i + 1) * 512]
    pb = [bank(i) for i in range(3)]
    bpb = bufs(3)
    pc = [bank(3), bank(4)]
    bpc = bufs(2)
    pmu, pe2 = bank(5), bank(6)
    bpmu, bpe2 = Buf(), Buf()
    C.ps_ss = bank(7)
    C.bps_ss = Buf()
    sv, dv = xview(src), xview(dst)
    rr = [0]

    def nextbank():
        i = rr[0] % 3
        rr[0] += 1
        return i

    def load_x(b):
        s = b % NX
        P.op("sync", lambda e: e.dma_start(out=xb[s], in_=sv[:, :, b * 512:(b + 1) * 512]), W=[bx[s]], dma=True)

    def rms(b):
        rms_block(P, C, xb[b % NX], bx[b % NX], h[b % 2], bh[b % 2], 0, pvb)

    def proj(col0, s):
        i = nextbank()
        for k in range(8):
            P.op("tensor", (lambda e, k=k, i=i: e.matmul(pb[i], wcf[:, k, col0:col0 + 128], h[s][:, k, :],
                                                          start=(k == 0), stop=(k == 7))),
                 R=[bwcf[k], bh[s]], W=[bpb[i]])
        return i

    def part_a(b, hooks=None):
        s = b % 2
        first = (b % (SEQ // 512) == 0)
        if first:
            P.op("gpsimd", lambda e: e.memset(z[s][:, :, 0:30], 0.0), W=[bzh[s]])
        else:
            P.op("gpsimd", lambda e: e.tensor_copy(out=z[s][:, :, 0:30], in_=z[1 - s][:, :, 512:542]),
                 R=bz[1 - s], W=[bzh[s]])
        for c in range(4):
            ci = c % 2
            ia = proj(c * 128, s)
            ib = proj(512 + c * 128, s)
            P.op("scalar", (lambda e, ib=ib, ci=ci: e.activation(out=sgb[ci], in_=pb[ib], func=AF.Sigmoid)),
                 R=[bpb[ib]], W=[bsgb[ci]])
            P.op("vector", (lambda e, ia=ia, ci=ci, c=c: e.tensor_tensor(out=z[s][:, c, 30:542], in0=pb[ia], in1=sgb[ci], op=ALU.mult)),
                 R=[bpb[ia], bsgb[ci]], W=[bz[s][c]])
            if hooks is not None:
                hooks[c]()
            for k in range(31):
                P.op("tensor", (lambda e, c=c, k=k, ci=ci: e.matmul(pc[ci], dg[:, c * 31 + k, :], z[s][:, c, k:k + 512],
                                                                     start=(k == 0), stop=(k == 30))),
                     R=[bdg, bz[s][c], bzh[s]], W=[bpc[ci]])
            P.op("scalar", (lambda e, c=c, ci=ci: e.activation(out=zc[s][:, c, :], in_=pc[ci], func=AF.Identity,
                                                               bias=pv[:, OD_CB + c:OD_CB + c + 1], scale=1.0)),
                 R=[bpc[ci], pvb], W=[bzc[s][c]])
            P.op("scalar", (lambda e, c=c, ci=ci: e.activation(out=zq[:, c, :], in_=pc[ci], func=AF.Square,
                                                               bias=pv[:, OD_CB + c:OD_CB + c + 1], scale=1.0)),
                 R=[bpc[ci], pvb], W=[bzq[c]])
            P.op("gpsimd", (lambda e, c=c: e.tensor_copy(out=zcb[:, c, :], in_=zc[s][:, c, :])), R=[bzc[s][c]], W=[bzcb[c]])
        for c in range(4):
            P.op("tensor", (lambda e, c=c: e.matmul(pmu, C.ones512, zcb[:, c, :], start=(c == 0), stop=(c == 3))),
                 R=[bzcb[c], C.bconst], W=[bpmu])
        for c in range(4):
            P.op("tensor", (lambda e, c=c: e.matmul(pe2, C.ones512, zq[:, c, :], start=(c == 0), stop=(c == 3))),
                 R=[bzq[c], C.bconst], W=[bpe2])
        P.op("scalar", lambda e: e.activation(out=mu[s], in_=pmu, func=AF.Copy), R=[bpmu], W=[bmu[s]])
        P.op("scalar", lambda e: e.activation(out=e2[s], in_=pe2, func=AF.Copy), R=[bpe2], W=[be2[s]])

    def part_b1(b, cs_=None):
        s = b % 2
        if cs_ is None or cs_ == 0:
            P.op("vector", lambda e: e.tensor_tensor(out=m2, in0=mu[s], in1=mu[s], op=ALU.mult), R=[bmu[s]], W=[bm2])
            P.op("vector", lambda e: e.tensor_tensor(out=m2, in0=e2[s], in1=m2, op=ALU.subtract), R=[be2[s], bm2], W=[bm2])
            P.op("scalar", lambda e: e.activation(out=rs, in_=m2, func=AF.Ln, bias=C.epsc[:, 0:1], scale=1.0),
                 R=[bm2, C.bconst], W=[brs])
            P.op("scalar", lambda e: e.activation(out=rs, in_=rs, func=AF.Exp, scale=-0.5), R=[brs], W=[brs])
        for c in (range(4) if cs_ is None else [cs_]):
            ci = c % 2
            P.op("vector", (lambda e, c=c, ci=ci: e.tensor_tensor(out=tt[ci], in0=zc[s][:, c, :], in1=mu[s], op=ALU.subtract)),
                 R=[bzc[s][c], bmu[s]], W=[btt[ci]])
            P.op("vector", (lambda e, c=c, ci=ci: e.tensor_tensor(out=tt[ci], in0=tt[ci], in1=rs, op=ALU.mult)),
                 R=[btt[ci], brs], W=[btt[ci]])
            P.op("scalar", (lambda e, c=c, ci=ci: e.activation(out=cat[:, c, :], in_=tt[ci], func=AF.Silu,
                                                               bias=pv[:, OD_LB + c:OD_LB + c + 1], scale=pv[:, OD_LG + c:OD_LG + c + 1])),
                 R=[btt[ci], pvb], W=[bcat[c]])

    def part_b2(b):
        sx = b % NX
        for m in range(8):
            i = nextbank()
            for c in range(4):
                P.op("tensor", (lambda e, m=m, c=c, i=i: e.matmul(pb[i], wout[:, c, m * 128:(m + 1) * 128], cat[:, c, :],
                                                                   start=(c == 0), stop=(c == 3))),
                     R=[bwout[c], bcat[c]], W=[bpb[i]])
            P.op("vector", (lambda e, m=m, i=i: e.tensor_tensor(out=xb[sx][:, m, :], in0=pb[i], in1=xb[sx][:, m, :], op=ALU.add)),
                 R=[bpb[i], bx[sx]], W=[bx[sx]])
        P.op("sync", lambda e: e.dma_start(out=dv[:, :, b * 512:(b + 1) * 512], in_=xb[sx]), R=[bx[sx]], dma=True)

    load_x(0)
    load_x(1)
    load_x(2)
    rms(0)
    part_a(0)
    rms(1)
    for b in range(NB):
        if b + 1 < NB:
            part_a(b + 1, hooks=[(lambda b=b, c=c: part_b1(b, c)) for c in range(4)])
        else:
            part_b1(b)
        if b + 2 < NB:
            rms(b + 2)
        part_b2(b)
        if b + 3 < NB:
            load_x(b + 3)
    P.barrier()
    A.pop()


KC0, VC0, KS0, VS0, KW0, VW0, GT0 = 512, 640, 768, 896, 1024, 1152, 1280
OD_POS = 160
TINY = 1e-30


def phase_nsa(P, C, l, src, addsrc, dst):
    A = C.A
    A.push()
    o_ = l // 2
    pv = C.pv
    W = C.W
    NBS = SEQ // 512
    wnsa = A.alloc(BF16, 8, 1304)
    bwn = bufs(8)
    wout = A.alloc(BF16, 4, 1024)
    bwo = bufs(4)
    for k in range(8):
        rows = W["w_in_odd"][o_, k * 128:(k + 1) * 128, :]
        P.op("gpsimd", (lambda e, k=k, rows=rows: e.dma_start(out=wnsa[:, k, :], in_=rows[:, 0:1304])),
             W=[bwn[k]], dma=True)
    load_weight_rows(P, wout, W["w_out_odd"][o_, 0:512, :], 4, bwo)
    pvb = Buf()
    P.op("sync", lambda e: e.dma_start(out=pv, in_=W["pvec"][l]), W=[pvb], dma=True)
    prot = A.alloc(BF16, 128)
    bd = A.alloc(BF16, 128)
    tri1 = A.alloc(BF16, 512)
    tri2 = A.alloc(BF16, 512)
    gext = A.alloc(F32, 126)
    mext = A.alloc(F32, 126)
    ovl = A.alloc(BF16, 2, 64)
    posb = A.alloc(BF16, 32)
    bcn = Buf()
    CN = W["nsac"]
    CB = W["nsacb"]
    P.op("sync", lambda e: e.dma_start(out=prot, in_=CB[:, 0:128]), W=[bcn], dma=True)
    P.op("sync", lambda e: e.dma_start(out=bd, in_=CB[:, 128:256]), W=[bcn], dma=True)
    for r in range(4):
        P.op("sync", (lambda e, r=r: e.dma_start(out=tri1[:, r * 128:(r + 1) * 128], in_=CB[:, 256:384])), W=[bcn], dma=True)
        P.op("sync", (lambda e, r=r: e.dma_start(out=tri2[:, r * 128:(r + 1) * 128], in_=CB[:, 384:512])), W=[bcn], dma=True)
    P.op("sync", lambda e: e.dma_start(out=gext, in_=CN[:, 512:638]), W=[bcn], dma=True)
    P.op("sync", lambda e: e.dma_start(out=mext, in_=CN[:, 640:766]), W=[bcn], dma=True)
    P.op("sync", lambda e: e.dma_start(out=ovl, in_=CB[:, 768:896].rearrange("p (a j) -> p a j", a=2)), W=[bcn], dma=True)
    P.op("vector", lambda e: e.tensor_copy(out=posb, in_=pv[:, OD_POS:OD_POS + 32]), R=[pvb], W=[bcn])
    QT = A.alloc(BF16, 4, SEQ)
    bQT = [bufs(NBS) for _ in range(4)]
    KS2 = [A.alloc(BF16, SEQ) for _ in range(2)]
    bKS = [bufs(NBS) for _ in range(2)]
    bE = Buf()
    KWT = A.alloc(BF16, SEQ)
    bKW = bufs(NBS)
    KCT = A.alloc(BF16, SEQ)
    bKC = bufs(NBS)
    VCT = A.alloc(BF16, SEQ)
    bVC = bufs(NBS)
    VS1 = A.alloc(BF16, 32, 2, 66)
    VW1 = A.alloc(BF16, 32, 2, 66)
    bVS = bufs(NBS)
    bVW = bufs(NBS)
    bones = Buf()
    GT = A.alloc(F32, 32, 24)
    bGT = bufs(NBS)
    KCN = [A.alloc(BF16, 256) for _ in range(2)]
    bKCN = bufs(2)
    VCM1 = A.alloc(BF16, 2, 2, 66)
    bVCM = bufs(2)
    selin = A.alloc(F32, 128)
    bselin = Buf()
    bz = A.alloc(F32, 4)
    bbz = Buf()
    P.op("sync", lambda e: e.dma_start(out=KS2[0][64:128, :], in_=W["eind"][:, :]), W=[bE], dma=True)
    P.op("sync", lambda e: e.dma_start(out=KS2[1][0:64, :], in_=W["eind"][:, :]), W=[bE], dma=True)
    P.op("vector", lambda e: e.memset(VS1[:, :, :, 64:65], 1.0), W=[bones])
    P.op("vector", lambda e: e.memset(VW1[:, :, :, 64:65], 1.0), W=[bones])
    P.op("vector", lambda e: e.memset(VCM1[:, :, :, 64:65], 1.0), W=[bones])
    P.op("vector", lambda e: e.memset(selin, 0.0), W=[bselin])
    ps = C.psum
    bank = lambda i: ps[:, i * 512:(i + 1) * 512]
    sv, dv, av = xview(src), xview(dst), xview(addsrc)
    P.barrier()

    def load_cmp(w1a, w1b, w2k, w2v, bw):
        for kind in range(2):
            w1 = W["cmp_w1"][o_, kind]
            w1v = w1.rearrange("(c p) h -> p c h", p=128)
            w1s = w1.rearrange("(c two d) h -> two d c h", two=2, d=64)
            import os
            lc = int(os.environ.get("LC", 15))
            for c4 in range(4):
                cs_ = slice(c4 * 4, c4 * 4 + 4)
                if lc & 1:
                    P.op("gpsimd", (lambda e, kind=kind, w1v=w1v, cs_=cs_: e.dma_start(out=w1a[kind][:, cs_, :], in_=w1v[:, cs_, :])),
                         W=[bw], dma=True)
                if lc & 2:
                    P.op("gpsimd", (lambda e, kind=kind, w1s=w1s, cs_=cs_: e.dma_start(out=w1b[kind][0:64, cs_, :], in_=w1s[1][:, cs_, :])),
                         W=[bw], dma=True)
                    P.op("gpsimd", (lambda e, kind=kind, w1s=w1s, cs_=cs_: e.dma_start(out=w1b[kind][64:128, cs_, :], in_=w1s[0][:, cs_, :])),
                         W=[bw], dma=True)
        w2 = W["cmp_w2"][o_]
        if lc & 4:
            for half in range(2):
                P.op("gpsimd", (lambda e, half=half: e.dma_start(out=w2k[:, :, half * 64:(half + 1) * 64],
                                                                  in_=w2[0].rearrange("(c p) d -> p c d", p=128))), W=[bw], dma=True)
        if lc & 8:
            P.op("gpsimd", lambda e: e.dma_start(out=w2v, in_=w2[1].rearrange("(c p) d -> p c d", p=128)), W=[bw], dma=True)

    def w1sel(w1a, w1b, kind, g, i):
        t = w1a[kind] if (i % 2) == g else w1b[kind]
        return t[g * 64:(g + 1) * 64, i // 2, :]

    def stage_cmp(first):
        A.push()
        w1a = [A.alloc(BF16, 16, 256) for _ in range(2)]
        w1b = [A.alloc(BF16, 16, 256) for _ in range(2)]
        w2k = A.alloc(BF16, 2, 128)
        w2v = A.alloc(BF16, 2, 64)
        bw = Buf()
        load_cmp(w1a, w1b, w2k, w2v, bw)
        gz = [A.alloc(BF16, 256) for _ in range(2)]
        bgz = bufs(2)
        sqk = A.alloc(BF16, 256)
        bsqk = Buf()
        rk = A.alloc(F32, 256)
        brk = Buf()
        pz = [bank(0), bank(1)]
        bpz = bufs(2)
        pk, pssk, pvc, pbias = bank(2), bank(3), bank(4), bank(5)
        bpk, bpssk, bpvc, bpbias = Buf(), Buf(), Buf(), Buf()
        import os
        cdbg = int(os.environ.get("CMP_DBG", 9))
        if cdbg < 2:
            P.barrier()
            A.pop()
            return
        if first:
            for kind in range(2 if cdbg != 2 else 1):
                g = kind
                for hc in range(2):
                    col = kind * 2 + hc
                    for i in range(32):
                        P.op("tensor", (lambda e, kind=kind, g=g, hc=hc, i=i, col=col: e.matmul(
                            pbias[:, col:col + 1], w1sel(w1a, w1b, kind, g, i)[:, hc * 128:(hc + 1) * 128],
                            posb[g * 64:(g + 1) * 64, i:i + 1], start=(i == 0), stop=(i == 31))),
                            R=[bw, bcn], W=[bpbias])
            P.op("vector", lambda e: e.tensor_copy(out=bz, in_=pbias[:, 0:4]), R=[bpbias], W=[bbz])
        if cdbg < 4:
            P.barrier()
            A.pop()
            return
        for kind in range(2):
            X = KCT if kind == 0 else VCT
            bX = bKC if kind == 0 else bVC
            Xv = X.rearrange("p (n s) -> p n s", s=16)
            for g in range(2 if cdbg > 4 else 1):
                for hc in range(2):
                    for i in range(32):
                        rhs = Xv[g * 64:(g + 1) * 64, 0:255, i] if i < 16 else Xv[g * 64:(g + 1) * 64, 1:256, i - 16]
                        P.op("tensor", (lambda e, kind=kind, g=g, hc=hc, i=i, rhs=rhs: e.matmul(
                            pz[hc][:, 0:255], w1sel(w1a, w1b, kind, g, i)[:, hc * 128:(hc + 1) * 128], rhs,
                            start=(i == 0), stop=(i == 31))), R=[bw] + bX, W=[bpz[hc]])
                    P.op("scalar", (lambda e, kind=kind, hc=hc: e.activation(out=gz[hc][:, 0:255], in_=pz[hc][:, 0:255],
                                                                            func=AF.Gelu_apprx_tanh,
                                                                            bias=bz[:, kind * 2 + hc:kind * 2 + hc + 1], scale=1.0)),
                         R=[bpz[hc], bbz], W=[bgz[hc]])
                if cdbg < 6:
                    continue
                if kind == 0:
                    for hc in range(2):
                        P.op("tensor", (lambda e, hc=hc: e.matmul(pk[:, 0:255], w2k[:, hc, :], gz[hc][:, 0:255],
                                                                   start=(hc == 0), stop=(hc == 1))),
                             R=[bw, bgz[hc]], W=[bpk])
                    P.op("scalar", lambda e: e.activation(out=sqk[:, 0:255], in_=pk[:, 0:255], func=AF.Square), R=[bpk], W=[bsqk])
                    P.op("tensor", lambda e: e.matmul(pssk[:, 0:255], bd, sqk[:, 0:255], start=True, stop=True),
                         R=[bsqk, bcn], W=[bpssk])
                    P.op("scalar", lambda e: e.activation(out=rk[:, 0:255], in_=pssk[:, 0:255], func=AF.Sqrt,
                                                          bias=C.epsc[:, 0:1], scale=1.0), R=[bpssk, C.bconst], W=[brk])
                    P.op("vector", lambda e: e.reciprocal(out=rk[:, 0:255], in_=rk[:, 0:255]), R=[brk], W=[brk])
                    P.op("vector", (lambda e, g=g: e.scalar_tensor_tensor(out=KCN[g][:, 0:255], in0=pk[:, 0:255],
                                                                         scalar=pv[:, OD_KN + 2:OD_KN + 3], in1=rk[:, 0:255],
                                                                         op0=ALU.mult, op1=ALU.mult)),
                         R=[bpk, brk, pvb], W=[bKCN[g]])
                else:
                    for nt in range(2):
                        nn = 128 if nt == 0 else 127
                        for hc in range(2):
                            P.op("tensor", (lambda e, nt=nt, nn=nn, hc=hc: e.matmul(
                                pvc[0:nn, nt * 64:(nt + 1) * 64], gz[hc][:, nt * 128:nt * 128 + nn], w2v[:, hc, :],
                                start=(hc == 0), stop=(hc == 1))), R=[bw, bgz[hc]], W=[bpvc])
                    P.op("scalar", (lambda e, g=g: e.activation(out=VCM1[:, g, :, 0:64],
                                                                in_=pvc[:, 0:128].rearrange("p (a d) -> p a d", a=2), func=AF.Copy)),
                         R=[bpvc], W=[bVCM[g]])
        P.barrier()
        A.pop()

    def stage_proj(sq_):
        A.push()
        xb = A.alloc(F32, 8, 512)
        bx = Buf()
        h = [A.alloc(BF16, 8, 512) for _ in range(2)]
        bh = bufs(2)
        C.sq = A.alloc(BF16, 8, 512)
        C.bsq = Buf()
        C.rstd = A.alloc(F32, 512)
        C.brstd = Buf()
        cs = [A.alloc(F32, 512) for _ in range(2)]
        sn = [A.alloc(F32, 512) for _ in range(2)]
        bcs = bufs(2)
        yb = [A.alloc(BF16, 512) for _ in range(2)]
        byb = bufs(2)
        qsq = [A.alloc(BF16, 512) for _ in range(2)]
        bqsq = bufs(2)
        t1 = [A.alloc(F32, 512) for _ in range(2)]
        bt1 = bufs(2)
        t2 = [A.alloc(F32, 512) for _ in range(2)]
        bt2 = bufs(2)
        rq = [A.alloc(F32, 512) for _ in range(2)]
        brq = bufs(2)
        pq = [bank(0), bank(1), bank(2)]
        bpq = bufs(3)
        ppr = [bank(3), bank(4)]
        bppr = bufs(2)
        pss = bank(5)
        bpss = Buf()
        ptm = bank(6)
        bptm = Buf()
        C.ps_ss = bank(7)
        C.bps_ss = Buf()
        rr = [0]
        cc = [0]
        tok0 = sq_ * SEQ

        def load_x(bs):
            P.op("sync", lambda e: e.dma_start(out=xb, in_=sv[:, :, tok0 + bs * 512:tok0 + (bs + 1) * 512]), W=[bx], dma=True)

        def rms(bs):
            s = bs % 2
            rms_block(P, C, xb, bx, h[s], bh[s], 0, pvb)

        def chunk(bs, col0, gcol, qscale, dests, norm=True):
            s = bs % 2
            ts = bs % 2
            i = rr[0] % 3
            rr[0] += 1
            j = cc[0] % 2
            cc[0] += 1
            for k in range(8):
                P.op("tensor", (lambda e, k=k: e.matmul(pq[i], wnsa[:, k, col0:col0 + 128], h[s][:, k, :],
                                                         start=(k == 0), stop=(k == 7))),
                     R=[bwn[k], bh[s]], W=[bpq[i]])
            if gcol is not None:
                P.op("scalar", lambda e: e.activation(out=yb[j], in_=pq[i], func=AF.Copy, scale=pv[:, gcol:gcol + 1]),
                     R=[bpq[i], pvb], W=[byb[j]])
            else:
                P.op("scalar", lambda e: e.activation(out=yb[j], in_=pq[i], func=AF.Copy), R=[bpq[i]], W=[byb[j]])
            P.op("tensor", lambda e: e.matmul(ppr[j], prot, yb[j], start=True, stop=True), R=[byb[j], bcn], W=[bppr[j]])
            if norm:
                P.op("scalar", lambda e: e.activation(out=qsq[j], in_=pq[i], func=AF.Square), R=[bpq[i]], W=[bqsq[j]])
                P.op("tensor", lambda e: e.matmul(pss, bd, qsq[j], start=True, stop=True), R=[bqsq[j], bcn], W=[bpss])
                P.op("scalar", lambda e: e.activation(out=rq[j], in_=pss, func=AF.Ln, bias=C.epsc[:, 0:1], scale=1.0),
                     R=[bpss, C.bconst], W=[brq[j]])
                P.op("scalar", lambda e: e.activation(out=rq[j], in_=rq[j], func=AF.Exp, scale=-0.5), R=[brq[j]], W=[brq[j]])
            if gcol is not None:
                P.op("vector", lambda e: e.scalar_tensor_tensor(out=t1[j], in0=pq[i], scalar=pv[:, gcol:gcol + 1], in1=cs[ts],
                                                                op0=ALU.mult, op1=ALU.mult),
                     R=[bpq[i], pvb, bcs[ts], byb[j], bqsq[j]], W=[bt1[j]])
            else:
                P.op("vector", lambda e: e.tensor_tensor(out=t1[j], in0=pq[i], in1=cs[ts], op=ALU.mult),
                     R=[bpq[i], bcs[ts], byb[j]], W=[bt1[j]])
            P.op("vector", lambda e: e.tensor_tensor(out=t2[j], in0=ppr[j], in1=sn[ts], op=ALU.mult),
                 R=[bppr[j], bcs[ts]], W=[bt2[j]])
            if norm:
                P.op("gpsimd", lambda e: e.tensor_tensor(out=t1[j], in0=t1[j], in1=t2[j], op=ALU.add),
                     R=[bt1[j], bt2[j]], W=[bt1[j]])
                for (dst_ap, prange, bdst) in dests:
                    lo, hi = prange
                    P.op("vector", (lambda e, dst_ap=dst_ap, lo=lo, hi=hi: e.scalar_tensor_tensor(
                        out=dst_ap, in0=t1[j][lo:hi], scalar=qscale, in1=rq[j][lo:hi], op0=ALU.mult, op1=ALU.mult)),
                        R=[bt1[j], brq[j]], W=[bdst])
            else:
                for (dst_ap, prange, bdst) in dests:
                    lo, hi = prange
                    P.op("vector", (lambda e, dst_ap=dst_ap, lo=lo, hi=hi: e.tensor_tensor(
                        out=dst_ap, in0=t1[j][lo:hi], in1=t2[j][lo:hi], op=ALU.add)),
                        R=[bt1[j], bt2[j]], W=[bdst])

        def do_block(bs):
            s = bs % 2
            ts = bs % 2
            c0 = bs * 512
            P.op("sync", lambda e: e.dma_start(out=cs[ts], in_=W["ropec"][:, c0:c0 + 512]), W=[bcs[ts]], dma=True)
            P.op("sync", lambda e: e.dma_start(out=sn[ts], in_=W["ropes"][:, c0:c0 + 512]), W=[bcs[ts]], dma=True)
            import os
            dbg = int(os.environ.get("NSA_DBG", 9))
            if dbg < 2:
                return
            for m in range(4):
                chunk(bs, m * 128, OD_GQ, 0.125, [(QT[:, m, c0:c0 + 512], (0, 128), bQT[m][bs])])
            if dbg < 3:
                return
            chunk(bs, KS0, OD_KN + 0, 1.0, [(KS2[0][0:64, c0:c0 + 512], (0, 64), bKS[0][bs]),
                                            (KS2[1][64:128, c0:c0 + 512], (64, 128), bKS[1][bs])])
            chunk(bs, KW0, OD_KN + 1, 1.0, [(KWT[:, c0:c0 + 512], (0, 128), bKW[bs])])
            if dbg < 4:
                return
            chunk(bs, KC0, None, 1.0, [(KCT[:, c0:c0 + 512], (0, 128), bKC[bs])], norm=False)
            if dbg < 5:
                return
            i = rr[0] % 3
            rr[0] += 1
            for k in range(8):
                P.op("tensor", (lambda e, k=k: e.matmul(pq[i], wnsa[:, k, VC0:VC0 + 128], h[s][:, k, :],
                                                         start=(k == 0), stop=(k == 7))),
                     R=[bwn[k], bh[s]], W=[bpq[i]])
            P.op("scalar", lambda e: e.activation(out=VCT[:, c0:c0 + 512], in_=pq[i], func=AF.Copy), R=[bpq[i]], W=[bVC[bs]])
            if dbg < 6:
                return
            for jt in range(4):
                tile_ = bs * 4 + jt
                for gi_, (col0, n, off) in enumerate(((VS0, 128, 0), (VW0, 128, 128), (GT0, 24, 256))):
                    if dbg < 7 + gi_ and dbg < 9 and not (dbg == 6 and gi_ == 0):
                        continue
                    for k in range(8):
                        P.op("tensor", (lambda e, k=k, jt=jt, col0=col0, n=n, off=off: e.matmul(
                            ptm[:, off:off + n], h[s][:, k, jt * 128:(jt + 1) * 128], wnsa[:, k, col0:col0 + n],
                            start=(k == 0), stop=(k == 7))), R=[bwn[k], bh[s]], W=[bptm])
                if dbg == 6:
                    continue
                cpm = int(os.environ.get("NSA_CP", 7))
                if cpm & 1:
                    P.op("scalar", (lambda e, tile_=tile_: e.activation(out=VS1[:, tile_, :, 0:64],
                                                                       in_=ptm[:, 0:128].rearrange("p (a d) -> p a d", a=2), func=AF.Copy)),
                         R=[bptm, bones], W=[bVS[bs]])
                if cpm & 2:
                    P.op("scalar", (lambda e, tile_=tile_: e.activation(out=VW1[:, tile_, :, 0:64],
                                                                       in_=ptm[:, 128:256].rearrange("p (a d) -> p a d", a=2), func=AF.Copy)),
                         R=[bptm, bones], W=[bVW[bs]])
                if cpm & 4:
                    P.op("scalar", (lambda e, tile_=tile_: e.activation(out=GT[:, tile_, :], in_=ptm[:, 256:280], func=AF.Sigmoid)),
                         R=[bptm], W=[bGT[bs]])

        load_x(0)
        rms(0)
        for bs in range(NBS):
            if bs + 1 < NBS:
                load_x(bs + 1)
                rms(bs + 1)
            do_block(bs)
        P.barrier()
        A.pop()

    def stage_attn(sq_):
        import os
        A.push()
        cm0 = A.alloc(BF16, 32, 128)
        cm1 = A.alloc(BF16, 16, 128)
        bcm = Buf()
        P.op("sync", lambda e: e.dma_start(out=cm0, in_=W["cmask0"].rearrange("p (a q) -> p a q", a=32)), W=[bcm], dma=True)
        P.op("sync", lambda e: e.dma_start(out=cm1, in_=W["cmask1"].rearrange("p (a q) -> p a q", a=16)), W=[bcm], dma=True)
        Q2 = [A.alloc(BF16, 512) for _ in range(2)]
        bQ2q = bufs(2)
        bQ2b = bufs(2)
        NPE = 5
        Pe = [A.alloc(BF16, 512) for _ in range(NPE)]
        bPe = bufs(NPE)
        Pc = [A.alloc(BF16, 512) for _ in range(2)]
        bPc = bufs(2)
        lmx = A.alloc(F32, 4)
        blmx = Buf()
        cf = A.alloc(F32, 4)
        bcf = Buf()
        rlc = A.alloc(F32, 4)
        brlc = Buf()
        tmpi = A.alloc(F32, 4, 64)
        btmpi = Buf()
        imp = A.alloc(F32, 64)
        bimp = Buf()
        score = A.alloc(F32, 64)
        bscore = Buf()
        m8 = A.alloc(F32, 8)
        bm8 = Buf()
        oacc = [A.alloc(F32, 4, 64) for _ in range(2)]
        boacc = bufs(2)
        otmp = A.alloc(F32, 4, 64)
        botmp = Buf()
        OT = A.alloc(BF16, 4, 512)
        bOT = bufs(4)
        xo = A.alloc(F32, 8, 512)
        bxo = Buf()
        NSB = 3
        psc = [bank(0), bank(1), bank(7)]
        bpsc = bufs(NSB)
        poc = bank(2).rearrange("p (r d) -> p r d", r=4)
        pos_ = bank(3).rearrange("p (r d) -> p r d", r=4)
        pow_ = bank(4).rearrange("p (r d) -> p r d", r=4)
        bpoc, bpos, bpow = Buf(), Buf(), Buf()
        pimp = bank(5)[:, 0:256].rearrange("p (r j) -> p r j", r=4)
        misc = bank(6)
        pT = misc[:, 0:128]
        pTo = misc[:, 128:256]
        bpimp = Buf()
        bpT = Buf()
        bpTo = bpT
        rr = [0]
        pp = [0]
        pc_ = [0]
        tok0 = sq_ * SEQ

        def sbank():
            i = rr[0] % NSB
            rr[0] += 1
            return i

        def pslot():
            i = pp[0] % NPE
            pp[0] += 1
            return i

        def q2copy(g, qb):
            hs = slice(g * 64, (g + 1) * 64)
            rQ = [bQT[m][qb // 4] for m in range(4)]
            P.op("gpsimd", lambda e: e.tensor_copy(out=Q2[g][hs, :].rearrange("p (r q) -> p r q", r=4),
                                                   in_=QT[hs, :, qb * 128:(qb + 1) * 128]),
                 R=rQ, W=[bQ2q[g]])

        def gate_coef(g, qb, br, psrc, bpsrc):
            P.op("vector", lambda e: e.tensor_scalar(out=lmx.unsqueeze(2), in0=psrc[:, :, 64:65], scalar1=TINY, scalar2=None, op0=ALU.max),
                 R=[bpsrc], W=[blmx])
            P.op("vector", lambda e: e.reciprocal(out=cf, in_=lmx), R=[blmx], W=[bcf])
            if br == 0:
                P.op("vector", lambda e: e.tensor_copy(out=rlc, in_=cf), R=[bcf], W=[brlc])
            P.op("vector", lambda e: e.tensor_tensor(out=cf, in0=cf,
                                                     in1=GT[:, qb, 12 * g:12 * g + 12].rearrange("p (r b) -> p b r", b=3)[:, br, :], op=ALU.mult),
                 R=[bcf, bGT[qb // 4]], W=[bcf])

        def make_tiles(g, qb, nxt):
            hs = slice(g * 64, (g + 1) * 64)
            bsl = slice((1 - g) * 64, (2 - g) * 64)
            q2 = Q2[g]
            oa = oacc[g]
            boa = boacc[g]
            bc_ = lambda: cf.unsqueeze(2).to_broadcast([128, 4, 64])
            tl = []
            ntl = 1 if qb < 16 else 2
            for nt in range(ntl):
                nn = 128 if nt == 0 else 127
                i = sbank()
                j = pslot()
                jc = pc_[0] % 2
                pc_[0] += 1

                def S(nt=nt, nn=nn, i=i):
                    P.op("tensor", lambda e: e.matmul(psc[i][0:nn, :], KCN[g][hs, nt * 128:nt * 128 + nn], q2[hs, :], start=True, stop=True),
                         R=[bKCN[g], bQ2q[g]], W=[bpsc[i]])

                def E(nt=nt, nn=nn, i=i, j=j, jc=jc):
                    P.op("scalar", lambda e: e.activation(out=Pe[j][0:nn, :], in_=psc[i][0:nn, :], func=AF.Exp), R=[bpsc[i]], W=[bPe[j]])
                    mk = cm0[0:nn, qb, :] if nt == 0 else cm1[0:nn, qb - 16, :]
                    P.op("vector", lambda e: e.tensor_tensor(
                        out=Pc[jc][0:nn, :].rearrange("p (r q) -> p r q", r=4), in0=Pe[j][0:nn, :].rearrange("p (r q) -> p r q", r=4),
                        in1=mk.unsqueeze(1).to_broadcast([nn, 4, 128]), op=ALU.mult),
                        R=[bPe[j], bcm], W=[bPc[jc]])

                def V(nt=nt, nn=nn, jc=jc):
                    for r in range(4):
                        P.op("tensor", (lambda e, r=r: e.matmul(poc[:, r, 0:65], Pc[jc][0:nn, r * 128:(r + 1) * 128], VCM1[0:nn, g, nt, 0:65],
                                                                start=(nt == 0 and r == 0), stop=(nt == ntl - 1))),
                             R=[bPc[jc], bVCM[g], bones], W=[bpoc])
                    for r in range(4):
                        P.op("tensor", (lambda e, r=r: e.matmul(pimp[:, r, :], Pc[jc][0:nn, r * 128:(r + 1) * 128], ovl[0:nn, nt, :],
                                                                start=(nt == 0 and r == 0), stop=(nt == ntl - 1))),
                             R=[bPc[jc], bcn], W=[bpimp])
                tl.append({"S": S, "E": E, "V": V, "post": None})

            def post_cmp():
                gate_coef(g, qb, 0, poc, bpoc)
                P.op("vector", lambda e: e.tensor_tensor(out=tmpi, in0=pimp, in1=rlc.unsqueeze(2).to_broadcast([128, 4, 64]), op=ALU.mult),
                     R=[bpimp, brlc], W=[btmpi])
                P.op("vector", lambda e: e.tensor_reduce(out=imp, in_=tmpi.rearrange("p r j -> p j r"), axis=mybir.AxisListType.X, op=ALU.add),
                     R=[btmpi], W=[bimp])
                w0 = 62 - 2 * qb
                P.op("vector", lambda e: e.tensor_tensor(out=score, in0=imp, in1=mext[:, w0:w0 + 64], op=ALU.mult), R=[bimp, bcn], W=[bscore])
                P.op("vector", lambda e: e.tensor_tensor(out=score, in0=score, in1=gext[:, w0:w0 + 64], op=ALU.add), R=[bscore, bcn], W=[bscore])
                P.op("vector", lambda e: e.memset(score[:, 0:1], 1.0e4), R=[bscore], W=[bscore])
                P.op("vector", lambda e: e.max(out=m8, in_=score), R=[bscore], W=[bm8])
                P.op("vector", lambda e: e.tensor_scalar(out=selin[:, (1 - g) * 64:(2 - g) * 64], in0=score, scalar1=m8[:, 7:8],
                                                         scalar2=NEGBIG, op0=ALU.is_lt, op1=ALU.mult),
                     R=[bscore, bm8], W=[bselin])
                P.op("vector", lambda e: e.tensor_tensor(out=oa, in0=poc[:, :, 0:64], in1=bc_(), op=ALU.mult), R=[bpoc, bcf], W=[boa])

            def post_cmp_pe():
                P.op("tensor", lambda e: e.transpose(pT, selin, C.identf), R=[bselin, C.bconst], W=[bpT])
                P.op("scalar", lambda e: e.activation(out=q2[bsl, :].rearrange("p (r q) -> p r q", r=4),
                                                      in_=pT[bsl, :].unsqueeze(1).to_broadcast([64, 4, 128]), func=AF.Copy),
                     R=[bpT], W=[bQ2b[g], bpT])
            tl[-1]["post"] = post_cmp
            tl[-1]["postpe"] = post_cmp_pe
            tl[-1]["tag"] = ("cmp", g, qb)
            k0 = max(0, qb - 4)
            for kt in range(k0, qb + 1):
                i = sbank()
                j = pslot()
                kb = kt // 4
                far = (kt == qb - 4)
                diag = (kt == qb)

                def S(kt=kt, i=i, far=far, diag=diag, kb=kb):
                    P.op("tensor", lambda e: e.matmul(psc[i], KWT[hs, kt * 128:(kt + 1) * 128], q2[hs, :], start=True, stop=not (far or diag)),
                         R=[bKW[kb], bQ2q[g]], W=[bpsc[i]])
                    if diag:
                        P.op("tensor", lambda e: e.matmul(psc[i], C.identb, tri1, start=False, stop=True), R=[bcn, C.bconst], W=[bpsc[i]])
                    if far:
                        P.op("tensor", lambda e: e.matmul(psc[i], C.identb, tri2, start=False, stop=True), R=[bcn, C.bconst], W=[bpsc[i]])

                def E(i=i, j=j):
                    P.op("scalar", lambda e: e.activation(out=Pe[j], in_=psc[i], func=AF.Exp), R=[bpsc[i]], W=[bPe[j]])

                def V(kt=kt, j=j, kb=kb):
                    for r in range(4):
                        P.op("tensor", (lambda e, r=r: e.matmul(pow_[:, r, 0:65], Pe[j][:, r * 128:(r + 1) * 128], VW1[:, kt, g, 0:65],
                                                                start=(kt == k0 and r == 0), stop=(kt == qb))),
                             R=[bPe[j], bVW[kb], bones], W=[bpow])
                tl.append({"S": S, "E": E, "V": V, "post": None})

            def post_win():
                gate_coef(g, qb, 2, pow_, bpow)
                P.op("vector", lambda e: e.tensor_tensor(out=otmp, in0=pow_[:, :, 0:64], in1=bc_(), op=ALU.mult), R=[bpow, bcf], W=[botmp])
                P.op("vector", lambda e: e.tensor_tensor(out=oa, in0=oa, in1=otmp, op=ALU.add), R=[boa, botmp], W=[boa])
            tl[-1]["post"] = post_win
            for kt in range(qb + 1):
                i = sbank()
                j = pslot()
                kb = kt // 4

                def S(kt=kt, i=i, kb=kb):
                    P.op("tensor", lambda e: e.matmul(psc[i], KS2[g][:, kt * 128:(kt + 1) * 128], q2, start=True, stop=(kt != qb)),
                         R=[bKS[g][kb], bE, bQ2q[g], bQ2b[g]], W=[bpsc[i]])
                    if kt == qb:
                        P.op("tensor", lambda e: e.matmul(psc[i], C.identb, tri1, start=False, stop=True), R=[bcn, C.bconst], W=[bpsc[i]])

                def E(i=i, j=j):
                    P.op("scalar", lambda e: e.activation(out=Pe[j], in_=psc[i], func=AF.Exp), R=[bpsc[i]], W=[bPe[j]])

                def V(kt=kt, j=j, kb=kb):
                    for r in range(4):
                        P.op("tensor", (lambda e, r=r: e.matmul(pos_[:, r, 0:65], Pe[j][:, r * 128:(r + 1) * 128], VS1[:, kt, g, 0:65],
                                                                start=(kt == 0 and r == 0), stop=(kt == qb))),
                             R=[bPe[j], bVS[kb], bones], W=[bpos])
                tl.append({"S": S, "E": E, "V": V, "post": None, "flush": ("cmp", g, qb) if kt == 0 else None})

            def post_sel():
                gate_coef(g, qb, 1, pos_, bpos)
                P.op("vector", lambda e: e.tensor_tensor(out=otmp, in0=pos_[:, :, 0:64], in1=bc_(), op=ALU.mult), R=[bpos, bcf], W=[botmp])
                P.op("vector", lambda e: e.tensor_tensor(out=oa, in0=oa, in1=otmp, op=ALU.add), R=[boa, botmp], W=[boa])

            def post_sel_pe():
                qsub = qb % 4
                for pair in range(2):
                    ch = 2 * g + pair
                    P.op("tensor", (lambda e, pair=pair: e.transpose(pTo, oa[:, 2 * pair:2 * pair + 2, :].rearrange("p r d -> p (r d)"), C.identf)),
                         R=[boa, C.bconst], W=[bpTo])
                    P.op("scalar", (lambda e, ch=ch: e.activation(out=OT[:, ch, qsub * 128:(qsub + 1) * 128], in_=pTo, func=AF.Copy)),
                         R=[bpTo], W=[bOT[ch], bpTo])
                if g == 1 and qb % 4 == 3:
                    out_block(qb // 4)
            tl[-1]["post"] = post_sel
            tl[-1]["postpe"] = post_sel_pe
            tl[-1]["tag"] = ("sel", g, qb)
            if nxt is not None:
                tl[0]["pre"] = (lambda: q2copy(*nxt))
            return tl

        def out_block(bs):
            t0 = tok0 + bs * 512
            P.op("sync", lambda e: e.dma_start(out=xo, in_=av[:, :, t0:t0 + 512]), W=[bxo], dma=True)
            for m in range(8):
                i = sbank()
                for c in range(4):
                    P.op("tensor", (lambda e, m=m, c=c, i=i: e.matmul(psc[i], wout[:, c, m * 128:(m + 1) * 128], OT[:, c, :],
                                                                       start=(c == 0), stop=(c == 3))),
                         R=[bwo[c], bOT[c]], W=[bpsc[i]])
                P.op("vector", (lambda e, m=m, i=i: e.tensor_tensor(out=xo[:, m, :], in0=psc[i], in1=xo[:, m, :], op=ALU.add)),
                     R=[bpsc[i], bxo], W=[bxo])
            P.op("sync", lambda e: e.dma_start(out=dv[:, :, t0:t0 + 512], in_=xo), R=[bxo], dma=True)

        nqb = int(os.environ.get("NSA_NQB", 32))
        order = [(g, qb) for qb in range(nqb) for g in range(2)]
        q2copy(*order[0])
        tiles = []
        for n_, (g, qb) in enumerate(order):
            nxt = order[n_ + 1] if n_ + 1 < len(order) else None
            tiles.extend(make_tiles(g, qb, nxt))
        LAG = int(os.environ.get("NSA_LAG", 2))
        DEFER = int(os.environ.get("NSA_DEFER", 3))
        pend = []
        for idx in range(len(tiles) + LAG):
            while pend and pend[0][0] <= idx:
                pend.pop(0)[2]()
            if idx < len(tiles):
                t = tiles[idx]
                if t.get("flush"):
                    for it in [p_ for p_ in pend if p_[1] == t["flush"]]:
                        pend.remove(it)
                        it[2]()
                if t.get("pre"):
                    t["pre"]()
                t["S"]()
                t["E"]()
            jx = idx - LAG
            if jx >= 0:
                t = tiles[jx]
                t["V"]()
                if t["post"]:
                    t["post"]()
                if t.get("postpe"):
                    pend.append([idx + DEFER, t["tag"], t["postpe"]])
        while pend:
            pend.pop(0)[2]()
        P.barrier()
        A.pop()

    import os
    stg = os.environ.get("NSA_STAGES", "123")
    for sq_ in range(int(os.environ.get("NSA_NSEQ", NSEQ))):
        if "1" in stg:
            stage_proj(sq_)
        if "2" in stg:
            stage_cmp(sq_ == 0)
        if "3" in stg:
            stage_attn(sq_)
    A.pop()

WSPEC = {
    "ffn_gate": [DEPTH, D, DFF], "ffn_up": [DEPTH, D, DFF], "ffn_down": [DEPTH, DFF, D],
    "w_in_even": [2, D, 2048], "pool_w": [2, 4, 128, 128], "w_out_even": [2, D, D],
    "w_in_odd": [2, D, 2328], "w_out_odd": [2, D, D],
    "cmp_w1": [2, 2, 2048, 256], "cmp_w2": [2, 2, 256, 64],
    "pvec": [DEPTH, 128, NPV],
    "consts": [128, 512],
    "nsac": [128, 1024],
    "nsacb": [128, 1024],
    "ropec": [128, SEQ], "ropes": [128, SEQ],
    "eind": [64, SEQ],
    "cmask0": [128, 32 * 128], "cmask1": [128, 16 * 128],
}


import os as _os
EARLY_SQ = _os.environ.get('EARLY_SQ', '1') == '1'
BF16_CONSTS = ("nsacb", "eind", "cmask0", "cmask1") if _os.environ.get("BFC", "1") == "1" else ()


def build(phases=None, dump=None):
    nc = bass.Bass("TRN2", target_bir_lowering=False)
    xT = nc.dram_tensor("xT", [D, TOK], F32, kind="ExternalInput").ap()
    W = {}
    for name, shp in WSPEC.items():
        W[name] = nc.dram_tensor(name, shp, BF16 if name in BF16_CONSTS else F32, kind="ExternalInput").ap()
    yT = nc.dram_tensor("yT", [D, TOK], F32, kind="ExternalOutput").ap()
    scrB = nc.dram_tensor("scrB", [D, TOK], F32).ap()
    P = Prog(nc)
    C = Ctx()
    C.W = W
    ARENA = 206 * 1024
    with nc.sbuf_tensor("arena", [128, ARENA], U8) as arena_t, nc.psum_tensor("psum", [128, 4096], F32) as psum_t:
        A = Arena(arena_t[:, :], ARENA)
        C.A = A
        C.psum = psum_t[:, :]
        C.pv = A.alloc(F32, NPV)
        C.ones_d = A.alloc(BF16, 128)
        C.invc = A.alloc(F32, 16)
        cst = A.alloc(F32, 512)
        C.bconst = Buf()
        bc0 = Buf()
        P.op("sync", lambda e: e.dma_start(out=cst, in_=W["consts"]), W=[bc0], dma=True)
        P.op("vector", lambda e: e.tensor_copy(out=C.invc, in_=cst[:, 0:16]), R=[bc0], W=[C.bconst])
        P.op("vector", lambda e: e.memset(C.ones_d, 1.0 / 1024.0), W=[C.bconst])
        C.ones512 = A.alloc(BF16, 128)
        P.op("vector", lambda e: e.memset(C.ones512, 1.0 / 512.0), W=[C.bconst])
        C.identf = cst[:, 128:256]
        C.identb = A.alloc(BF16, 128)
        P.op("vector", lambda e: e.tensor_copy(out=C.identb, in_=cst[:, 128:256]), R=[bc0], W=[C.bconst])
        C.epsc = A.alloc(F32, 2)
        P.op("vector", lambda e: e.memset(C.epsc, EPS), W=[C.bconst])
        P.barrier()

        if phases is None:
            phases = default_phases()
        bufmap = {"x": xT, "A": yT, "B": scrB}
        for ph in phases:
            kind = ph[0]
            if kind == "even":
                phase_even(P, C, ph[1], bufmap[ph[2]], bufmap[ph[3]])
            elif kind == "nsa":
                phase_nsa(P, C, ph[1], bufmap[ph[2]], bufmap[ph[3]], bufmap[ph[4]])
            elif kind == "conf":
                phase_conf(P, C, ph[1], bufmap[ph[2]], bufmap[ph[3]])
            elif kind == "ffn":
                phase_ffn(P, C, ph[1], ph[2], bufmap[ph[3]], bufmap[ph[4]] if ph[4] else None, bufmap[ph[5]])
            else:
                raise ValueError(kind)

        sems = {}
        import contextlib
        with contextlib.ExitStack() as st:
            esem = {e: st.enter_context(nc.semaphore("e_" + e)) for e in ENGS}
            dsem = {"sync": [st.enter_context(nc.semaphore(f"ds{i}")) for i in range(12)],
                    "gpsimd": [st.enter_context(nc.semaphore(f"dg{i}")) for i in range(12)],
                    "scalar": [st.enter_context(nc.semaphore(f"da{i}")) for i in range(4)],
                    "vector": [], "tensor": []}
            P.emit(esem, dsem)
    return nc


def default_phases():
    ph = []
    for l in range(DEPTH):
        src = "x" if l == 0 else "A"
        if l % 2 == 0:
            ph.append(("even", l, src, "A"))
        else:
            ph.append(("conf", l, "A", "B"))
            ph.append(("nsa", l, "A", "B", "A"))
        ph.append(("ffn", l, 0, "A", None, "B"))
        ph.append(("ffn", l, 1, "A", "B", "A"))
    return ph


def host_pvec(inp):
    pv = np.zeros((DEPTH, 128, NPV), np.float32)
    col = lambda v: np.ascontiguousarray(v.reshape(-1, 128).T)
    for l in range(DEPTH):
        pv[l, :, 0:8] = col(inp["norm_mix"][l])
        pv[l, :, 8:16] = col(inp["norm_ffn"][l])
        if l % 2 == 0:
            e = l // 2
            ca = inp["conv_a"][e]
            for c in range(4):
                for k in range(3):
                    pv[l, :, 16 + c * 3 + k] = ca[k, c * 128:(c + 1) * 128]
            pv[l, :, 28:32] = col(inp["pool_scale"][e])
        else:
            o = l // 2
            pv[l, :, OD_GQ] = np.tile(inp["q_norm"][o], 2)
            for i in range(3):
                pv[l, :, OD_KN + i] = np.tile(inp["k_norm"][o, i], 2)
            pv[l, :, OD_CB:OD_CB + 4] = col(inp["conf_dw_b"][o])
            pv[l, :, OD_LG:OD_LG + 4] = col(inp["conf_ln_g"][o])
            pv[l, :, OD_LB:OD_LB + 4] = col(inp["conf_ln_b"][o])
            dw = inp["conf_dw"][o]
            for c in range(4):
                pv[l, :, OD_DW + c * 31:OD_DW + (c + 1) * 31] = dw[:, c * 128:(c + 1) * 128].T
            pv[l, 0:64, OD_POS:OD_POS + 32] = inp["cmp_pos"][o, 0].T
            pv[l, 64:128, OD_POS:OD_POS + 32] = inp["cmp_pos"][o, 1].T
    return pv


def host_consts():
    c = np.zeros((128, 512), np.float32)
    c[:, 0:16] = (1.0 / (np.arange(16) + 1.0))[None, :]
    c[:, 128:256] = np.eye(128, dtype=np.float32)
    return c


def host_nsa_consts():
    out = {}
    c = np.zeros((128, 1024), np.float32)
    m = np.arange(128)
    perm = np.where((m % 64) < 32, m + 32, m - 32)
    prot = np.zeros((128, 128), np.float32)
    prot[perm, m] = 1.0
    c[:, 0:128] = prot
    c[:, 128:256] = ((m[:, None] // 64) == (m[None, :] // 64)).astype(np.float32) / 64.0
    k = np.arange(128)[:, None]
    q = np.arange(128)[None, :]
    c[:, 256:384] = np.where(k > q, NEGBIG, 0.0)
    c[:, 384:512] = np.where(k <= q, NEGBIG, 0.0)
    ql = np.arange(128)[:, None]
    cq = (ql >= 64).astype(np.int64)
    dl = np.arange(126)[None, :] - 62
    g = np.zeros((128, 126), np.float32)
    g[dl > cq] = -1.0
    g[(dl == cq) | (dl == cq - 1)] = 1.0e4
    c[:, 512:638] = g
    c[:, 640:766] = (dl < cq - 1).astype(np.float32)
    for nt in range(2):
        n = nt * 128 + np.arange(128)[:, None]
        j = np.arange(64)[None, :]
        ov = ((16 * n < 64 * j + 64) & (16 * n + 31 >= 64 * j) & (n < 255)).astype(np.float32)
        c[:, 768 + nt * 64:768 + (nt + 1) * 64] = ov
    out["nsac"] = c
    inv = 1.0 / (10000.0 ** (np.arange(0, 64, 2, dtype=np.float32) / 64.0))
    ang = np.arange(SEQ, dtype=np.float32)[:, None] * inv[None, :].astype(np.float32)
    cos = np.cos(ang).astype(np.float32).T
    sin = np.sin(ang).astype(np.float32).T
    out["ropec"] = np.ascontiguousarray(np.tile(cos, (4, 1)))
    out["ropes"] = np.ascontiguousarray(np.concatenate([-sin, sin, -sin, sin], axis=0))
    out["eind"] = (np.arange(64)[:, None] == (np.arange(SEQ)[None, :] // 64)).astype(np.float32)
    t = np.arange(32 * 128)[None, :]
    n0 = np.arange(128)[:, None]
    out["cmask0"] = (16 * n0 + 31 <= t).astype(np.float32)
    t1 = 16 * 128 + np.arange(16 * 128)[None, :]
    out["cmask1"] = (16 * (n0 + 128) + 31 <= t1).astype(np.float32)
    return out


def make_in_maps(inp, ncores=8):
    pv = host_pvec(inp)
    consts = host_consts()
    shared = {k: np.ascontiguousarray(inp[k], dtype=np.float32) for k in WSPEC if k in inp}
    shared["pvec"] = pv
    shared["consts"] = consts
    import ml_dtypes
    hc = host_nsa_consts()
    hc["nsacb"] = hc["nsac"]
    for k_ in BF16_CONSTS:
        hc[k_] = hc[k_].astype(ml_dtypes.bfloat16)
    shared.update(hc)
    wio = shared["w_in_odd"].copy()
    qcols = wio[:, :, 0:512].reshape(2, D, 2, 4, 64).transpose(0, 1, 3, 2, 4).reshape(2, D, 512)
    wio[:, :, 0:512] = qcols
    shared["w_in_odd"] = wio
    maps = []
    x = inp["x"]
    for c in range(ncores):
        xs = x[c * NSEQ:(c + 1) * NSEQ].reshape(TOK, D)
        m = dict(shared)
        m["xT"] = np.ascontiguousarray(xs.T)
        maps.append(m)
    return maps


def kernel(**inputs):
    inp = {k: np.asarray(v) for k, v in inputs.items()}
    nc = build()
    maps = make_in_maps(inp, 8)
    res = run_bass_kernel_spmd(nc, maps, core_ids=list(range(8)))
    out = np.empty((8 * NSEQ, SEQ, D), np.float32)
    for c in range(8):
        yT = res.results[c]["yT"]
        out[c * NSEQ:(c + 1) * NSEQ] = np.ascontiguousarray(yT.T).reshape(NSEQ, SEQ, D)
    return out
```

```python
import numpy as np
import concourse.bass as bass
import concourse.mybir as mybir
from concourse.bass_utils import run_bass_kernel_spmd

F32 = mybir.dt.float32
BF16 = mybir.dt.bfloat16
U8 = mybir.dt.uint8
AF = mybir.ActivationFunctionType
ALU = mybir.AluOpType

D = 1024
SEQ = 4096
NSEQ = 2
TOK = NSEQ * SEQ
NB = TOK // 512
DFF = 2816
DEPTH = 4
EPS = 1e-6
NEGBIG = -30000.0
ENGS = ["tensor", "vector", "scalar", "gpsimd", "sync"]
NPV = 192


class Op:
    __slots__ = ("eng", "fn", "deps", "is_dma", "sem", "val", "pos", "signal", "prev", "bar")


class Buf:
    __slots__ = ("w", "r", "rd")

    def __init__(self):
        self.w = None
        self.r = {}
        self.rd = []


class Prog:
    def __init__(self, nc):
        self.nc = nc
        self.q = {e: [] for e in ENGS}
        self.dmas = []

    def op(self, eng, fn, R=(), W=(), dma=False):
        o = Op()
        o.eng = eng
        o.fn = fn
        o.is_dma = dma
        o.signal = dma
        o.prev = None
        o.sem = None
        o.val = 0
        o.bar = False
        deps = set()
        for b in R:
            if b.w is not None:
                deps.add(b.w)
        for b in W:
            if b.w is not None:
                deps.add(b.w)
            deps.update(b.r.values())
            deps.update(b.rd)
        for b in R:
            if dma:
                b.rd.append(o)
            else:
                b.r[eng] = o
        for b in W:
            b.w = o
            b.r = {}
            b.rd = []
        deps.discard(o)
        o.deps = deps
        for d in deps:
            d.signal = True
        o.pos = len(self.q[eng])
        self.q[eng].append(o)
        if dma:
            self.dmas.append(o)
        return o

    def barrier(self):
        lasts = [self.q[e][-1] for e in ENGS if self.q[e]]
        deps = set(lasts) | set(self.dmas)
        self.dmas = []
        for e in ENGS:
            o = Op()
            o.eng = e
            o.fn = None
            o.bar = True
            o.is_dma = False
            o.signal = False
            o.prev = None
            o.sem = None
            o.val = 0
            o.deps = set(deps)
            for d in deps:
                d.signal = True
            o.pos = len(self.q[e])
            self.q[e].append(o)

    def emit(self, esem, dsem):
        nc = self.nc
        for e in ENGS:
            cnt = 0
            slot_last = {}
            nd = 0
            for o in self.q[e]:
                if o.is_dma:
                    ns = len(dsem[e])
                    s = nd % ns
                    o.sem = dsem[e][s]
                    o.val = 16 * (nd // ns + 1)
                    o.prev = slot_last.get(s)
                    slot_last[s] = o
                    nd += 1
                elif o.signal:
                    cnt += 1
                    o.sem = esem[e]
                    o.val = cnt

        def run(ename, eng):
            waited = {}
            for o in self.q[ename]:
                waits = []
                for d in o.deps:
                    if (not d.is_dma) and d.eng == ename and not o.bar:
                        if ename == "tensor":
                            continue
                        if o.pos - d.pos > 3:
                            continue
                    waits.append((d.sem, d.val))
                if o.prev is not None:
                    waits.append((o.prev.sem, o.prev.val))
                for (s, v) in waits:
                    key = id(s)
                    if waited.get(key, 0) >= v:
                        continue
                    waited[key] = v
                    eng.wait_ge(s, v)
                if o.bar:
                    if o.signal:
                        eng.sem_inc(o.sem, 1)
                    continue
                inst = o.fn(eng)
                if o.signal:
                    inst.then_inc(o.sem, 16 if o.is_dma else 1)

        with nc.Block() as block:
            @block.tensor
            def _(t):
                run("tensor", t)

            @block.vector
            def _(v):
                run("vector", v)

            @block.scalar
            def _(s):
                run("scalar", s)

            @block.gpsimd
            def _(g):
                run("gpsimd", g)

            @block.sync
            def _(sy):
                run("sync", sy)


class Arena:
    def __init__(self, ap, size):
        self.ap = ap
        self.size = size
        self.off = 0
        self.stack = []

    def push(self):
        self.stack.append(self.off)

    def pop(self):
        self.off = self.stack.pop()

    def alloc(self, dtype, *free):
        n = 1
        for f in free:
            n *= f
        es = 4 if dtype == F32 else 2
        off = (self.off + 63) // 64 * 64
        nb = n * es
        assert off + nb <= self.size, ("SBUF arena overflow", off, nb, self.size)
        self.off = off + nb
        self.peak = max(getattr(self, "peak", 0), self.off)
        v = self.ap[:, off:off + nb].bitcast(dtype)
        if len(free) > 1:
            names = [f"d{i}" for i in range(len(free))]
            kw = {names[i]: free[i] for i in range(1, len(free))}
            v = v.rearrange("p (" + " ".join(names) + ") -> p " + " ".join(names), **kw)
        return v


class Ctx:
    pass


def bufs(n):
    return [Buf() for _ in range(n)]


def load_weight_rows(P, dst, src, nk, bl, eng="gpsimd"):
    for k in range(nk):
        P.op(eng, (lambda e, k=k: e.dma_start(out=dst[:, k, :], in_=src[k * 128:(k + 1) * 128, :])),
             W=[bl[k]], dma=True)


def rms_square(P, C, xb, bx):
    sq, bsq = C.sq, C.bsq
    P.op("scalar", lambda e: e.activation(out=sq, in_=xb, func=AF.Square), R=[bx], W=[bsq])


def rms_block(P, C, xb, bx, h, bh, gcol0, pvb, do_square=True):
    sq, bsq, rstd, brstd, ps, bps = C.sq, C.bsq, C.rstd, C.brstd, C.ps_ss, C.bps_ss
    if do_square:
        rms_square(P, C, xb, bx)
    for k in range(8):
        P.op("tensor", (lambda e, k=k: e.matmul(ps, C.ones_d[:, :], sq[:, k, :], start=(k == 0), stop=(k == 7))),
             R=[bsq, C.bconst], W=[bps])
    P.op("scalar", lambda e: e.activation(out=rstd, in_=ps, func=AF.Ln, bias=C.epsc[:, 0:1], scale=1.0),
         R=[bps, C.bconst], W=[brstd])
    P.op("scalar", lambda e: e.activation(out=rstd, in_=rstd, func=AF.Exp, scale=-0.5), R=[brstd], W=[brstd])
    for k in range(8):
        P.op("vector", (lambda e, k=k: e.scalar_tensor_tensor(out=h[:, k, :], in0=xb[:, k, :],
                                                             scalar=C.pv[:, gcol0 + k:gcol0 + k + 1], in1=rstd,
                                                             op0=ALU.mult, op1=ALU.mult)),
             R=[bx, brstd, pvb], W=[bh])


def xview(ap2d):
    return ap2d.rearrange("(k p) t -> p k t", p=128)


def phase_ffn(P, C, l, hf, src, addsrc, dst):
    A = C.A
    A.push()
    NJ = 11
    wg = A.alloc(BF16, 8, NJ * 128)
    wu = A.alloc(BF16, 8, NJ * 128)
    wd = A.alloc(BF16, NJ, 1024)
    bwg, bwu, bwd = bufs(8), bufs(8), bufs(NJ)
    c0 = hf * NJ * 128
    load_weight_rows(P, wg, C.W["ffn_gate"][l, :, c0:c0 + NJ * 128], 8, bwg)
    load_weight_rows(P, wu, C.W["ffn_up"][l, :, c0:c0 + NJ * 128], 8, bwu)
    load_weight_rows(P, wd, C.W["ffn_down"][l, c0:c0 + NJ * 128, :], NJ, bwd)
    pvb = Buf()
    P.op("sync", lambda e: e.dma_start(out=C.pv, in_=C.W["pvec"][l]), W=[pvb], dma=True)

    xb = [A.alloc(F32, 8, 512) for _ in range(2)]
    bx = bufs(2)
    ob = A.alloc(F32, 8, 512)
    bob = Buf()
    h = [A.alloc(BF16, 8, 512) for _ in range(2)]
    bh = bufs(2)
    C.sq = A.alloc(BF16, 8, 512)
    C.bsq = Buf()
    C.rstd = A.alloc(F32, 512)
    C.brstd = Buf()
    a = A.alloc(BF16, NJ, 512)
    ba = bufs(NJ)
    sg = [A.alloc(F32, 512) for _ in range(2)]
    bsg = bufs(2)
    ps = C.psum
    bank = lambda i: ps[:, i * 512:(i + 1) * 512]
    pg = [bank(0), bank(1)]
    pu = [bank(2), bank(3)]
    po = [bank(4), bank(5), bank(7)]
    C.ps_ss = bank(6)
    bpg, bpu, bpo = bufs(2), bufs(2), bufs(3)
    C.bps_ss = Buf()
    sv, dv = xview(src), xview(dst)
    av = xview(addsrc) if addsrc is not None else None

    def load_x(b):
        s = b % 2
        P.op("sync", lambda e: e.dma_start(out=xb[s], in_=sv[:, :, b * 512:(b + 1) * 512]), W=[bx[s]], dma=True)

    def rms(b, do_square=True):
        s = b % 2
        rms_block(P, C, xb[s], bx[s], h[s], bh[s], 8, pvb, do_square)

    load_x(0)
    load_x(1)
    rms(0)
    cntl = [0]

    def do_block(b):
        s = b % 2
        if av is not None:
            P.op("sync", lambda e, b=b: e.dma_start(out=ob, in_=av[:, :, b * 512:(b + 1) * 512]), W=[bob], dma=True)
        for j in range(NJ):
            i = cntl[0] % 2
            cntl[0] += 1
            for k in range(8):
                P.op("tensor", (lambda e, k=k, j=j, i=i: e.matmul(pg[i], wg[:, k, j * 128:(j + 1) * 128], h[s][:, k, :],
                                                                   start=(k == 0), stop=(k == 7))),
                     R=[bwg[k], bh[s]], W=[bpg[i]])
            for k in range(8):
                P.op("tensor", (lambda e, k=k, j=j, i=i: e.matmul(pu[i], wu[:, k, j * 128:(j + 1) * 128], h[s][:, k, :],
                                                                   start=(k == 0), stop=(k == 7))),
                     R=[bwu[k], bh[s]], W=[bpu[i]])
            P.op("scalar", (lambda e, i=i: e.activation(out=sg[i], in_=pg[i], func=AF.Silu)), R=[bpg[i]], W=[bsg[i]])
            P.op("vector", (lambda e, i=i, j=j: e.tensor_tensor(out=a[:, j, :], in0=sg[i], in1=pu[i], op=ALU.mult)),
                 R=[bsg[i], bpu[i]], W=[ba[j]])
            if j == 5 and b + 1 < NB and EARLY_SQ:
                rms_square(P, C, xb[1 - s], bx[1 - s])
        for m in range(8):
            i = m % 3
            if m == 3 and b + 1 < NB:
                rms(b + 1, do_square=not EARLY_SQ)
            for j in range(NJ):
                P.op("tensor", (lambda e, m=m, j=j, i=i: e.matmul(po[i], wd[:, j, m * 128:(m + 1) * 128], a[:, j, :],
                                                                   start=(j == 0), stop=(j == NJ - 1))),
                     R=[bwd[j], ba[j]], W=[bpo[i]])
            if av is None:
                P.op("vector", (lambda e, m=m, i=i: e.tensor_tensor(out=ob[:, m, :], in0=po[i], in1=xb[s][:, m, :], op=ALU.add)),
                     R=[bpo[i], bx[s]], W=[bob])
            else:
                P.op("vector", (lambda e, m=m, i=i: e.tensor_tensor(out=ob[:, m, :], in0=po[i], in1=ob[:, m, :], op=ALU.add)),
                     R=[bpo[i], bob], W=[bob])
        if b + 2 < NB:
            load_x(b + 2)
        P.op("sync", lambda e, b=b: e.dma_start(out=dv[:, :, b * 512:(b + 1) * 512], in_=ob), R=[bob], dma=True)

    for b in range(NB):
        do_block(b)
    P.barrier()
    A.pop()


def phase_even(P, C, l, src, dst):
    A = C.A
    A.push()
    e_ = l // 2
    win = A.alloc(BF16, 8, 2048)
    wout = A.alloc(BF16, 8, 1024)
    wpool = A.alloc(BF16, 4, 128)
    bwin, bwout, bwp = bufs(8), bufs(8), Buf()
    load_weight_rows(P, win, C.W["w_in_even"][e_], 8, bwin)
    load_weight_rows(P, wout, C.W["w_out_even"][e_], 8, bwout)
    for g in range(4):
        P.op("gpsimd", (lambda e, g=g: e.dma_start(out=wpool[:, g, :], in_=C.W["pool_w"][e_, g])), W=[bwp], dma=True)
    pvb = Buf()
    P.op("sync", lambda e: e.dma_start(out=C.pv, in_=C.W["pvec"][l]), W=[pvb], dma=True)
    pv = C.pv
    CA0 = 16
    PS0 = 28

    xb = [A.alloc(F32, 8, 512) for _ in range(2)]
    bx = bufs(2)
    h = [A.alloc(BF16, 8, 512) for _ in range(2)]
    bh = bufs(2)
    C.sq = A.alloc(BF16, 8, 512)
    C.bsq = Buf()
    C.rstd = A.alloc(F32, 512)
    C.brstd = Buf()
    cv = [A.alloc(F32, 4, 514) for _ in range(2)]
    bcv = [bufs(4) for _ in range(2)]
    bcvh = bufs(2)
    vp = [A.alloc(F32, 4, 528) for _ in range(2)]
    bvp = [bufs(4) for _ in range(2)]
    bvph = bufs(2)
    cgs = [A.alloc(F32, 512) for _ in range(2)]
    bcgs = bufs(2)
    tt = [A.alloc(F32, 512) for _ in range(2)]
    btt = bufs(2)
    sA = A.alloc(F32, 528)
    sB = A.alloc(F32, 528)
    bsA, bsB = Buf(), Buf()
    pl = [A.alloc(BF16, 512) for _ in range(4)]
    bpl = bufs(4)
    tmpc = A.alloc(F32, 16)
    btmpc = Buf()
    cat = A.alloc(BF16, 8, 512)
    bcat = bufs(8)
    ps = C.psum
    bank = lambda i: ps[:, i * 512:(i + 1) * 512]
    pb = [bank(i) for i in range(5)]
    bpb = bufs(5)
    po = [bank(5), bank(6)]
    bpo = bufs(2)
    C.ps_ss = bank(7)
    C.bps_ss = Buf()
    sv, dv = xview(src), xview(dst)
    rr = [0]

    def nextbank():
        i = rr[0] % 5
        rr[0] += 1
        return i

    def load_x(b):
        s = b % 2
        P.op("sync", lambda e: e.dma_start(out=xb[s], in_=sv[:, :, b * 512:(b + 1) * 512]), W=[bx[s]], dma=True)

    def rms(b):
        s = b % 2
        rms_block(P, C, xb[s], bx[s], h[s], bh[s], 0, pvb)

    def proj(m, s):
        i = nextbank()
        for k in range(8):
            P.op("tensor", (lambda e, k=k, i=i: e.matmul(pb[i], win[:, k, m * 128:(m + 1) * 128], h[s][:, k, :],
                                                          start=(k == 0), stop=(k == 7))),
                 R=[bwin[k], bh[s]], W=[bpb[i]])
        return i

    load_x(0)
    load_x(1)
    rms(0)
    WIN = (2, 4, 8, 16)

    def do_block(b):
        s = b % 2
        first = (b % (SEQ // 512) == 0)
        if first:
            P.op("gpsimd", lambda e: e.memset(cv[s][:, :, 0:2], 0.0), W=[bcvh[s]])
            P.op("gpsimd", lambda e: e.memset(vp[s][:, :, 0:16], 0.0), W=[bvph[s]])
        else:
            P.op("gpsimd", lambda e: e.tensor_copy(out=cv[s][:, :, 0:2], in_=cv[1 - s][:, :, 512:514]),
                 R=bcv[1 - s], W=[bcvh[s]])
            P.op("gpsimd", lambda e: e.tensor_copy(out=vp[s][:, :, 0:16], in_=vp[1 - s][:, :, 512:528]),
                 R=bvp[1 - s], W=[bvph[s]])
        for g in range(4):
            gi = g
            w = WIN[g]
            i = proj(12 + g, s)
            V = vp[s]
            P.op("scalar", (lambda e, i=i, g=g: e.activation(out=V[:, g, 16:528], in_=pb[i], func=AF.Copy)),
                 R=[bpb[i]], W=[bvp[s][g]])
            rv = [bvp[s][g], bvph[s]]
            P.op("vector", (lambda e, g=g: e.tensor_tensor(out=sA[:, 1:528], in0=V[:, g, 1:528], in1=V[:, g, 0:527], op=ALU.add)),
                 R=rv, W=[bsA])
            fin, bfin = sA, bsA
            if w >= 4:
                P.op("vector", lambda e: e.tensor_tensor(out=sB[:, 3:528], in0=sA[:, 3:528], in1=sA[:, 1:526], op=ALU.add),
                     R=[bsA], W=[bsB])
                fin, bfin = sB, bsB
            if w >= 8:
                P.op("vector", lambda e: e.tensor_tensor(out=sA[:, 7:528], in0=sB[:, 7:528], in1=sB[:, 3:524], op=ALU.add),
                     R=[bsB], W=[bsA])
                fin, bfin = sA, bsA
            if w >= 16:
                P.op("vector", lambda e: e.tensor_tensor(out=sB[:, 15:528], in0=sA[:, 15:528], in1=sA[:, 7:520], op=ALU.add),
                     R=[bsA], W=[bsB])
                fin, bfin = sB, bsB
            P.op("vector", (lambda e, fin=fin, g=g, gi=gi, w=w: e.scalar_tensor_tensor(out=pl[gi], in0=fin[:, 16:528], scalar=1.0 / w,
                                                                                      in1=V[:, g, 16:528], op0=ALU.mult, op1=ALU.subtract)),
                 R=[bfin] + rv, W=[bpl[gi]])
            if first:
                P.op("vector", (lambda e, fin=fin, w=w: e.tensor_tensor(out=tmpc[:, 0:w - 1], in0=fin[:, 16:16 + w - 1], in1=C.invc[:, 0:w - 1], op=ALU.mult)),
                     R=[bfin, C.bconst], W=[btmpc])
                P.op("vector", (lambda e, g=g, gi=gi, w=w: e.tensor_tensor(out=pl[gi][:, 0:w - 1], in0=tmpc[:, 0:w - 1], in1=V[:, g, 16:16 + w - 1], op=ALU.subtract)),
                     R=[btmpc] + rv, W=[bpl[gi]])
        for c in range(4):
            ci = c % 2
            i = proj(4 + c, s)
            P.op("scalar", (lambda e, i=i, ci=ci: e.activation(out=cgs[ci], in_=pb[i], func=AF.Copy)),
                 R=[bpb[i]], W=[bcgs[ci]])
            i = proj(8 + c, s)
            P.op("vector", (lambda e, i=i, ci=ci, c=c: e.tensor_tensor(out=cv[s][:, c, 2:514], in0=pb[i], in1=cgs[ci], op=ALU.mult)),
                 R=[bpb[i], bcgs[ci]], W=[bcv[s][c]])
            i = proj(c, s)
            w = lambda k, c=c: pv[:, CA0 + c * 3 + k:CA0 + c * 3 + k + 1]
            P.op("vector", (lambda e, ci=ci, c=c, w=w: e.tensor_scalar(out=tt[ci], in0=cv[s][:, c, 2:514], scalar1=w(2), scalar2=None, op0=ALU.mult)),
                 R=[bcv[s][c], pvb], W=[btt[ci]])
            P.op("vector", (lambda e, ci=ci, c=c, w=w: e.scalar_tensor_tensor(out=tt[ci], in0=cv[s][:, c, 1:513], scalar=w(1), in1=tt[ci], op0=ALU.mult, op1=ALU.add)),
                 R=[bcv[s][c], bcvh[s], pvb, btt[ci]], W=[btt[ci]])
            P.op("vector", (lambda e, ci=ci, c=c, w=w: e.scalar_tensor_tensor(out=tt[ci], in0=cv[s][:, c, 0:512], scalar=w(0), in1=tt[ci], op0=ALU.mult, op1=ALU.add)),
                 R=[bcv[s][c], bcvh[s], pvb, btt[ci]], W=[btt[ci]])
            P.op("vector", (lambda e, ci=ci, c=c, i=i: e.tensor_tensor(out=cat[:, c, :], in0=tt[ci], in1=pb[i], op=ALU.mult)),
                 R=[btt[ci], bpb[i]], W=[bcat[c]])
        for g in range(4):
            gi = g
            i = nextbank()
            P.op("tensor", (lambda e, i=i, g=g, gi=gi: e.matmul(pb[i], wpool[:, g, :], pl[gi], start=True, stop=True)),
                 R=[bwp, bpl[gi]], W=[bpb[i]])
            P.op("scalar", (lambda e, i=i, g=g: e.activation(out=cat[:, 4 + g, :], in_=pb[i], func=AF.Copy, scale=pv[:, PS0 + g:PS0 + g + 1])),
                 R=[bpb[i], pvb], W=[bcat[4 + g]])
        if b + 1 < NB:
            rms(b + 1)
        for m in range(8):
            i = m % 2
            for c in range(8):
                P.op("tensor", (lambda e, m=m, c=c, i=i: e.matmul(po[i], wout[:, c, m * 128:(m + 1) * 128], cat[:, c, :],
                                                                   start=(c == 0), stop=(c == 7))),
                     R=[bwout[c], bcat[c]], W=[bpo[i]])
            P.op("vector", (lambda e, m=m, i=i: e.tensor_tensor(out=xb[s][:, m, :], in0=po[i], in1=xb[s][:, m, :], op=ALU.add)),
                 R=[bpo[i], bx[s]], W=[bx[s]])
        P.op("sync", lambda e, b=b: e.dma_start(out=dv[:, :, b * 512:(b + 1) * 512], in_=xb[s]), R=[bx[s]], dma=True)
        if b + 2 < NB:
            load_x(b + 2)

    for b in range(NB):
        do_block(b)
    P.barrier()
    A.pop()


OD_GQ = 16
OD_KN = 17
OD_CB = 20
OD_LG = 24
OD_LB = 28
OD_DW = 32


def phase_conf(P, C, l, src, dst):
    A = C.A
    A.push()
    o_ = l // 2
    wcf = A.alloc(BF16, 8, 1024)
    wout = A.alloc(BF16, 4, 1024)
    bwcf, bwout = bufs(8), bufs(4)
    load_weight_rows(P, wcf, C.W["w_in_odd"][o_, :, 1304:2328], 8, bwcf)
    load_weight_rows(P, wout, C.W["w_out_odd"][o_, 512:1024, :], 4, bwout)
    pvb = Buf()
    P.op("sync", lambda e: e.dma_start(out=C.pv, in_=C.W["pvec"][l]), W=[pvb], dma=True)
    pv = C.pv
    dg = A.alloc(BF16, 124, 128)
    bdg = Buf()
    for idx in range(124):
        P.op("vector", (lambda e, idx=idx: e.tensor_scalar(out=dg[:, idx, :], in0=C.identb, scalar1=pv[:, OD_DW + idx:OD_DW + idx + 1],
                                                          scalar2=None, op0=ALU.mult)),
             R=[pvb, C.bconst], W=[bdg])
    NX = 3
    xb = [A.alloc(F32, 8, 512) for _ in range(NX)]
    bx = bufs(NX)
    h = [A.alloc(BF16, 8, 512) for _ in range(2)]
    bh = bufs(2)
    C.sq = A.alloc(BF16, 8, 512)
    C.bsq = Buf()
    C.rstd = A.alloc(F32, 512)
    C.brstd = Buf()
    z = [A.alloc(BF16, 4, 542) for _ in range(2)]
    bz = [bufs(4) for _ in range(2)]
    bzh = bufs(2)
    sgb = [A.alloc(F32, 512) for _ in range(2)]
    bsgb = bufs(2)
    zc = [A.alloc(F32, 4, 512) for _ in range(2)]
    bzc = [bufs(4) for _ in range(2)]
    zcb = A.alloc(BF16, 4, 512)
    bzcb = bufs(4)
    zq = A.alloc(BF16, 4, 512)
    bzq = bufs(4)
    mu = [A.alloc(F32, 512) for _ in range(2)]
    e2 = [A.alloc(F32, 512) for _ in range(2)]
    bmu, be2 = bufs(2), bufs(2)
    m2 = A.alloc(F32, 512)
    rs = A.alloc(F32, 512)
    bm2, brs = Buf(), Buf()
    tt = [A.alloc(F32, 512) for _ in range(2)]
    btt = bufs(2)
    cat = A.alloc(BF16, 4, 512)
    bcat = bufs(4)
    ps = C.psum
    bank = lambda i: ps[:, i * 512:(i + 1) * 512]
    pb = [bank(i) for i in range(3)]
    bpb = bufs(3)
    pc = [bank(3), bank(4)]
    bpc = bufs(2)
    pmu, pe2 = bank(5), bank(6)
    bpmu, bpe2 = Buf(), Buf()
    C.ps_ss = bank(7)
    C.bps_ss = Buf()
    sv, dv = xview(src), xview(dst)
    rr = [0]

    def nextbank():
        i = rr[0] % 3
        rr[0] += 1
        return i

    def load_x(b):
        s = b % NX
        P.op("sync", lambda e: e.dma_start(out=xb[s], in_=sv[:, :, b * 512:(b + 1) * 512]), W=[bx[s]], dma=True)

    def rms(b):
        rms_block(P, C, xb[b % NX], bx[b % NX], h[b % 2], bh[b % 2], 0, pvb)

    def proj(col0, s):
        i = nextbank()
        for k in range(8):
            P.op("tensor", (lambda e, k=k, i=i: e.matmul(pb[i], wcf[:, k, col0:col0 + 128], h[s][:, k, :],
                                                          start=(k == 0), stop=(k == 7))),
                 R=[bwcf[k], bh[s]], W=[bpb[i]])
        return i

    def part_a(b, hooks=None):
        s = b % 2
        first = (b % (SEQ // 512) == 0)
        if first:
            P.op("gpsimd", lambda e: e.memset(z[s][:, :, 0:30], 0.0), W=[bzh[s]])
        else:
            P.op("gpsimd", lambda e: e.tensor_copy(out=z[s][:, :, 0:30], in_=z[1 - s][:, :, 512:542]),
                 R=bz[1 - s], W=[bzh[s]])
        for c in range(4):
            ci = c % 2
            ia = proj(c * 128, s)
            ib = proj(512 + c * 128, s)
            P.op("scalar", (lambda e, ib=ib, ci=ci: e.activation(out=sgb[ci], in_=pb[ib], func=AF.Sigmoid)),
                 R=[bpb[ib]], W=[bsgb[ci]])
            P.op("vector", (lambda e, ia=ia, ci=ci, c=c: e.tensor_tensor(out=z[s][:, c, 30:542], in0=pb[ia], in1=sgb[ci], op=ALU.mult)),
                 R=[bpb[ia], bsgb[ci]], W=[bz[s][c]])
            if hooks is not None:
                hooks[c]()
            for k in range(31):
                P.op("tensor", (lambda e, c=c, k=k, ci=ci: e.matmul(pc[ci], dg[:, c * 31 + k, :], z[s][:, c, k:k + 512],
                                                                     start=(k == 0), stop=(k == 30))),
                     R=[bdg, bz[s][c], bzh[s]], W=[bpc[ci]])
            P.op("scalar", (lambda e, c=c, ci=ci: e.activation(out=zc[s][:, c, :], in_=pc[ci], func=AF.Identity,
                                                               bias=pv[:, OD_CB + c:OD_CB + c + 1], scale=1.0)),
                 R=[bpc[ci], pvb], W=[bzc[s][c]])
            P.op("scalar", (lambda e, c=c, ci=ci: e.activation(out=zq[:, c, :], in_=pc[ci], func=AF.Square,
                                                               bias=pv[:, OD_CB + c:OD_CB + c + 1], scale=1.0)),
                 R=[bpc[ci], pvb], W=[bzq[c]])
            P.op("gpsimd", (lambda e, c=c: e.tensor_copy(out=zcb[:, c, :], in_=zc[s][:, c, :])), R=[bzc[s][c]], W=[bzcb[c]])
        for c in range(4):
            P.op("tensor", (lambda e, c=c: e.matmul(pmu, C.ones512, zcb[:, c, :], start=(c == 0), stop=(c == 3))),
                 R=[bzcb[c], C.bconst], W=[bpmu])
        for c in range(4):
            P.op("tensor", (lambda e, c=c: e.matmul(pe2, C.ones512, zq[:, c, :], start=(c == 0), stop=(c == 3))),
                 R=[bzq[c], C.bconst], W=[bpe2])
        P.op("scalar", lambda e: e.activation(out=mu[s], in_=pmu, func=AF.Copy), R=[bpmu], W=[bmu[s]])
        P.op("scalar", lambda e: e.activation(out=e2[s], in_=pe2, func=AF.Copy), R=[bpe2], W=[be2[s]])

    def part_b1(b, cs_=None):
        s = b % 2
        if cs_ is None or cs_ == 0:
            P.op("vector", lambda e: e.tensor_tensor(out=m2, in0=mu[s], in1=mu[s], op=ALU.mult), R=[bmu[s]], W=[bm2])
            P.op("vector", lambda e: e.tensor_tensor(out=m2, in0=e2[s], in1=m2, op=ALU.subtract), R=[be2[s], bm2], W=[bm2])
            P.op("scalar", lambda e: e.activation(out=rs, in_=m2, func=AF.Ln, bias=C.epsc[:, 0:1], scale=1.0),
                 R=[bm2, C.bconst], W=[brs])
            P.op("scalar", lambda e: e.activation(out=rs, in_=rs, func=AF.Exp, scale=-0.5), R=[brs], W=[brs])
        for c in (range(4) if cs_ is None else [cs_]):
            ci = c % 2
            P.op("vector", (lambda e, c=c, ci=ci: e.tensor_tensor(out=tt[ci], in0=zc[s][:, c, :], in1=mu[s], op=ALU.subtract)),
                 R=[bzc[s][c], bmu[s]], W=[btt[ci]])
            P.op("vector", (lambda e, c=c, ci=ci: e.tensor_tensor(out=tt[ci], in0=tt[ci], in1=rs, op=ALU.mult)),
                 R=[btt[ci], brs], W=[btt[ci]])
            P.op("scalar", (lambda e, c=c, ci=ci: e.activation(out=cat[:, c, :], in_=tt[ci], func=AF.Silu,
                                                               bias=pv[:, OD_LB + c:OD_LB + c + 1], scale=pv[:, OD_LG + c:OD_LG + c + 1])),
                 R=[btt[ci], pvb], W=[bcat[c]])

    def part_b2(b):
        sx = b % NX
        for m in range(8):
            i = nextbank()
            for c in range(4):
                P.op("tensor", (lambda e, m=m, c=c, i=i: e.matmul(pb[i], wout[:, c, m * 128:(m + 1) * 128], cat[:, c, :],
                                                                   start=(c == 0), stop=(c == 3))),
                     R=[bwout[c], bcat[c]], W=[bpb[i]])
            P.op("vector", (lambda e, m=m, i=i: e.tensor_tensor(out=xb[sx][:, m, :], in0=pb[i], in1=xb[sx][:, m, :], op=ALU.add)),
                 R=[bpb[i], bx[sx]], W=[bx[sx]])
        P.op("sync", lambda e: e.dma_start(out=dv[:, :, b * 512:(b + 1) * 512], in_=xb[sx]), R=[bx[sx]], dma=True)

    load_x(0)
    load_x(1)
    load_x(2)
    rms(0)
    part_a(0)
    rms(1)
    for b in range(NB):
        if b + 1 < NB:
            part_a(b + 1, hooks=[(lambda b=b, c=c: part_b1(b, c)) for c in range(4)])
        else:
            part_b1(b)
        if b + 2 < NB:
            rms(b + 2)
        part_b2(b)
        if b + 3 < NB:
            load_x(b + 3)
    P.barrier()
    A.pop()


KC0, VC0, KS0, VS0, KW0, VW0, GT0 = 512, 640, 768, 896, 1024, 1152, 1280
OD_POS = 160
TINY = 1e-30


def phase_nsa(P, C, l, src, addsrc, dst):
    A = C.A
    A.push()
    o_ = l // 2
    pv = C.pv
    W = C.W
    NBS = SEQ // 512
    wnsa = A.alloc(BF16, 8, 1304)
    bwn = bufs(8)
    wout = A.alloc(BF16, 4, 1024)
    bwo = bufs(4)
    for k in range(8):
        rows = W["w_in_odd"][o_, k * 128:(k + 1) * 128, :]
        P.op("gpsimd", (lambda e, k=k, rows=rows: e.dma_start(out=wnsa[:, k, :], in_=rows[:, 0:1304])),
             W=[bwn[k]], dma=True)
    load_weight_rows(P, wout, W["w_out_odd"][o_, 0:512, :], 4, bwo)
    pvb = Buf()
    P.op("sync", lambda e: e.dma_start(out=pv, in_=W["pvec"][l]), W=[pvb], dma=True)
    prot = A.alloc(BF16, 128)
    bd = A.alloc(BF16, 128)
    tri1 = A.alloc(BF16, 512)
    tri2 = A.alloc(BF16, 512)
    gext = A.alloc(F32, 126)
    mext = A.alloc(F32, 126)
    ovl = A.alloc(BF16, 2, 64)
    posb = A.alloc(BF16, 32)
    bcn = Buf()
    CN = W["nsac"]
    CB = W["nsacb"]
    P.op("sync", lambda e: e.dma_start(out=prot, in_=CB[:, 0:128]), W=[bcn], dma=True)
    P.op("sync", lambda e: e.dma_start(out=bd, in_=CB[:, 128:256]), W=[bcn], dma=True)
    for r in range(4):
        P.op("sync", (lambda e, r=r: e.dma_start(out=tri1[:, r * 128:(r + 1) * 128], in_=CB[:, 256:384])), W=[bcn], dma=True)
        P.op("sync", (lambda e, r=r: e.dma_start(out=tri2[:, r * 128:(r + 1) * 128], in_=CB[:, 384:512])), W=[bcn], dma=True)
    P.op("sync", lambda e: e.dma_start(out=gext, in_=CN[:, 512:638]), W=[bcn], dma=True)
    P.op("sync", lambda e: e.dma_start(out=mext, in_=CN[:, 640:766]), W=[bcn], dma=True)
    P.op("sync", lambda e: e.dma_start(out=ovl, in_=CB[:, 768:896].rearrange("p (a j) -> p a j", a=2)), W=[bcn], dma=True)
    P.op("vector", lambda e: e.tensor_copy(out=posb, in_=pv[:, OD_POS:OD_POS + 32]), R=[pvb], W=[bcn])
    QT = A.alloc(BF16, 4, SEQ)
    bQT = [bufs(NBS) for _ in range(4)]
    KS2 = [A.alloc(BF16, SEQ) for _ in range(2)]
    bKS = [bufs(NBS) for _ in range(2)]
    bE = Buf()
    KWT = A.alloc(BF16, SEQ)
    bKW = bufs(NBS)
    KCT = A.alloc(BF16, SEQ)
    bKC = bufs(NBS)
    VCT = A.alloc(BF16, SEQ)
    bVC = bufs(NBS)
    VS1 = A.alloc(BF16, 32, 2, 66)
    VW1 = A.alloc(BF16, 32, 2, 66)
    bVS = bufs(NBS)
    bVW = bufs(NBS)
    bones = Buf()
    GT = A.alloc(F32, 32, 24)
    bGT = bufs(NBS)
    KCN = [A.alloc(BF16, 256) for _ in range(2)]
    bKCN = bufs(2)
    VCM1 = A.alloc(BF16, 2, 2, 66)
    bVCM = bufs(2)
    selin = A.alloc(F32, 128)
    bselin = Buf()
    bz = A.alloc(F32, 4)
    bbz = Buf()
    P.op("sync", lambda e: e.dma_start(out=KS2[0][64:128, :], in_=W["eind"][:, :]), W=[bE], dma=True)
    P.op("sync", lambda e: e.dma_start(out=KS2[1][0:64, :], in_=W["eind"][:, :]), W=[bE], dma=True)
    P.op("vector", lambda e: e.memset(VS1[:, :, :, 64:65], 1.0), W=[bones])
    P.op("vector", lambda e: e.memset(VW1[:, :, :, 64:65], 1.0), W=[bones])
    P.op("vector", lambda e: e.memset(VCM1[:, :, :, 64:65], 1.0), W=[bones])
    P.op("vector", lambda e: e.memset(selin, 0.0), W=[bselin])
    ps = C.psum
    bank = lambda i: ps[:, i * 512:(i + 1) * 512]
    sv, dv, av = xview(src), xview(dst), xview(addsrc)
    P.barrier()

    def load_cmp(w1a, w1b, w2k, w2v, bw):
        for kind in range(2):
            w1 = W["cmp_w1"][o_, kind]
            w1v = w1.rearrange("(c p) h -> p c h", p=128)
            w1s = w1.rearrange("(c two d) h -> two d c h", two=2, d=64)
            import os
            lc = int(os.environ.get("LC", 15))
            for c4 in range(4):
                cs_ = slice(c4 * 4, c4 * 4 + 4)
                if lc & 1:
                    P.op("gpsimd", (lambda e, kind=kind, w1v=w1v, cs_=cs_: e.dma_start(out=w1a[kind][:, cs_, :], in_=w1v[:, cs_, :])),
                         W=[bw], dma=True)
                if lc & 2:
                    P.op("gpsimd", (lambda e, kind=kind, w1s=w1s, cs_=cs_: e.dma_start(out=w1b[kind][0:64, cs_, :], in_=w1s[1][:, cs_, :])),
                         W=[bw], dma=True)
                    P.op("gpsimd", (lambda e, kind=kind, w1s=w1s, cs_=cs_: e.dma_start(out=w1b[kind][64:128, cs_, :], in_=w1s[0][:, cs_, :])),
                         W=[bw], dma=True)
        w2 = W["cmp_w2"][o_]
        if lc & 4:
            for half in range(2):
                P.op("gpsimd", (lambda e, half=half: e.dma_start(out=w2k[:, :, half * 64:(half + 1) * 64],
                                                                  in_=w2[0].rearrange("(c p) d -> p c d", p=128))), W=[bw], dma=True)
        if lc & 8:
            P.op("gpsimd", lambda e: e.dma_start(out=w2v, in_=w2[1].rearrange("(c p) d -> p c d", p=128)), W=[bw], dma=True)

    def w1sel(w1a, w1b, kind, g, i):
        t = w1a[kind] if (i % 2) == g else w1b[kind]
        return t[g * 64:(g + 1) * 64, i // 2, :]

    def stage_cmp(first):
        A.push()
        w1a = [A.alloc(BF16, 16, 256) for _ in range(2)]
        w1b = [A.alloc(BF16, 16, 256) for _ in range(2)]
        w2k = A.alloc(BF16, 2, 128)
        w2v = A.alloc(BF16, 2, 64)
        bw = Buf()
        load_cmp(w1a, w1b, w2k, w2v, bw)
        gz = [A.alloc(BF16, 256) for _ in range(2)]
        bgz = bufs(2)
        sqk = A.alloc(BF16, 256)
        bsqk = Buf()
        rk = A.alloc(F32, 256)
        brk = Buf()
        pz = [bank(0), bank(1)]
        bpz = bufs(2)
        pk, pssk, pvc, pbias = bank(2), bank(3), bank(4), bank(5)
        bpk, bpssk, bpvc, bpbias = Buf(), Buf(), Buf(), Buf()
        import os
        cdbg = int(os.environ.get("CMP_DBG", 9))
        if cdbg < 2:
            P.barrier()
            A.pop()
            return
        if first:
            for kind in range(2 if cdbg != 2 else 1):
                g = kind
                for hc in range(2):
                    col = kind * 2 + hc
                    for i in range(32):
                        P.op("tensor", (lambda e, kind=kind, g=g, hc=hc, i=i, col=col: e.matmul(
                            pbias[:, col:col + 1], w1sel(w1a, w1b, kind, g, i)[:, hc * 128:(hc + 1) * 128],
                            posb[g * 64:(g + 1) * 64, i:i + 1], start=(i == 0), stop=(i == 31))),
                            R=[bw, bcn], W=[bpbias])
            P.op("vector", lambda e: e.tensor_copy(out=bz, in_=pbias[:, 0:4]), R=[bpbias], W=[bbz])
        if cdbg < 4:
            P.barrier()
            A.pop()
            return
        for kind in range(2):
            X = KCT if kind == 0 else VCT
            bX = bKC if kind == 0 else bVC
            Xv = X.rearrange("p (n s) -> p n s", s=16)
            for g in range(2 if cdbg > 4 else 1):
                for hc in range(2):
                    for i in range(32):
                        rhs = Xv[g * 64:(g + 1) * 64, 0:255, i] if i < 16 else Xv[g * 64:(g + 1) * 64, 1:256, i - 16]
                        P.op("tensor", (lambda e, kind=kind, g=g, hc=hc, i=i, rhs=rhs: e.matmul(
                            pz[hc][:, 0:255], w1sel(w1a, w1b, kind, g, i)[:, hc * 128:(hc + 1) * 128], rhs,
                            start=(i == 0), stop=(i == 31))), R=[bw] + bX, W=[bpz[hc]])
                    P.op("scalar", (lambda e, kind=kind, hc=hc: e.activation(out=gz[hc][:, 0:255], in_=pz[hc][:, 0:255],
                                                                            func=AF.Gelu_apprx_tanh,
                                                                            bias=bz[:, kind * 2 + hc:kind * 2 + hc + 1], scale=1.0)),
                         R=[bpz[hc], bbz], W=[bgz[hc]])
                if cdbg < 6:
                    continue
                if kind == 0:
                    for hc in range(2):
                        P.op("tensor", (lambda e, hc=hc: e.matmul(pk[:, 0:255], w2k[:, hc, :], gz[hc][:, 0:255],
                                                                   start=(hc == 0), stop=(hc == 1))),
                             R=[bw, bgz[hc]], W=[bpk])
                    P.op("scalar", lambda e: e.activation(out=sqk[:, 0:255], in_=pk[:, 0:255], func=AF.Square), R=[bpk], W=[bsqk])
                    P.op("tensor", lambda e: e.matmul(pssk[:, 0:255], bd, sqk[:, 0:255], start=True, stop=True),
                         R=[bsqk, bcn], W=[bpssk])
                    P.op("scalar", lambda e: e.activation(out=rk[:, 0:255], in_=pssk[:, 0:255], func=AF.Sqrt,
                                                          bias=C.epsc[:, 0:1], scale=1.0), R=[bpssk, C.bconst], W=[brk])
                    P.op("vector", lambda e: e.reciprocal(out=rk[:, 0:255], in_=rk[:, 0:255]), R=[brk], W=[brk])
                    P.op("vector", (lambda e, g=g: e.scalar_tensor_tensor(out=KCN[g][:, 0:255], in0=pk[:, 0:255],
                                                                         scalar=pv[:, OD_KN + 2:OD_KN + 3], in1=rk[:, 0:255],
                                                                         op0=ALU.mult, op1=ALU.mult)),
                         R=[bpk, brk, pvb], W=[bKCN[g]])
                else:
                    for nt in range(2):
                        nn = 128 if nt == 0 else 127
                        for hc in range(2):
                            P.op("tensor", (lambda e, nt=nt, nn=nn, hc=hc: e.matmul(
                                pvc[0:nn, nt * 64:(nt + 1) * 64], gz[hc][:, nt * 128:nt * 128 + nn], w2v[:, hc, :],
                                start=(hc == 0), stop=(hc == 1))), R=[bw, bgz[hc]], W=[bpvc])
                    P.op("scalar", (lambda e, g=g: e.activation(out=VCM1[:, g, :, 0:64],
                                                                in_=pvc[:, 0:128].rearrange("p (a d) -> p a d", a=2), func=AF.Copy)),
                         R=[bpvc], W=[bVCM[g]])
        P.barrier()
        A.pop()

    def stage_proj(sq_):
        A.push()
        xb = A.alloc(F32, 8, 512)
        bx = Buf()
        h = [A.alloc(BF16, 8, 512) for _ in range(2)]
        bh = bufs(2)
        C.sq = A.alloc(BF16, 8, 512)
        C.bsq = Buf()
        C.rstd = A.alloc(F32, 512)
        C.brstd = Buf()
        cs = [A.alloc(F32, 512) for _ in range(2)]
        sn = [A.alloc(F32, 512) for _ in range(2)]
        bcs = bufs(2)
        yb = [A.alloc(BF16, 512) for _ in range(2)]
        byb = bufs(2)
        qsq = [A.alloc(BF16, 512) for _ in range(2)]
        bqsq = bufs(2)
        t1 = [A.alloc(F32, 512) for _ in range(2)]
        bt1 = bufs(2)
        t2 = [A.alloc(F32, 512) for _ in range(2)]
        bt2 = bufs(2)
        rq = [A.alloc(F32, 512) for _ in range(2)]
        brq = bufs(2)
        pq = [bank(0), bank(1), bank(2)]
        bpq = bufs(3)
        ppr = [bank(3), bank(4)]
        bppr = bufs(2)
        pss = bank(5)
        bpss = Buf()
        ptm = bank(6)
        bptm = Buf()
        C.ps_ss = bank(7)
        C.bps_ss = Buf()
        rr = [0]
        cc = [0]
        tok0 = sq_ * SEQ

        def load_x(bs):
            P.op("sync", lambda e: e.dma_start(out=xb, in_=sv[:, :, tok0 + bs * 512:tok0 + (bs + 1) * 512]), W=[bx], dma=True)

        def rms(bs):
            s = bs % 2
            rms_block(P, C, xb, bx, h[s], bh[s], 0, pvb)

        def chunk(bs, col0, gcol, qscale, dests, norm=True):
            s = bs % 2
            ts = bs % 2
            i = rr[0] % 3
            rr[0] += 1
            j = cc[0] % 2
            cc[0] += 1
            for k in range(8):
                P.op("tensor", (lambda e, k=k: e.matmul(pq[i], wnsa[:, k, col0:col0 + 128], h[s][:, k, :],
                                                         start=(k == 0), stop=(k == 7))),
                     R=[bwn[k], bh[s]], W=[bpq[i]])
            if gcol is not None:
                P.op("scalar", lambda e: e.activation(out=yb[j], in_=pq[i], func=AF.Copy, scale=pv[:, gcol:gcol + 1]),
                     R=[bpq[i], pvb], W=[byb[j]])
            else:
                P.op("scalar", lambda e: e.activation(out=yb[j], in_=pq[i], func=AF.Copy), R=[bpq[i]], W=[byb[j]])
            P.op("tensor", lambda e: e.matmul(ppr[j], prot, yb[j], start=True, stop=True), R=[byb[j], bcn], W=[bppr[j]])
            if norm:
                P.op("scalar", lambda e: e.activation(out=qsq[j], in_=pq[i], func=AF.Square), R=[bpq[i]], W=[bqsq[j]])
                P.op("tensor", lambda e: e.matmul(pss, bd, qsq[j], start=True, stop=True), R=[bqsq[j], bcn], W=[bpss])
                P.op("scalar", lambda e: e.activation(out=rq[j], in_=pss, func=AF.Ln, bias=C.epsc[:, 0:1], scale=1.0),
                     R=[bpss, C.bconst], W=[brq[j]])
                P.op("scalar", lambda e: e.activation(out=rq[j], in_=rq[j], func=AF.Exp, scale=-0.5), R=[brq[j]], W=[brq[j]])
            if gcol is not None:
                P.op("vector", lambda e: e.scalar_tensor_tensor(out=t1[j], in0=pq[i], scalar=pv[:, gcol:gcol + 1], in1=cs[ts],
                                                                op0=ALU.mult, op1=ALU.mult),
                     R=[bpq[i], pvb, bcs[ts], byb[j], bqsq[j]], W=[bt1[j]])
            else:
                P.op("vector", lambda e: e.tensor_tensor(out=t1[j], in0=pq[i], in1=cs[ts], op=ALU.mult),
                     R=[bpq[i], bcs[ts], byb[j]], W=[bt1[j]])
            P.op("vector", lambda e: e.tensor_tensor(out=t2[j], in0=ppr[j], in1=sn[ts], op=ALU.mult),
                 R=[bppr[j], bcs[ts]], W=[bt2[j]])
            if norm:
                P.op("gpsimd", lambda e: e.tensor_tensor(out=t1[j], in0=t1[j], in1=t2[j], op=ALU.add),
                     R=[bt1[j], bt2[j]], W=[bt1[j]])
                for (dst_ap, prange, bdst) in dests:
                    lo, hi = prange
                    P.op("vector", (lambda e, dst_ap=dst_ap, lo=lo, hi=hi: e.scalar_tensor_tensor(
                        out=dst_ap, in0=t1[j][lo:hi], scalar=qscale, in1=rq[j][lo:hi], op0=ALU.mult, op1=ALU.mult)),
                        R=[bt1[j], brq[j]], W=[bdst])
            else:
                for (dst_ap, prange, bdst) in dests:
                    lo, hi = prange
                    P.op("vector", (lambda e, dst_ap=dst_ap, lo=lo, hi=hi: e.tensor_tensor(
                        out=dst_ap, in0=t1[j][lo:hi], in1=t2[j][lo:hi], op=ALU.add)),
                        R=[bt1[j], bt2[j]], W=[bdst])

        def do_block(bs):
            s = bs % 2
            ts = bs % 2
            c0 = bs * 512
            P.op("sync", lambda e: e.dma_start(out=cs[ts], in_=W["ropec"][:, c0:c0 + 512]), W=[bcs[ts]], dma=True)
            P.op("sync", lambda e: e.dma_start(out=sn[ts], in_=W["ropes"][:, c0:c0 + 512]), W=[bcs[ts]], dma=True)
            import os
            dbg = int(os.environ.get("NSA_DBG", 9))
            if dbg < 2:
                return
            for m in range(4):
                chunk(bs, m * 128, OD_GQ, 0.125, [(QT[:, m, c0:c0 + 512], (0, 128), bQT[m][bs])])
            if dbg < 3:
                return
            chunk(bs, KS0, OD_KN + 0, 1.0, [(KS2[0][0:64, c0:c0 + 512], (0, 64), bKS[0][bs]),
                                            (KS2[1][64:128, c0:c0 + 512], (64, 128), bKS[1][bs])])
            chunk(bs, KW0, OD_KN + 1, 1.0, [(KWT[:, c0:c0 + 512], (0, 128), bKW[bs])])
            if dbg < 4:
                return
            chunk(bs, KC0, None, 1.0, [(KCT[:, c0:c0 + 512], (0, 128), bKC[bs])], norm=False)
            if dbg < 5:
                return
            i = rr[0] % 3
            rr[0] += 1
            for k in range(8):
                P.op("tensor", (lambda e, k=k: e.matmul(pq[i], wnsa[:, k, VC0:VC0 + 128], h[s][:, k, :],
                                                         start=(k == 0), stop=(k == 7))),
                     R=[bwn[k], bh[s]], W=[bpq[i]])
            P.op("scalar", lambda e: e.activation(out=VCT[:, c0:c0 + 512], in_=pq[i], func=AF.Copy), R=[bpq[i]], W=[bVC[bs]])
            if dbg < 6:
                return
            for jt in range(4):
                tile_ = bs * 4 + jt
                for gi_, (col0, n, off) in enumerate(((VS0, 128, 0), (VW0, 128, 128), (GT0, 24, 256))):
                    if dbg < 7 + gi_ and dbg < 9 and not (dbg == 6 and gi_ == 0):
                        continue
                    for k in range(8):
                        P.op("tensor", (lambda e, k=k, jt=jt, col0=col0, n=n, off=off: e.matmul(
                            ptm[:, off:off + n], h[s][:, k, jt * 128:(jt + 1) * 128], wnsa[:, k, col0:col0 + n],
                            start=(k == 0), stop=(k == 7))), R=[bwn[k], bh[s]], W=[bptm])
                if dbg == 6:
                    continue
                cpm = int(os.environ.get("NSA_CP", 7))
                if cpm & 1:
                    P.op("scalar", (lambda e, tile_=tile_: e.activation(out=VS1[:, tile_, :, 0:64],
                                                                       in_=ptm[:, 0:128].rearrange("p (a d) -> p a d", a=2), func=AF.Copy)),
                         R=[bptm, bones], W=[bVS[bs]])
                if cpm & 2:
                    P.op("scalar", (lambda e, tile_=tile_: e.activation(out=VW1[:, tile_, :, 0:64],
                                                                       in_=ptm[:, 128:256].rearrange("p (a d) -> p a d", a=2), func=AF.Copy)),
                         R=[bptm, bones], W=[bVW[bs]])
                if cpm & 4:
                    P.op("scalar", (lambda e, tile_=tile_: e.activation(out=GT[:, tile_, :], in_=ptm[:, 256:280], func=AF.Sigmoid)),
                         R=[bptm], W=[bGT[bs]])

        load_x(0)
        rms(0)
        for bs in range(NBS):
            if bs + 1 < NBS:
                load_x(bs + 1)
                rms(bs + 1)
            do_block(bs)
        P.barrier()
        A.pop()

    def stage_attn(sq_):
        import os
        A.push()
        cm0 = A.alloc(BF16, 32, 128)
        cm1 = A.alloc(BF16, 16, 128)
        bcm = Buf()
        P.op("sync", lambda e: e.dma_start(out=cm0, in_=W["cmask0"].rearrange("p (a q) -> p a q", a=32)), W=[bcm], dma=True)
        P.op("sync", lambda e: e.dma_start(out=cm1, in_=W["cmask1"].rearrange("p (a q) -> p a q", a=16)), W=[bcm], dma=True)
        Q2 = [A.alloc(BF16, 512) for _ in range(2)]
        bQ2q = bufs(2)
        bQ2b = bufs(2)
        NPE = 5
        Pe = [A.alloc(BF16, 512) for _ in range(NPE)]
        bPe = bufs(NPE)
        Pc = [A.alloc(BF16, 512) for _ in range(2)]
        bPc = bufs(2)
        lmx = A.alloc(F32, 4)
        blmx = Buf()
        cf = A.alloc(F32, 4)
        bcf = Buf()
        rlc = A.alloc(F32, 4)
        brlc = Buf()
        tmpi = A.alloc(F32, 4, 64)
        btmpi = Buf()
        imp = A.alloc(F32, 64)
        bimp = Buf()
        score = A.alloc(F32, 64)
        bscore = Buf()
        m8 = A.alloc(F32, 8)
        bm8 = Buf()
        oacc = [A.alloc(F32, 4, 64) for _ in range(2)]
        boacc = bufs(2)
        otmp = A.alloc(F32, 4, 64)
        botmp = Buf()
        OT = A.alloc(BF16, 4, 512)
        bOT = bufs(4)
        xo = A.alloc(F32, 8, 512)
        bxo = Buf()
        NSB = 3
        psc = [bank(0), bank(1), bank(7)]
        bpsc = bufs(NSB)
        poc = bank(2).rearrange("p (r d) -> p r d", r=4)
        pos_ = bank(3).rearrange("p (r d) -> p r d", r=4)
        pow_ = bank(4).rearrange("p (r d) -> p r d", r=4)
        bpoc, bpos, bpow = Buf(), Buf(), Buf()
        pimp = bank(5)[:, 0:256].rearrange("p (r j) -> p r j", r=4)
        misc = bank(6)
        pT = misc[:, 0:128]
        pTo = misc[:, 128:256]
        bpimp = Buf()
        bpT = Buf()
        bpTo = bpT
        rr = [0]
        pp = [0]
        pc_ = [0]
        tok0 = sq_ * SEQ

        def sbank():
            i = rr[0] % NSB
            rr[0] += 1
            return i

        def pslot():
            i = pp[0] % NPE
            pp[0] += 1
            return i

        def q2copy(g, qb):
            hs = slice(g * 64, (g + 1) * 64)
            rQ = [bQT[m][qb // 4] for m in range(4)]
            P.op("gpsimd", lambda e: e.tensor_copy(out=Q2[g][hs, :].rearrange("p (r q) -> p r q", r=4),
                                                   in_=QT[hs, :, qb * 128:(qb + 1) * 128]),
                 R=rQ, W=[bQ2q[g]])

        def gate_coef(g, qb, br, psrc, bpsrc):
            P.op("vector", lambda e: e.tensor_scalar(out=lmx.unsqueeze(2), in0=psrc[:, :, 64:65], scalar1=TINY, scalar2=None, op0=ALU.max),
                 R=[bpsrc], W=[blmx])
            P.op("vector", lambda e: e.reciprocal(out=cf, in_=lmx), R=[blmx], W=[bcf])
            if br == 0:
                P.op("vector", lambda e: e.tensor_copy(out=rlc, in_=cf), R=[bcf], W=[brlc])
            P.op("vector", lambda e: e.tensor_tensor(out=cf, in0=cf,
                                                     in1=GT[:, qb, 12 * g:12 * g + 12].rearrange("p (r b) -> p b r", b=3)[:, br, :], op=ALU.mult),
                 R=[bcf, bGT[qb // 4]], W=[bcf])

        def make_tiles(g, qb, nxt):
            hs = slice(g * 64, (g + 1) * 64)
            bsl = slice((1 - g) * 64, (2 - g) * 64)
            q2 = Q2[g]
            oa = oacc[g]
            boa = boacc[g]
            bc_ = lambda: cf.unsqueeze(2).to_broadcast([128, 4, 64])
            tl = []
            ntl = 1 if qb < 16 else 2
            for nt in range(ntl):
                nn = 128 if nt == 0 else 127
                i = sbank()
                j = pslot()
                jc = pc_[0] % 2
                pc_[0] += 1

                def S(nt=nt, nn=nn, i=i):
                    P.op("tensor", lambda e: e.matmul(psc[i][0:nn, :], KCN[g][hs, nt * 128:nt * 128 + nn], q2[hs, :], start=True, stop=True),
                         R=[bKCN[g], bQ2q[g]], W=[bpsc[i]])

                def E(nt=nt, nn=nn, i=i, j=j, jc=jc):
                    P.op("scalar", lambda e: e.activation(out=Pe[j][0:nn, :], in_=psc[i][0:nn, :], func=AF.Exp), R=[bpsc[i]], W=[bPe[j]])
                    mk = cm0[0:nn, qb, :] if nt == 0 else cm1[0:nn, qb - 16, :]
                    P.op("vector", lambda e: e.tensor_tensor(
                        out=Pc[jc][0:nn, :].rearrange("p (r q) -> p r q", r=4), in0=Pe[j][0:nn, :].rearrange("p (r q) -> p r q", r=4),
                        in1=mk.unsqueeze(1).to_broadcast([nn, 4, 128]), op=ALU.mult),
                        R=[bPe[j], bcm], W=[bPc[jc]])

                def V(nt=nt, nn=nn, jc=jc):
                    for r in range(4):
                        P.op("tensor", (lambda e, r=r: e.matmul(poc[:, r, 0:65], Pc[jc][0:nn, r * 128:(r + 1) * 128], VCM1[0:nn, g, nt, 0:65],
                                                                start=(nt == 0 and r == 0), stop=(nt == ntl - 1))),
                             R=[bPc[jc], bVCM[g], bones], W=[bpoc])
                    for r in range(4):
                        P.op("tensor", (lambda e, r=r: e.matmul(pimp[:, r, :], Pc[jc][0:nn, r * 128:(r + 1) * 128], ovl[0:nn, nt, :],
                                                                start=(nt == 0 and r == 0), stop=(nt == ntl - 1))),
                             R=[bPc[jc], bcn], W=[bpimp])
                tl.append({"S": S, "E": E, "V": V, "post": None})

            def post_cmp():
                gate_coef(g, qb, 0, poc, bpoc)
                P.op("vector", lambda e: e.tensor_tensor(out=tmpi, in0=pimp, in1=rlc.unsqueeze(2).to_broadcast([128, 4, 64]), op=ALU.mult),
                     R=[bpimp, brlc], W=[btmpi])
                P.op("vector", lambda e: e.tensor_reduce(out=imp, in_=tmpi.rearrange("p r j -> p j r"), axis=mybir.AxisListType.X, op=ALU.add),
                     R=[btmpi], W=[bimp])
                w0 = 62 - 2 * qb
                P.op("vector", lambda e: e.tensor_tensor(out=score, in0=imp, in1=mext[:, w0:w0 + 64], op=ALU.mult), R=[bimp, bcn], W=[bscore])
                P.op("vector", lambda e: e.tensor_tensor(out=score, in0=score, in1=gext[:, w0:w0 + 64], op=ALU.add), R=[bscore, bcn], W=[bscore])
                P.op("vector", lambda e: e.memset(score[:, 0:1], 1.0e4), R=[bscore], W=[bscore])
                P.op("vector", lambda e: e.max(out=m8, in_=score), R=[bscore], W=[bm8])
                P.op("vector", lambda e: e.tensor_scalar(out=selin[:, (1 - g) * 64:(2 - g) * 64], in0=score, scalar1=m8[:, 7:8],
                                                         scalar2=NEGBIG, op0=ALU.is_lt, op1=ALU.mult),
                     R=[bscore, bm8], W=[bselin])
                P.op("vector", lambda e: e.tensor_tensor(out=oa, in0=poc[:, :, 0:64], in1=bc_(), op=ALU.mult), R=[bpoc, bcf], W=[boa])

            def post_cmp_pe():
                P.op("tensor", lambda e: e.transpose(pT, selin, C.identf), R=[bselin, C.bconst], W=[bpT])
                P.op("scalar", lambda e: e.activation(out=q2[bsl, :].rearrange("p (r q) -> p r q", r=4),
                                                      in_=pT[bsl, :].unsqueeze(1).to_broadcast([64, 4, 128]), func=AF.Copy),
                     R=[bpT], W=[bQ2b[g], bpT])
            tl[-1]["post"] = post_cmp
            tl[-1]["postpe"] = post_cmp_pe
            tl[-1]["tag"] = ("cmp", g, qb)
            k0 = max(0, qb - 4)
            for kt in range(k0, qb + 1):
                i = sbank()
                j = pslot()
                kb = kt // 4
                far = (kt == qb - 4)
                diag = (kt == qb)

                def S(kt=kt, i=i, far=far, diag=diag, kb=kb):
                    P.op("tensor", lambda e: e.matmul(psc[i], KWT[hs, kt * 128:(kt + 1) * 128], q2[hs, :], start=True, stop=not (far or diag)),
                         R=[bKW[kb], bQ2q[g]], W=[bpsc[i]])
                    if diag:
                        P.op("tensor", lambda e: e.matmul(psc[i], C.identb, tri1, start=False, stop=True), R=[bcn, C.bconst], W=[bpsc[i]])
                    if far:
                        P.op("tensor", lambda e: e.matmul(psc[i], C.identb, tri2, start=False, stop=True), R=[bcn, C.bconst], W=[bpsc[i]])

                def E(i=i, j=j):
                    P.op("scalar", lambda e: e.activation(out=Pe[j], in_=psc[i], func=AF.Exp), R=[bpsc[i]], W=[bPe[j]])

                def V(kt=kt, j=j, kb=kb):
                    for r in range(4):
                        P.op("tensor", (lambda e, r=r: e.matmul(pow_[:, r, 0:65], Pe[j][:, r * 128:(r + 1) * 128], VW1[:, kt, g, 0:65],
                                                                start=(kt == k0 and r == 0), stop=(kt == qb))),
                             R=[bPe[j], bVW[kb], bones], W=[bpow])
                tl.append({"S": S, "E": E, "V": V, "post": None})

            def post_win():
                gate_coef(g, qb, 2, pow_, bpow)
                P.op("vector", lambda e: e.tensor_tensor(out=otmp, in0=pow_[:, :, 0:64], in1=bc_(), op=ALU.mult), R=[bpow, bcf], W=[botmp])
                P.op("vector", lambda e: e.tensor_tensor(out=oa, in0=oa, in1=otmp, op=ALU.add), R=[boa, botmp], W=[boa])
            tl[-1]["post"] = post_win
            for kt in range(qb + 1):
                i = sbank()
                j = pslot()
                kb = kt // 4

                def S(kt=kt, i=i, kb=kb):
                    P.op("tensor", lambda e: e.matmul(psc[i], KS2[g][:, kt * 128:(kt + 1) * 128], q2, start=True, stop=(kt != qb)),
                         R=[bKS[g][kb], bE, bQ2q[g], bQ2b[g]], W=[bpsc[i]])
                    if kt == qb:
                        P.op("tensor", lambda e: e.matmul(psc[i], C.identb, tri1, start=False, stop=True), R=[bcn, C.bconst], W=[bpsc[i]])

                def E(i=i, j=j):
                    P.op("scalar", lambda e: e.activation(out=Pe[j], in_=psc[i], func=AF.Exp), R=[bpsc[i]], W=[bPe[j]])

                def V(kt=kt, j=j, kb=kb):
                    for r in range(4):
                        P.op("tensor", (lambda e, r=r: e.matmul(pos_[:, r, 0:65], Pe[j][:, r * 128:(r + 1) * 128], VS1[:, kt, g, 0:65],
                                                                start=(kt == 0 and r == 0), stop=(kt == qb))),
                             R=[bPe[j], bVS[kb], bones], W=[bpos])
                tl.append({"S": S, "E": E, "V": V, "post": None, "flush": ("cmp", g, qb) if kt == 0 else None})

            def post_sel():
                gate_coef(g, qb, 1, pos_, bpos)
                P.op("vector", lambda e: e.tensor_tensor(out=otmp, in0=pos_[:, :, 0:64], in1=bc_(), op=ALU.mult), R=[bpos, bcf], W=[botmp])
                P.op("vector", lambda e: e.tensor_tensor(out=oa, in0=oa, in1=otmp, op=ALU.add), R=[boa, botmp], W=[boa])

            def post_sel_pe():
                qsub = qb % 4
                for pair in range(2):
                    ch = 2 * g + pair
                    P.op("tensor", (lambda e, pair=pair: e.transpose(pTo, oa[:, 2 * pair:2 * pair + 2, :].rearrange("p r d -> p (r d)"), C.identf)),
                         R=[boa, C.bconst], W=[bpTo])
                    P.op("scalar", (lambda e, ch=ch: e.activation(out=OT[:, ch, qsub * 128:(qsub + 1) * 128], in_=pTo, func=AF.Copy)),
                         R=[bpTo], W=[bOT[ch], bpTo])
                if g == 1 and qb % 4 == 3:
                    out_block(qb // 4)
            tl[-1]["post"] = post_sel
            tl[-1]["postpe"] = post_sel_pe
            tl[-1]["tag"] = ("sel", g, qb)
            if nxt is not None:
                tl[0]["pre"] = (lambda: q2copy(*nxt))
            return tl

        def out_block(bs):
            t0 = tok0 + bs * 512
            P.op("sync", lambda e: e.dma_start(out=xo, in_=av[:, :, t0:t0 + 512]), W=[bxo], dma=True)
            for m in range(8):
                i = sbank()
                for c in range(4):
                    P.op("tensor", (lambda e, m=m, c=c, i=i: e.matmul(psc[i], wout[:, c, m * 128:(m + 1) * 128], OT[:, c, :],
                                                                       start=(c == 0), stop=(c == 3))),
                         R=[bwo[c], bOT[c]], W=[bpsc[i]])
                P.op("vector", (lambda e, m=m, i=i: e.tensor_tensor(out=xo[:, m, :], in0=psc[i], in1=xo[:, m, :], op=ALU.add)),
                     R=[bpsc[i], bxo], W=[bxo])
            P.op("sync", lambda e: e.dma_start(out=dv[:, :, t0:t0 + 512], in_=xo), R=[bxo], dma=True)

        nqb = int(os.environ.get("NSA_NQB", 32))
        order = [(g, qb) for qb in range(nqb) for g in range(2)]
        q2copy(*order[0])
        tiles = []
        for n_, (g, qb) in enumerate(order):
            nxt = order[n_ + 1] if n_ + 1 < len(order) else None
            tiles.extend(make_tiles(g, qb, nxt))
        LAG = int(os.environ.get("NSA_LAG", 2))
        DEFER = int(os.environ.get("NSA_DEFER", 3))
        pend = []
        for idx in range(len(tiles) + LAG):
            while pend and pend[0][0] <= idx:
                pend.pop(0)[2]()
            if idx < len(tiles):
                t = tiles[idx]
                if t.get("flush"):
                    for it in [p_ for p_ in pend if p_[1] == t["flush"]]:
                        pend.remove(it)
                        it[2]()
                if t.get("pre"):
                    t["pre"]()
                t["S"]()
                t["E"]()
            jx = idx - LAG
            if jx >= 0:
                t = tiles[jx]
                t["V"]()
                if t["post"]:
                    t["post"]()
                if t.get("postpe"):
                    pend.append([idx + DEFER, t["tag"], t["postpe"]])
        while pend:
            pend.pop(0)[2]()
        P.barrier()
        A.pop()

    import os
    stg = os.environ.get("NSA_STAGES", "123")
    for sq_ in range(int(os.environ.get("NSA_NSEQ", NSEQ))):
        if "1" in stg:
            stage_proj(sq_)
        if "2" in stg:
            stage_cmp(sq_ == 0)
        if "3" in stg:
            stage_attn(sq_)
    A.pop()

WSPEC = {
    "ffn_gate": [DEPTH, D, DFF], "ffn_up": [DEPTH, D, DFF], "ffn_down": [DEPTH, DFF, D],
    "w_in_even": [2, D, 2048], "pool_w": [2, 4, 128, 128], "w_out_even": [2, D, D],
    "w_in_odd": [2, D, 2328], "w_out_odd": [2, D, D],
    "cmp_w1": [2, 2, 2048, 256], "cmp_w2": [2, 2, 256, 64],
    "pvec": [DEPTH, 128, NPV],
    "consts": [128, 512],
    "nsac": [128, 1024],
    "nsacb": [128, 1024],
    "ropec": [128, SEQ], "ropes": [128, SEQ],
    "eind": [64, SEQ],
    "cmask0": [128, 32 * 128], "cmask1": [128, 16 * 128],
}


import os as _os
EARLY_SQ = _os.environ.get('EARLY_SQ', '1') == '1'
BF16_CONSTS = ("nsacb", "eind", "cmask0", "cmask1") if _os.environ.get("BFC", "1") == "1" else ()


def build(phases=None, dump=None):
    nc = bass.Bass("TRN2", target_bir_lowering=False)
    xT = nc.dram_tensor("xT", [D, TOK], F32, kind="ExternalInput").ap()
    W = {}
    for name, shp in WSPEC.items():
        W[name] = nc.dram_tensor(name, shp, BF16 if name in BF16_CONSTS else F32, kind="ExternalInput").ap()
    yT = nc.dram_tensor("yT", [D, TOK], F32, kind="ExternalOutput").ap()
    scrB = nc.dram_tensor("scrB", [D, TOK], F32).ap()
    P = Prog(nc)
    C = Ctx()
    C.W = W
    ARENA = 206 * 1024
    with nc.sbuf_tensor("arena", [128, ARENA], U8) as arena_t, nc.psum_tensor("psum", [128, 4096], F32) as psum_t:
        A = Arena(arena_t[:, :], ARENA)
        C.A = A
        C.psum = psum_t[:, :]
        C.pv = A.alloc(F32, NPV)
        C.ones_d = A.alloc(BF16, 128)
        C.invc = A.alloc(F32, 16)
        cst = A.alloc(F32, 512)
        C.bconst = Buf()
        bc0 = Buf()
        P.op("sync", lambda e: e.dma_start(out=cst, in_=W["consts"]), W=[bc0], dma=True)
        P.op("vector", lambda e: e.tensor_copy(out=C.invc, in_=cst[:, 0:16]), R=[bc0], W=[C.bconst])
        P.op("vector", lambda e: e.memset(C.ones_d, 1.0 / 1024.0), W=[C.bconst])
        C.ones512 = A.alloc(BF16, 128)
        P.op("vector", lambda e: e.memset(C.ones512, 1.0 / 512.0), W=[C.bconst])
        C.identf = cst[:, 128:256]
        C.identb = A.alloc(BF16, 128)
        P.op("vector", lambda e: e.tensor_copy(out=C.identb, in_=cst[:, 128:256]), R=[bc0], W=[C.bconst])
        C.epsc = A.alloc(F32, 2)
        P.op("vector", lambda e: e.memset(C.epsc, EPS), W=[C.bconst])
        P.barrier()

        if phases is None:
            phases = default_phases()
        bufmap = {"x": xT, "A": yT, "B": scrB}
        for ph in phases:
            kind = ph[0]
            if kind == "even":
                phase_even(P, C, ph[1], bufmap[ph[2]], bufmap[ph[3]])
            elif kind == "nsa":
                phase_nsa(P, C, ph[1], bufmap[ph[2]], bufmap[ph[3]], bufmap[ph[4]])
            elif kind == "conf":
                phase_conf(P, C, ph[1], bufmap[ph[2]], bufmap[ph[3]])
            elif kind == "ffn":
                phase_ffn(P, C, ph[1], ph[2], bufmap[ph[3]], bufmap[ph[4]] if ph[4] else None, bufmap[ph[5]])
            else:
                raise ValueError(kind)

        sems = {}
        import contextlib
        with contextlib.ExitStack() as st:
            esem = {e: st.enter_context(nc.semaphore("e_" + e)) for e in ENGS}
            dsem = {"sync": [st.enter_context(nc.semaphore(f"ds{i}")) for i in range(12)],
                    "gpsimd": [st.enter_context(nc.semaphore(f"dg{i}")) for i in range(12)],
                    "scalar": [st.enter_context(nc.semaphore(f"da{i}")) for i in range(4)],
                    "vector": [], "tensor": []}
            P.emit(esem, dsem)
    return nc


def default_phases():
    ph = []
    for l in range(DEPTH):
        src = "x" if l == 0 else "A"
        if l % 2 == 0:
            ph.append(("even", l, src, "A"))
        else:
            ph.append(("conf", l, "A", "B"))
            ph.append(("nsa", l, "A", "B", "A"))
        ph.append(("ffn", l, 0, "A", None, "B"))
        ph.append(("ffn", l, 1, "A", "B", "A"))
    return ph


def host_pvec(inp):
    pv = np.zeros((DEPTH, 128, NPV), np.float32)
    col = lambda v: np.ascontiguousarray(v.reshape(-1, 128).T)
    for l in range(DEPTH):
        pv[l, :, 0:8] = col(inp["norm_mix"][l])
        pv[l, :, 8:16] = col(inp["norm_ffn"][l])
        if l % 2 == 0:
            e = l // 2
            ca = inp["conv_a"][e]
            for c in range(4):
                for k in range(3):
                    pv[l, :, 16 + c * 3 + k] = ca[k, c * 128:(c + 1) * 128]
            pv[l, :, 28:32] = col(inp["pool_scale"][e])
        else:
            o = l // 2
            pv[l, :, OD_GQ] = np.tile(inp["q_norm"][o], 2)
            for i in range(3):
                pv[l, :, OD_KN + i] = np.tile(inp["k_norm"][o, i], 2)
            pv[l, :, OD_CB:OD_CB + 4] = col(inp["conf_dw_b"][o])
            pv[l, :, OD_LG:OD_LG + 4] = col(inp["conf_ln_g"][o])
            pv[l, :, OD_LB:OD_LB + 4] = col(inp["conf_ln_b"][o])
            dw = inp["conf_dw"][o]
            for c in range(4):
                pv[l, :, OD_DW + c * 31:OD_DW + (c + 1) * 31] = dw[:, c * 128:(c + 1) * 128].T
            pv[l, 0:64, OD_POS:OD_POS + 32] = inp["cmp_pos"][o, 0].T
            pv[l, 64:128, OD_POS:OD_POS + 32] = inp["cmp_pos"][o, 1].T
    return pv


def host_consts():
    c = np.zeros((128, 512), np.float32)
    c[:, 0:16] = (1.0 / (np.arange(16) + 1.0))[None, :]
    c[:, 128:256] = np.eye(128, dtype=np.float32)
    return c


def host_nsa_consts():
    out = {}
    c = np.zeros((128, 1024), np.float32)
    m = np.arange(128)
    perm = np.where((m % 64) < 32, m + 32, m - 32)
    prot = np.zeros((128, 128), np.float32)
    prot[perm, m] = 1.0
    c[:, 0:128] = prot
    c[:, 128:256] = ((m[:, None] // 64) == (m[None, :] // 64)).astype(np.float32) / 64.0
    k = np.arange(128)[:, None]
    q = np.arange(128)[None, :]
    c[:, 256:384] = np.where(k > q, NEGBIG, 0.0)
    c[:, 384:512] = np.where(k <= q, NEGBIG, 0.0)
    ql = np.arange(128)[:, None]
    cq = (ql >= 64).astype(np.int64)
    dl = np.arange(126)[None, :] - 62
    g = np.zeros((128, 126), np.float32)
    g[dl > cq] = -1.0
    g[(dl == cq) | (dl == cq - 1)] = 1.0e4
    c[:, 512:638] = g
    c[:, 640:766] = (dl < cq - 1).astype(np.float32)
    for nt in range(2):
        n = nt * 128 + np.arange(128)[:, None]
        j = np.arange(64)[None, :]
        ov = ((16 * n < 64 * j + 64) & (16 * n + 31 >= 64 * j) & (n < 255)).astype(np.float32)
        c[:, 768 + nt * 64:768 + (nt + 1) * 64] = ov
    out["nsac"] = c
    inv = 1.0 / (10000.0 ** (np.arange(0, 64, 2, dtype=np.float32) / 64.0))
    ang = np.arange(SEQ, dtype=np.float32)[:, None] * inv[None, :].astype(np.float32)
    cos = np.cos(ang).astype(np.float32).T
    sin = np.sin(ang).astype(np.float32).T
    out["ropec"] = np.ascontiguousarray(np.tile(cos, (4, 1)))
    out["ropes"] = np.ascontiguousarray(np.concatenate([-sin, sin, -sin, sin], axis=0))
    out["eind"] = (np.arange(64)[:, None] == (np.arange(SEQ)[None, :] // 64)).astype(np.float32)
    t = np.arange(32 * 128)[None, :]
    n0 = np.arange(128)[:, None]
    out["cmask0"] = (16 * n0 + 31 <= t).astype(np.float32)
    t1 = 16 * 128 + np.arange(16 * 128)[None, :]
    out["cmask1"] = (16 * (n0 + 128) + 31 <= t1).astype(np.float32)
    return out


def make_in_maps(inp, ncores=8):
    pv = host_pvec(inp)
    consts = host_consts()
    shared = {k: np.ascontiguousarray(inp[k], dtype=np.float32) for k in WSPEC if k in inp}
    shared["pvec"] = pv
    shared["consts"] = consts
    import ml_dtypes
    hc = host_nsa_consts()
    hc["nsacb"] = hc["nsac"]
    for k_ in BF16_CONSTS:
        hc[k_] = hc[k_].astype(ml_dtypes.bfloat16)
    shared.update(hc)
    wio = shared["w_in_odd"].copy()
    qcols = wio[:, :, 0:512].reshape(2, D, 2, 4, 64).transpose(0, 1, 3, 2, 4).reshape(2, D, 512)
    wio[:, :, 0:512] = qcols
    shared["w_in_odd"] = wio
    maps = []
    x = inp["x"]
    for c in range(ncores):
        xs = x[c * NSEQ:(c + 1) * NSEQ].reshape(TOK, D)
        m = dict(shared)
        m["xT"] = np.ascontiguousarray(xs.T)
        maps.append(m)
    return maps


def kernel(**inputs):
    inp = {k: np.asarray(v) for k, v in inputs.items()}
    nc = build()
    maps = make_in_maps(inp, 8)
    res = run_bass_kernel_spmd(nc, maps, core_ids=list(range(8)))
    out = np.empty((8 * NSEQ, SEQ, D), np.float32)
    for c in range(8):
        yT = res.results[c]["yT"]
        out[c * NSEQ:(c + 1) * NSEQ] = np.ascontiguousarray(yT.T).reshape(NSEQ, SEQ, D)
    return out
```
